# Optimizing a Trainium2 kernel written in Bass

```python
import jax, jax.numpy as jnp
from jax import lax
import numpy as np

D_MODEL = 2048
BATCH = 1
SEQ = 8192
DEPTH = 4

HEAD_DIM = 128
GROUPS = 4
BRANCH_W = GROUPS * HEAD_DIM
N_BRANCH = 4
CONV_W = 3
CHUNK = 128
POOL_WINDOWS = (2, 4, 8, 16)
Q_BLOCK = 128
PEER_HEADS = 8
PEER_KEYS = 128
PEER_N = PEER_KEYS * PEER_KEYS
PEER_DKEY = 256
PEER_HALF = PEER_DKEY // 2
PEER_TOPK = 16
TOKEN_BLOCK = 128
EPS = 1e-6

OFF_A = 0
OFF_B = OFF_A + 3 * BRANCH_W
OFF_C = OFF_B + 2 * BRANCH_W
OFF_D = OFF_C + BRANCH_W
OFF_G = OFF_D + 3 * BRANCH_W + GROUPS
IN_COLS = OFF_G + N_BRANCH * D_MODEL

kernel_name = "hybrid_conv_sgu_pool_fox_peer"


def rmsnorm(x, g):
    xf = x.astype(jnp.float32)
    y = xf * lax.rsqrt(jnp.mean(xf * xf, axis=-1, keepdims=True) + EPS)
    return (y * g.astype(jnp.float32)).astype(x.dtype)


def short_conv_mixer(cols, conv_w):
    b, c, h = jnp.split(cols, 3, axis=-1)
    z = c * h
    kern = conv_w[:, None, :].astype(z.dtype)
    y = lax.conv_general_dilated(z, kern, window_strides=(1,), padding=((CONV_W - 1, 0),),
                                 dimension_numbers=('NWC', 'WIO', 'NWC'),
                                 feature_group_count=BRANCH_W)
    return b * y


def sgu_mixer(cols, norm_g, w_s, bias):
    zc = jax.nn.gelu(cols)
    u, v = jnp.split(zc, 2, axis=-1)
    v = rmsnorm(v, norm_g)
    bsz, s, _ = v.shape
    v = v.reshape(bsz, s // CHUNK, CHUNK, GROUPS, HEAD_DIM)
    mask = jnp.tril(jnp.ones((CHUNK, CHUNK), dtype=bool))
    w = jnp.where(mask[None], w_s, 0).astype(v.dtype)
    sv = jnp.einsum('gts,bcsgd->bctgd', w, v) + bias.T[:, :, None].astype(v.dtype)
    return u * sv.reshape(bsz, s, BRANCH_W)


def pool_mixer(p, pool_w, scale):
    bsz, s, _ = p.shape
    pf = p.astype(jnp.float32)
    cs = jnp.concatenate([jnp.zeros((bsz, 1, BRANCH_W), jnp.float32), jnp.cumsum(pf, axis=1)], axis=1)
    t = jnp.arange(1, s + 1, dtype=jnp.float32)
    outs = []
    for g, w in enumerate(POOL_WINDOWS):
        sl = slice(g * HEAD_DIM, (g + 1) * HEAD_DIM)
        csg = cs[..., sl]
        lag = jnp.concatenate([jnp.zeros((bsz, w - 1, HEAD_DIM), jnp.float32), csg[:, :s - w + 1]], axis=1)
        mean = (csg[:, 1:] - lag) / jnp.minimum(t, float(w))[None, :, None]
        outs.append(mean - pf[..., sl])
    pooled = jnp.stack(outs, axis=2).astype(p.dtype)
    y = jnp.einsum('bsgi,gio->bsgo', pooled, pool_w)
    return y.reshape(bsz, s, BRANCH_W) * scale


def forgetting_attention(cols, forget_b):
    bsz, s, _ = cols.shape
    q = cols[..., 0:BRANCH_W].reshape(bsz, s, GROUPS, HEAD_DIM)
    k = cols[..., BRANCH_W:2 * BRANCH_W].reshape(bsz, s, GROUPS, HEAD_DIM)
    v = cols[..., 2 * BRANCH_W:3 * BRANCH_W].reshape(bsz, s, GROUPS, HEAD_DIM)
    f_logit = cols[..., 3 * BRANCH_W:].astype(jnp.float32) + forget_b.astype(jnp.float32)
    F = jnp.cumsum(jax.nn.log_sigmoid(f_logit), axis=1)
    Fk = jnp.transpose(F, (0, 2, 1))
    nblk = s // Q_BLOCK
    qb = q.reshape(bsz, nblk, Q_BLOCK, GROUPS, HEAD_DIM).transpose(1, 0, 2, 3, 4)
    Fb = F.reshape(bsz, nblk, Q_BLOCK, GROUPS).transpose(1, 0, 3, 2)
    key_pos = jnp.arange(s)
    scale = HEAD_DIM ** -0.5

    def block(args):
        q_i, F_i, i = args
        qpos = i * Q_BLOCK + jnp.arange(Q_BLOCK)
        sc = jnp.einsum('bqhd,bkhd->bhqk', q_i, k).astype(jnp.float32) * scale
        sc = sc + F_i[..., None] - Fk[:, :, None, :]
        sc = jnp.where((qpos[:, None] >= key_pos[None, :])[None, None], sc, -jnp.inf)
        pr = jax.nn.softmax(sc, axis=-1).astype(v.dtype)
        return jnp.einsum('bhqk,bkhd->bqhd', pr, v)

    o = lax.map(block, (qb, Fb, jnp.arange(nblk)))
    return o.transpose(1, 0, 2, 3, 4).reshape(bsz, s, BRANCH_W)


def peer_ffn(h, wq, keys, u_tab, v_tab):
    bsz, s, d = h.shape
    q = (h @ wq).reshape(bsz, s, PEER_HEADS, 2, PEER_HALF)
    sc = jnp.einsum('bshpc,hpnc->bshpn', q, keys).astype(jnp.float32)
    vals, idx = lax.top_k(sc, PEER_TOPK)
    cand = vals[..., 0, :, None] + vals[..., 1, None, :]
    cand_idx = idx[..., 0, :, None] * PEER_KEYS + idx[..., 1, None, :]
    kk = PEER_TOPK * PEER_TOPK
    top_vals, top_pos = lax.top_k(cand.reshape(bsz, s, PEER_HEADS, kk), PEER_TOPK)
    expert = jnp.take_along_axis(cand_idx.reshape(bsz, s, PEER_HEADS, kk), top_pos, axis=-1)
    gate = jax.nn.softmax(top_vals, axis=-1).astype(h.dtype)
    nblk = (bsz * s) // TOKEN_BLOCK
    hb = h.reshape(nblk, TOKEN_BLOCK, d)
    eb = expert.reshape(nblk, TOKEN_BLOCK, PEER_HEADS, PEER_TOPK)
    gb = gate.reshape(nblk, TOKEN_BLOCK, PEER_HEADS, PEER_TOPK)

    def block(args):
        h_i, e_i, g_i = args
        u = u_tab[e_i]
        a = g_i * jax.nn.gelu(jnp.einsum('td,thkd->thk', h_i, u))
        return jnp.einsum('thk,thkd->td', a, v_tab[e_i])

    out = lax.map(block, (hb, eb, gb))
    return out.reshape(bsz, s, d)


def setup_inputs(seed: int = 0) -> dict:
    key = jax.random.key(seed)
    ks = jax.random.split(key, 24)
    f32 = jnp.float32
    nrm = lambda k, shape, sc: jax.random.normal(k, shape, f32) * sc
    D = D_MODEL
    x = nrm(ks[0], (BATCH, SEQ, D), 1.0)
    norm1_g = 1.0 + nrm(ks[1], (DEPTH, D), 0.05)
    w_mix = nrm(ks[2], (DEPTH, D, OFF_D + 3 * BRANCH_W), D ** -0.5)
    w_fg = nrm(ks[3], (DEPTH, D, GROUPS), 0.1 * D ** -0.5)
    w_gate = nrm(ks[4], (DEPTH, D, N_BRANCH * D), D ** -0.5)
    w_in = jnp.concatenate([w_mix, w_fg, w_gate], axis=-1)
    conv_w = nrm(ks[5], (DEPTH, CONV_W, BRANCH_W), CONV_W ** -0.5)
    sgu_norm_g = 1.0 + nrm(ks[6], (DEPTH, BRANCH_W), 0.05)
    sgu_w = nrm(ks[7], (DEPTH, GROUPS, CHUNK, CHUNK), CHUNK ** -0.5)
    sgu_b = 1.0 + nrm(ks[8], (DEPTH, GROUPS, CHUNK), 0.1)
    pool_w = nrm(ks[9], (DEPTH, GROUPS, HEAD_DIM, HEAD_DIM), HEAD_DIM ** -0.5)
    pool_scale = 1.0 + nrm(ks[10], (DEPTH, BRANCH_W), 0.1)
    forget_b = 2.0 + nrm(ks[11], (DEPTH, GROUPS), 0.5)
    w_branch = nrm(ks[12], (DEPTH, N_BRANCH, BRANCH_W, D), BRANCH_W ** -0.5)
    w_out = nrm(ks[13], (DEPTH, D, D), D ** -0.5)
    norm2_g = 1.0 + nrm(ks[14], (DEPTH, D), 0.05)
    peer_wq = nrm(ks[15], (DEPTH, D, PEER_HEADS * PEER_DKEY), D ** -0.5)
    peer_keys = nrm(ks[16], (DEPTH, PEER_HEADS, 2, PEER_KEYS, PEER_HALF), PEER_HALF ** -0.5)
    peer_u = nrm(ks[17], (DEPTH, PEER_N, D), D ** -0.5)
    peer_v = nrm(ks[18], (DEPTH, PEER_N, D), (PEER_HEADS * PEER_TOPK) ** -0.5)
    final_g = 1.0 + nrm(ks[19], (D,), 0.05)
    return {"x": x, "norm1_g": norm1_g, "w_in": w_in, "conv_w": conv_w,
            "sgu_norm_g": sgu_norm_g, "sgu_w": sgu_w, "sgu_b": sgu_b,
            "pool_w": pool_w, "pool_scale": pool_scale, "forget_b": forget_b,
            "w_branch": w_branch, "w_out": w_out, "norm2_g": norm2_g,
            "peer_wq": peer_wq, "peer_keys": peer_keys, "peer_u": peer_u,
            "peer_v": peer_v, "final_g": final_g}


def reference(x, norm1_g, w_in, conv_w, sgu_norm_g, sgu_w, sgu_b, pool_w, pool_scale,
              forget_b, w_branch, w_out, norm2_g, peer_wq, peer_keys, peer_u, peer_v, final_g):
    bsz, s, d = x.shape
    h = x
    for l in range(DEPTH):
        xn = rmsnorm(h, norm1_g[l])
        z = xn @ w_in[l]
        oa = short_conv_mixer(z[..., OFF_A:OFF_B], conv_w[l])
        ob = sgu_mixer(z[..., OFF_B:OFF_C], sgu_norm_g[l], sgu_w[l], sgu_b[l])
        oc = pool_mixer(z[..., OFF_C:OFF_D], pool_w[l], pool_scale[l])
        od = forgetting_attention(z[..., OFF_D:OFF_G], forget_b[l])
        branches = jnp.stack([oa, ob, oc, od], axis=2)
        gates = jax.nn.sigmoid(z[..., OFF_G:].reshape(bsz, s, N_BRANCH, d))
        y = jnp.einsum('bsnm,nmd->bsnd', branches, w_branch[l])
        merged = jnp.einsum('bsnd,bsnd->bsd', gates, y)
        h = h + merged @ w_out[l]
        h = h + peer_ffn(rmsnorm(h, norm2_g[l]), peer_wq[l], peer_keys[l], peer_u[l], peer_v[l])
    return rmsnorm(h, final_g)
```

```python
import numpy as np
from contextlib import ExitStack
import concourse.bass as bass
import concourse.mybir as mybir
from concourse.bass_utils import run_bass_kernel_spmd

F32 = mybir.dt.float32
BF16 = mybir.dt.bfloat16
AF = mybir.ActivationFunctionType
ALU = mybir.AluOpType
AX = mybir.AxisListType

NCORES = 8
D = 2048
SEQ = 8192
T = SEQ // NCORES
NCH = D // 128
DEPTH = 4
BW = 512
OFF_A = 0
OFF_B = 1536
OFF_C = 2560
OFF_D = 3072
OFF_G = OFF_D + 1536 + 4
IN_COLS = OFF_G + 4 * D
EPS = 1e-6
NEG = -1.0e5
PEER_N = 16384


class Sched:
    def __init__(self, nc, n_dma_sems=24):
        self.nc = nc
        self.E = {'pe': nc.tensor, 'act': nc.scalar, 'dve': nc.vector,
                  'pool': nc.gpsimd, 'sp': nc.sync}
        self.sem = {e: nc.alloc_semaphore(name=f"sem_{e}") for e in ('pe', 'act', 'dve', 'pool')}
        self.cnt = {e: 0 for e in self.sem}
        self.dsem = [nc.alloc_semaphore(name=f"dsem{i}") for i in range(n_dma_sems)]
        self.dval = [0] * n_dma_sems
        self.dnext = 0
        self.known = {e: {} for e in self.E}
        self.lastw = {}
        self.readers = {}
        self.pend = {e: (set(), set()) for e in self.E}
        self.barrier = []
        self.seen = set()

    def _handle(self, key):
        return self.sem[key[1]] if key[0] == 'c' else self.dsem[key[1]]

    def _wait(self, e, toks):
        best = {}
        for (k, v) in toks:
            if v > best.get(k, 0):
                best[k] = v
        for k, v in best.items():
            if self.known[e].get(k, 0) >= v:
                continue
            if k == ('c', 'pe') and e == 'pe':
                continue
            self.E[e].wait_ge(self._handle(k), v)
            self.known[e][k] = v

    def scope_barrier(self):
        self.barrier = [(('c', k), v) for k, v in self.cnt.items() if v > 0]
        self.barrier += [(('d', i), v) for i, v in enumerate(self.dval) if v > 0]
        self.seen = set()

    def _deps(self, e, r, w):
        toks = []
        for k in w:
            if k not in self.seen:
                self.seen.add(k)
                toks.extend(self.barrier)
        for k in r:
            toks.extend(self.lastw.get(k, ()))
        for k in w:
            toks.extend(self.lastw.get(k, ()))
            for sk, sv in self.readers.get(k, {}).items():
                if sk == ('c', e):
                    continue
                toks.append((sk, sv))
        return toks

    def _register(self, tok, r, w, append=False):
        for k in w:
            if append:
                self.lastw.setdefault(k, []).append(tok)
            else:
                self.lastw[k] = [tok]
            self.readers[k] = {}
        for k in r:
            if k in w:
                continue
            d = self.readers.setdefault(k, {})
            if d.get(tok[0], 0) < tok[1]:
                d[tok[0]] = tok[1]

    def op(self, e, fn, r=(), w=(), fin=True):
        self._wait(e, self._deps(e, r, w))
        ins = fn()
        pr, pw = self.pend[e]
        pr.update(r)
        pw.update(w)
        if fin:
            self.cnt[e] += 1
            ins.then_inc(self.sem[e], 1)
            tok = (('c', e), self.cnt[e])
            self._register(tok, pr, pw)
            self.pend[e] = (set(), set())
        return ins

    def dma(self, q, out, in_, r=(), w=(), append=False):
        toks = self._deps(q, r, w) if not append else []
        i = self.dnext
        self.dnext = (self.dnext + 1) % len(self.dsem)
        if self.dval[i] > 0:
            toks.append((('d', i), self.dval[i]))
        self._wait(q, toks)
        ins = self.E[q].dma_start(out=out, in_=in_)
        self.dval[i] += 16
        ins.then_inc(self.dsem[i], 16)
        tok = (('d', i), self.dval[i])
        self._register(tok, set(r), set(w), append)
        return tok

    def wait_all(self, e):
        toks = [(('c', k), v) for k, v in self.cnt.items() if v > 0]
        toks += [(('d', i), v) for i, v in enumerate(self.dval) if v > 0]
        self._wait(e, toks)


class B:
    def __init__(self):
        self.nc = bass.Bass("TRN2", target_bir_lowering=False)
        self.S = Sched(self.nc)
        self.stack = ExitStack()
        self.uid = 0

    def din(self, name, shape, dt=F32):
        return self.nc.dram_tensor(name, list(shape), dt, kind="ExternalInput").ap()

    def dout(self, name, shape, dt=F32):
        return self.nc.dram_tensor(name, list(shape), dt, kind="ExternalOutput").ap()

    def sb(self, name, shape, dt=F32, stack=None):
        st = stack if stack is not None else self.stack
        return st.enter_context(self.nc.sbuf_tensor("s_" + name, list(shape), dt))

    def ps(self, name, stack=None):
        st = stack if stack is not None else self.stack
        return st.enter_context(self.nc.psum_tensor("p_" + name, [128, 512], F32))


def emit_consts(b, cst_d):
    S = b.S
    c = {}
    c['f32'] = b.sb("cst_f32", [128, 3, 128], F32)
    c['bf'] = b.sb("cst_bf", [128, 3, 128], BF16)
    S.dma('sp', c['f32'][:], cst_d, w=['cst_f32'])
    S.dma('pool', c['bf'][:], cst_d, w=['cst_bf'])
    c['eps'] = b.sb("cst_eps", [128, 1], F32)
    S.op('dve', lambda: b.nc.vector.memset(c['eps'][:], EPS), w=['cst_eps'])
    c['ident_f'] = c['f32'][:, 0, :]
    c['U_f'] = c['f32'][:, 1, :]
    c['ones_f'] = c['f32'][:, 2, :]
    c['ident_b'] = c['bf'][:, 0, :]
    c['U_b'] = c['bf'][:, 1, :]
    c['ones_b'] = c['bf'][:, 2, :]
    return c


def emit_rmsnorm(b, c, hT, hkey, g_sb, gkey, xnT, xkey, ps_list, ntok):
    S, nc = b.S, b.nc
    with ExitStack() as st:
        sq = [b.sb(f"rn_sq{i}_{b.uid}", [128, 512], F32, st) for i in range(2)]
        rstd = b.sb(f"rn_rstd_{b.uid}", [128, 512], F32, st)
        b.uid += 1
        for hf in range(ntok // 512):
            cols = slice(hf * 512, (hf + 1) * 512)
            ps = ps_list[hf % len(ps_list)]
            pk = ps['k']
            for ch in range(NCH):
                s = sq[ch % 2]
                sk = f"rn_sq{ch % 2}"
                S.op('act', lambda s=s, ch=ch: nc.scalar.activation(out=s[:], in_=hT[:, ch, cols], func=AF.Square),
                     r=[hkey], w=[sk])
                S.op('pe', lambda s=s, ch=ch: nc.tensor.matmul(ps['t'][:], c['ones_f'], s[:], start=(ch == 0), stop=(ch == NCH - 1)),
                     r=[sk, 'cst_f32'], w=[pk], fin=True)
            S.op('act', lambda: nc.scalar.activation(out=rstd[:], in_=ps['t'][:], func=AF.Sqrt, scale=1.0 / D, bias=c['eps'][:, 0:1]),
                 r=[pk, 'cst_eps'], w=['rn_rstd'])
            S.op('dve', lambda: nc.vector.reciprocal(out=rstd[:], in_=rstd[:]), r=['rn_rstd'], w=['rn_rstd'])
            for ch in range(NCH):
                S.op('dve', lambda ch=ch: nc.vector.scalar_tensor_tensor(
                    out=xnT[:, ch, cols], in0=hT[:, ch, cols], scalar=g_sb[:, ch:ch + 1], in1=rstd[:],
                    op0=ALU.mult, op1=ALU.mult), r=[hkey, gkey, 'rn_rstd'], w=[xkey])
    S.scope_barrier()


class WRing:
    def __init__(self, b, name, n, stack=None, width=512):
        self.b = b
        self.n = n
        self.name = name
        self.width = width
        self.t = [b.sb(f"{name}{i}", [128, NCH, width], BF16, stack) for i in range(n)]
        self.i = 0

    def load(self, src_rows_cols, ncols=None):
        b = self.b
        i = self.i
        self.i = (self.i + 1) % self.n
        t = self.t[i]
        key = f"{self.name}{i}"
        nco = src_rows_cols.shape[1]
        src = src_rows_cols.rearrange("(c p) n -> p c n", p=128)
        for q in range(4):
            b.S.dma('pool', t[:, 4 * q:4 * q + 4, 0:nco], src[:, 4 * q:4 * q + 4, :], w=[key], append=(q > 0))
        return t, key


def build_kv():
    b = B()
    nc, S = b.nc, b.S
    hT_d = b.din("hT", [128, NCH, T])
    g1_d = b.din("g1", [128, NCH])
    w_in = b.din("w_kv", [D, 2564])
    fb_d = b.din("fb", [128, 4])
    cst_d = b.din("cst", [128, 3, 128])
    kT_o = b.dout("kT", [4, 128, T], BF16)
    v_o = b.dout("v", [4, 128, T // 128, 128], BF16)
    fl_o = b.dout("flog", [128, T // 128, 4])
    halo_o = b.dout("halo", [128, 4, 17])

    c = emit_consts(b, cst_d)
    hT = b.sb("hT", [128, NCH, T])
    xnT = b.sb("xnT", [128, NCH, T], BF16)
    g1 = b.sb("g1", [128, NCH])
    fb = b.sb("fb", [128, 4])
    S.dma('sp', g1[:], g1_d, w=['g1'])
    S.dma('sp', fb[:], fb_d, w=['fb'])
    for ch in range(NCH):
        S.dma('sp', hT[:, ch, :], hT_d[:, ch, :], w=['hT'])
    P = [{'t': b.ps(f"ps{i}"), 'k': f"ps{i}"} for i in range(6)]
    ring = WRing(b, "wr", 3)
    ksb = b.sb("ksb", [128, 4, T], BF16)
    vsb = b.sb("vsb", [128, T // 128, 512], BF16)
    wf = b.sb("wf", [128, NCH, 4], BF16)
    fl = b.sb("fl", [128, T // 128, 4])
    halo = b.sb("halo", [128, 4, 17])
    ctmp = b.sb("ctmp", [128, 4, 16])
    emit_rmsnorm(b, c, hT, 'hT', g1, 'g1', xnT, 'xnT', P[0:2], T)
    wt, wk = ring.load(w_in[:, 0:512])
    pi = 0
    for h in range(4):
        for hf in range(T // 512):
            ps = P[pi % 4]
            pi += 1
            cols = slice(hf * 512, (hf + 1) * 512)
            for ch in range(NCH):
                S.op('pe', lambda ch=ch, ps=ps, cols=cols, h=h: nc.tensor.matmul(
                    ps['t'][:], wt[:, ch, h * 128:(h + 1) * 128], xnT[:, ch, cols], start=(ch == 0), stop=(ch == NCH - 1)),
                    r=[wk, 'xnT'], w=[ps['k']], fin=(ch == NCH - 1))
            S.op('act', lambda ps=ps, cols=cols, h=h: nc.scalar.copy(out=ksb[:, h, cols], in_=ps['t'][:]),
                 r=[ps['k']], w=['ksb'])
    for h in range(4):
        S.dma('sp', kT_o[h], ksb[:, h, :], r=['ksb'])
    wt, wk = ring.load(w_in[:, 512:1024])
    for tt in range(T // 128):
        ps = P[pi % 4]
        pi += 1
        tc = slice(tt * 128, (tt + 1) * 128)
        for ch in range(NCH):
            S.op('pe', lambda ch=ch, ps=ps, tc=tc: nc.tensor.matmul(
                ps['t'][:], xnT[:, ch, tc], wt[:, ch, :], start=(ch == 0), stop=(ch == NCH - 1)),
                r=[wk, 'xnT'], w=[ps['k']], fin=(ch == NCH - 1))
        S.op('act', lambda ps=ps, tt=tt: nc.scalar.copy(out=vsb[:, tt, :], in_=ps['t'][:]), r=[ps['k']], w=['vsb'])
    for h in range(4):
        S.dma('sp', v_o[h], vsb[:, :, h * 128:(h + 1) * 128], r=['vsb'])
    S.dma('pool', wf[:], w_in[:, 2560:2564].rearrange("(c p) n -> p c n", p=128), w=['wf'])
    psf = P[4]
    for tt in range(T // 128):
        tc = slice(tt * 128, (tt + 1) * 128)
        for ch in range(NCH):
            S.op('pe', lambda ch=ch, tc=tc, tt=tt: nc.tensor.matmul(
                psf['t'][:, tt * 4:(tt + 1) * 4], xnT[:, ch, tc], wf[:, ch, :], start=(ch == 0), stop=(ch == NCH - 1)),
                r=['wf', 'xnT'], w=[psf['k']], fin=(ch == NCH - 1 and tt == T // 128 - 1))
    S.op('dve', lambda: nc.vector.tensor_tensor(
        out=fl[:], in0=psf['t'][:, 0:(T // 128) * 4].rearrange("p (j h) -> p j h", h=4),
        in1=fb[:].unsqueeze(1).to_broadcast([128, T // 128, 4]), op=ALU.add), r=[psf['k'], 'fb'], w=['fl'])
    S.op('act', lambda: nc.scalar.activation(out=fl[:], in_=fl[:], func=AF.Exp, scale=-1.0), r=['fl'], w=['fl'])
    S.op('act', lambda: nc.scalar.activation(out=fl[:], in_=fl[:], func=AF.Ln, bias=1.0), r=['fl'], w=['fl'])
    S.op('dve', lambda: nc.vector.tensor_scalar(out=fl[:], in0=fl[:], scalar1=-1.0, scalar2=None, op0=ALU.mult),
         r=['fl'], w=['fl'])
    S.dma('sp', fl_o, fl[:], r=['fl'])
    tl = slice(T - 16, T)
    psh = P[5]
    col_sets = [1024, 1536, 2048]
    for si, c0 in enumerate(col_sets):
        wt, wk = ring.load(w_in[:, c0:c0 + 512])
        for dt in range(4):
            o = (si * 4 + dt) * 16
            for ch in range(NCH):
                S.op('pe', lambda ch=ch, dt=dt, o=o, wt=wt: nc.tensor.matmul(
                    psh['t'][:, o:o + 16], wt[:, ch, dt * 128:(dt + 1) * 128], xnT[:, ch, tl],
                    start=(ch == 0), stop=(ch == NCH - 1)),
                    r=[wk, 'xnT'], w=[psh['k']], fin=(ch == NCH - 1 and dt == 3))
    S.op('act', lambda: nc.scalar.copy(out=ctmp[:], in_=psh['t'][:, 0:64].rearrange("p (d t) -> p d t", t=16)),
         r=[psh['k']], w=['ctmp'])
    S.op('dve', lambda: nc.vector.tensor_tensor(
        out=halo[:, :, 0:2], in0=ctmp[:, :, 14:16],
        in1=psh['t'][:, 64:128].rearrange("p (d t) -> p d t", t=16)[:, :, 14:16], op=ALU.mult),
        r=['ctmp', psh['k']], w=['halo'])
    S.op('dve', lambda: nc.vector.tensor_copy(
        out=halo[:, :, 2:17], in_=psh['t'][:, 128:192].rearrange("p (d t) -> p d t", t=16)[:, :, 1:16]),
        r=[psh['k']], w=['halo'])
    S.dma('sp', halo_o, halo[:], r=['halo'])
    S.wait_all('sp')
    b.stack.close()
    return nc


def build_mix():
    b = B()
    nc, S = b.nc, b.S
    NB = SEQ // 128
    NJ = T // 128
    NH = T // 512
    hT_d = b.din("hT", [128, NCH, T])
    g1_d = b.din("g1", [128, NCH])
    w_in = b.din("w_in", [D, IN_COLS])
    cw_d = b.din("cw", [128, 4, 3])
    sgg_d = b.din("sgg", [128, 512])
    swT_d = b.din("swT", [128, 4, 128])
    sgb_d = b.din("sgb", [128, 4, 128])
    pw_d = b.din("pw", [128, 4, 128])
    psc_d = b.din("psc", [128, 4])
    wbr_d = b.din("wbr", [4, 512, D])
    wout_d = b.din("wout", [D, D])
    KT_d = b.din("KT", [4, 128, SEQ], BF16)
    V_d = b.din("V", [4, 128, NB, 128], BF16)
    fla_d = b.din("fla", [128, NB, 4])
    flo_d = b.din("flo", [128, NJ, 4])
    bef_d = b.din("bef", [128, NB])
    halo_d = b.din("halo", [128, 4, 17])
    qpos_d = b.din("qpos", [128, T])
    kpos_d = b.din("kpos", [128, NB])
    cst_d = b.din("cst", [128, 3, 128])
    hT_o = b.dout("hTo", [128, NCH, T])
    dbg_br = b.dout("dbg_br", [128, 4, 4, T], BF16)
    dbg_m = b.dout("dbg_m", [128, NCH, T], BF16)

    c = emit_consts(b, cst_d)
    P = [{'t': b.ps(f"ps{i}"), 'k': f"ps{i}"} for i in range(8)]
    xnT = b.sb("xnT", [128, NCH, T], BF16)
    br = b.sb("br", [128, 4, 4, T], BF16)
    g1 = b.sb("g1", [128, NCH])
    cw = b.sb("cw", [128, 4, 3])
    sgg = b.sb("sgg", [128, 512])
    swT = b.sb("swT", [128, 4, 128])
    swTm = b.sb("swTm", [128, 4, 128], BF16)
    sgb = b.sb("sgb", [128, 4, 128])
    pwb = b.sb("pwb", [128, 4, 128], BF16)
    psc = b.sb("psc", [128, 4])
    halo = b.sb("halo", [128, 4, 17])
    qpos = b.sb("qpos", [128, T])
    kpos = b.sb("kpos", [128, NB])
    for (t_, d_, k_) in [(g1, g1_d, 'g1'), (cw, cw_d, 'cw'), (sgg, sgg_d, 'sgg'), (swT, swT_d, 'swT'), (sgb, sgb_d, 'sgb'),
                         (psc, psc_d, 'psc'), (halo, halo_d, 'halo'), (qpos, qpos_d, 'qpos'), (kpos, kpos_d, 'kpos')]:
        S.dma('sp', t_[:], d_, w=[k_])
    S.dma('pool', pwb[:], pw_d, w=['pwb'])
    S.op('dve', lambda: nc.vector.tensor_tensor(out=swTm[:], in0=swT[:], in1=c['U_f'].unsqueeze(1).to_broadcast([128, 4, 128]),
                                                op=ALU.mult), r=['swT', 'cst_f32'], w=['swTm'])

    with ExitStack() as st0:
        hT = b.sb("hT", [128, NCH, T], F32, st0)
        for ch in range(NCH):
            S.dma('sp', hT[:, ch, :], hT_d[:, ch, :], w=[f'hT{ch}'])
        S.op('act', lambda: nc.scalar.copy(out=c['eps'][:, 0:1], in_=c['eps'][:, 0:1]),
             r=[f'hT{ch}' for ch in range(NCH)] + ['cst_eps'], w=['hTj'])
        emit_rmsnorm(b, c, hT, 'hTj', g1, 'g1', xnT, 'xnT', P[0:2], T)
    S.scope_barrier()

    stA = ExitStack()
    negFk = b.sb("negFk", [128, 4, NB], F32, stA)
    FqB = b.sb("FqB", [128, 4, T], F32, stA)
    qT = b.sb("qT", [128, 4, T], BF16, stA)
    with ExitStack() as stf:
        FL = b.sb("FL", [128, NB * 4], F32, stf)
        FLo = b.sb("FLo", [128, NJ * 4], F32, stf)
        bef = b.sb("bef", [128, NB], F32, stf)
        TotT = b.sb("TotT", [128, 4, NB], F32, stf)
        FlocT = b.sb("FlocT", [128, 4, NB], F32, stf)
        incl = b.sb("incl", [128, 4, NB], F32, stf)
        ones64 = b.sb("ones64", [128, NB], F32, stf)
        pref = b.sb("pref", [128, 4], F32, stf)
        tmpF = b.sb("tmpF", [128, 4, NB], F32, stf)
        TotoT = b.sb("TotoT", [128, 4, NJ], F32, stf)
        FlocoT = b.sb("FlocoT", [128, 4, NJ], F32, stf)
        inclo = b.sb("inclo", [128, 4, NJ], F32, stf)
        Fown = b.sb("Fown", [128, 4, NJ], F32, stf)
        dg = [b.sb(f"dg{i}", [128, 128], F32, stf) for i in range(2)]
        S.dma('sp', FL[:], fla_d.rearrange("p b h -> p (b h)"), w=['FL'])
        S.dma('sp', FLo[:], flo_d.rearrange("p b h -> p (b h)"), w=['FLo'])
        S.dma('sp', bef[:], bef_d, w=['bef'])
        S.op('dve', lambda: nc.vector.memset(ones64[:], 1.0), w=['ones64'])
        S.op('pe', lambda: nc.tensor.matmul(P[2]['t'][:, 0:NB * 4], c['U_f'], FL[:], start=True, stop=True),
             r=['FL', 'cst_f32'], w=['ps2'])
        S.op('pe', lambda: nc.tensor.matmul(P[3]['t'][:, 0:NB * 4], c['ones_f'], FL[:], start=True, stop=True),
             r=['FL', 'cst_f32'], w=['ps3'])
        S.op('act', lambda: nc.scalar.copy(out=FlocT[:], in_=P[2]['t'][:, 0:NB * 4].rearrange("p (b h) -> p h b", h=4)),
             r=['ps2'], w=['FlocT'])
        S.op('act', lambda: nc.scalar.copy(out=TotT[:], in_=P[3]['t'][:, 0:NB * 4].rearrange("p (b h) -> p h b", h=4)),
             r=['ps3'], w=['TotT'])
        for h in range(4):
            S.op('dve', lambda h=h: nc.vector.tensor_tensor_scan(out=incl[:, h, :], data0=ones64[:], data1=TotT[:, h, :],
                                                                initial=0.0, op0=ALU.mult, op1=ALU.add),
                 r=['ones64', 'TotT'], w=['incl'])
        S.op('dve', lambda: nc.vector.tensor_tensor(out=tmpF[:], in0=incl[:], in1=TotT[:], op=ALU.subtract),
             r=['incl', 'TotT'], w=['tmpF'])
        S.op('dve', lambda: nc.vector.tensor_tensor(out=tmpF[:], in0=tmpF[:], in1=FlocT[:], op=ALU.add),
             r=['tmpF', 'FlocT'], w=['tmpF'])
        S.op('dve', lambda: nc.vector.tensor_scalar(out=negFk[:], in0=tmpF[:], scalar1=-1.0, scalar2=None, op0=ALU.mult),
             r=['tmpF'], w=['negFk'])
        S.op('dve', lambda: nc.vector.tensor_tensor(out=tmpF[:], in0=TotT[:], in1=bef[:].unsqueeze(1).to_broadcast([128, 4, NB]),
                                                    op=ALU.mult), r=['TotT', 'bef', 'negFk'], w=['tmpF'])
        S.op('dve', lambda: nc.vector.reduce_sum(out=pref[:], in_=tmpF[:], axis=AX.X), r=['tmpF'], w=['pref'])
        S.op('pe', lambda: nc.tensor.matmul(P[2]['t'][:, 0:NJ * 4], c['U_f'], FLo[:], start=True, stop=True),
             r=['FLo', 'cst_f32'], w=['ps2'])
        S.op('pe', lambda: nc.tensor.matmul(P[3]['t'][:, 0:NJ * 4], c['ones_f'], FLo[:], start=True, stop=True),
             r=['FLo', 'cst_f32'], w=['ps3'])
        S.op('act', lambda: nc.scalar.copy(out=FlocoT[:], in_=P[2]['t'][:, 0:NJ * 4].rearrange("p (b h) -> p h b", h=4)),
             r=['ps2'], w=['FlocoT'])
        S.op('act', lambda: nc.scalar.copy(out=TotoT[:], in_=P[3]['t'][:, 0:NJ * 4].rearrange("p (b h) -> p h b", h=4)),
             r=['ps3'], w=['TotoT'])
        for h in range(4):
            S.op('dve', lambda h=h: nc.vector.tensor_tensor_scan(out=inclo[:, h, :], data0=ones64[:, 0:NJ], data1=TotoT[:, h, :],
                                                                initial=0.0, op0=ALU.mult, op1=ALU.add),
                 r=['ones64', 'TotoT'], w=['inclo'])
        S.op('dve', lambda: nc.vector.tensor_tensor(out=Fown[:], in0=inclo[:], in1=TotoT[:], op=ALU.subtract),
             r=['inclo', 'TotoT'], w=['Fown'])
        S.op('dve', lambda: nc.vector.tensor_tensor(out=Fown[:], in0=Fown[:], in1=FlocoT[:], op=ALU.add),
             r=['Fown', 'FlocoT'], w=['Fown'])
        S.op('dve', lambda: nc.vector.tensor_tensor(out=Fown[:], in0=Fown[:], in1=pref[:].unsqueeze(2).to_broadcast([128, 4, NJ]),
                                                    op=ALU.add), r=['Fown', 'pref'], w=['Fown'])
        for h in range(4):
            for hf in range(NH):
                ps = P[4 + (h * NH + hf) % 2]
                for jj in range(4):
                    j = hf * 4 + jj
                    d = dg[(j + h) % 2]
                    dk = f"dg{(j + h) % 2}"
                    S.op('dve', lambda d=d, h=h, j=j: nc.vector.tensor_scalar(out=d[:], in0=c['ident_f'], scalar1=Fown[:, h, j:j + 1],
                                                                            scalar2=None, op0=ALU.mult),
                         r=['Fown', 'cst_f32'], w=[dk])
                    S.op('pe', lambda d=d, jj=jj, ps=ps: nc.tensor.matmul(ps['t'][:, jj * 128:(jj + 1) * 128], c['ones_f'], d[:],
                                                                       start=True, stop=True),
                         r=[dk, 'cst_f32'], w=[ps['k']])
                S.op('act', lambda ps=ps, h=h, hf=hf: nc.scalar.copy(out=FqB[:, h, hf * 512:(hf + 1) * 512], in_=ps['t'][:]),
                     r=[ps['k']], w=['FqB'])
    S.scope_barrier()

    with ExitStack() as st1:
        ring = WRing(b, "wr", 3, st1)
        zz = b.sb("zz", [128, T + 2], F32, st1)
        bsb = b.sb("bsb", [128, T], F32, st1)
        acc = b.sb("acc", [128, T], F32, st1)
        ctmp = b.sb("ctmp", [128, 512], F32, st1)
        ugT = b.sb("ugT", [128, 4, T], BF16, st1)
        gv = b.sb("gv", [128, 512], F32, st1)
        junk = b.sb("junk", [128, 512], F32, st1)
        ss = b.sb("ss", [128, 1], F32, st1)
        vn = b.sb("vn", [128, NJ, 512], BF16, st1)
        pz = b.sb("pz", [128, T + 15], F32, st1)
        pa = b.sb("pa", [128, T + 15], F32, st1)
        pb = b.sb("pb", [128, T + 15], F32, st1)
        idv = b.sb("idv", [128, T], F32, st1)
        pl = b.sb("pl", [128, T], BF16, st1)
        pi = [0]

        def nextps():
            p_ = P[pi[0] % 6]
            pi[0] += 1
            return p_

        def proj_fm(wt, wk, dt, hf, ps):
            cols = slice(hf * 512, (hf + 1) * 512)
            for ch in range(NCH):
                S.op('pe', lambda ch=ch: nc.tensor.matmul(ps['t'][:], wt[:, ch, dt * 128:(dt + 1) * 128], xnT[:, ch, cols],
                                                         start=(ch == 0), stop=(ch == NCH - 1)),
                     r=[wk, 'xnT'], w=[ps['k']], fin=(ch == NCH - 1))

        wt, wk = ring.load(w_in[:, OFF_D:OFF_D + 512])
        for h in range(4):
            for hf in range(NH):
                ps = nextps()
                proj_fm(wt, wk, h, hf, ps)
                S.op('act', lambda ps=ps, h=h, hf=hf: nc.scalar.copy(out=qT[:, h, hf * 512:(hf + 1) * 512], in_=ps['t'][:]),
                     r=[ps['k']], w=['qT'])
        wb_t, wb_k = ring.load(w_in[:, OFF_A:OFF_A + 512])
        wc_t, wc_k = ring.load(w_in[:, OFF_A + 512:OFF_A + 1024])
        wh_t, wh_k = ring.load(w_in[:, OFF_A + 1024:OFF_A + 1536])
        for dt in range(4):
            S.op('dve', lambda dt=dt: nc.vector.tensor_copy(out=zz[:, 0:2], in_=halo[:, dt, 0:2]), r=['halo'], w=['zz'])
            for hf in range(NH):
                cols = slice(hf * 512, (hf + 1) * 512)
                pc, ph, pb_ = nextps(), nextps(), nextps()
                proj_fm(wc_t, wc_k, dt, hf, pc)
                proj_fm(wh_t, wh_k, dt, hf, ph)
                proj_fm(wb_t, wb_k, dt, hf, pb_)
                S.op('act', lambda pc=pc: nc.scalar.copy(out=ctmp[:], in_=pc['t'][:]), r=[pc['k']], w=['ctmp'])
                S.op('dve', lambda ph=ph, hf=hf: nc.vector.tensor_tensor(out=zz[:, 2 + hf * 512:2 + (hf + 1) * 512], in0=ctmp[:],
                                                                        in1=ph['t'][:], op=ALU.mult),
                     r=['ctmp', ph['k']], w=['zz'])
                S.op('act', lambda pb_=pb_, cols=cols: nc.scalar.copy(out=bsb[:, cols], in_=pb_['t'][:]), r=[pb_['k']], w=['bsb'])
            S.op('dve', lambda dt=dt: nc.vector.tensor_scalar(out=acc[:], in0=zz[:, 2:T + 2], scalar1=cw[:, dt, 2:3], scalar2=None,
                                                             op0=ALU.mult), r=['zz', 'cw'], w=['acc'])
            S.op('dve', lambda dt=dt: nc.vector.scalar_tensor_tensor(out=acc[:], in0=zz[:, 1:T + 1], scalar=cw[:, dt, 1:2], in1=acc[:],
                                                                    op0=ALU.mult, op1=ALU.add), r=['zz', 'cw', 'acc'], w=['acc'])
            S.op('dve', lambda dt=dt: nc.vector.scalar_tensor_tensor(out=acc[:], in0=zz[:, 0:T], scalar=cw[:, dt, 0:1], in1=acc[:],
                                                                    op0=ALU.mult, op1=ALU.add), r=['zz', 'cw', 'acc'], w=['acc'])
            S.op('dve', lambda dt=dt: nc.vector.tensor_tensor(out=br[:, 0, dt, :], in0=acc[:], in1=bsb[:], op=ALU.mult),
                 r=['acc', 'bsb'], w=['br0'])
        wu_t, wu_k = ring.load(w_in[:, OFF_B:OFF_B + 512])
        wv_t, wv_k = ring.load(w_in[:, OFF_B + 512:OFF_B + 1024])
        for dt in range(4):
            for hf in range(NH):
                ps = nextps()
                proj_fm(wu_t, wu_k, dt, hf, ps)
                S.op('act', lambda ps=ps, dt=dt, hf=hf: nc.scalar.activation(out=ugT[:, dt, hf * 512:(hf + 1) * 512], in_=ps['t'][:],
                                                                            func=AF.Gelu_apprx_tanh), r=[ps['k']], w=['ugT'])
        for tt in range(NJ):
            ps = nextps()
            tc = slice(tt * 128, (tt + 1) * 128)
            for ch in range(NCH):
                S.op('pe', lambda ch=ch, ps=ps, tc=tc: nc.tensor.matmul(ps['t'][:], xnT[:, ch, tc], wv_t[:, ch, :],
                                                                       start=(ch == 0), stop=(ch == NCH - 1)),
                     r=[wv_k, 'xnT'], w=[ps['k']], fin=(ch == NCH - 1))
            S.op('act', lambda ps=ps: nc.scalar.activation(out=gv[:], in_=ps['t'][:], func=AF.Gelu_apprx_tanh), r=[ps['k']], w=['gv'])
            S.op('act', lambda: nc.scalar.activation(out=junk[:], in_=gv[:], func=AF.Square, accum_out=ss[:, 0:1]),
                 r=['gv'], w=['junk', 'ss'])
            S.op('act', lambda: nc.scalar.activation(out=ss[:], in_=ss[:], func=AF.Sqrt, scale=1.0 / 512, bias=c['eps'][:, 0:1]),
                 r=['ss', 'cst_eps'], w=['ss'])
            S.op('dve', lambda: nc.vector.reciprocal(out=ss[:], in_=ss[:]), r=['ss'], w=['ss'])
            S.op('dve', lambda tt=tt: nc.vector.scalar_tensor_tensor(out=vn[:, tt, :], in0=gv[:], scalar=ss[:, 0:1], in1=sgg[:],
                                                                    op0=ALU.mult, op1=ALU.mult), r=['gv', 'ss', 'sgg'], w=['vn'])
        for g in range(4):
            for hf in range(NH):
                ps = nextps()
                for jj in range(4):
                    S.op('pe', lambda jj=jj, ps=ps, g=g, hf=hf: nc.tensor.matmul(
                        ps['t'][:, jj * 128:(jj + 1) * 128], vn[:, hf * 4 + jj, g * 128:(g + 1) * 128], swTm[:, g, :],
                        start=True, stop=True), r=['vn', 'swTm'], w=[ps['k']], fin=(jj == 3))
                S.op('dve', lambda ps=ps, g=g: nc.vector.tensor_tensor(
                    out=ctmp[:].rearrange("p (j t) -> p j t", t=128), in0=ps['t'][:].rearrange("p (j t) -> p j t", t=128),
                    in1=sgb[:, g, :].unsqueeze(1).to_broadcast([128, 4, 128]), op=ALU.add), r=[ps['k'], 'sgb'], w=['ctmp'])
                S.op('dve', lambda g=g, hf=hf: nc.vector.tensor_tensor(out=br[:, 1, g, hf * 512:(hf + 1) * 512], in0=ctmp[:],
                                                                      in1=ugT[:, g, hf * 512:(hf + 1) * 512], op=ALU.mult),
                     r=['ctmp', 'ugT'], w=['br1'])
        wp_t, wp_k = ring.load(w_in[:, OFF_C:OFF_C + 512])
        L = T + 15
        for g in range(4):
            w_ = 2 ** (g + 1)
            S.op('dve', lambda g=g: nc.vector.tensor_copy(out=pz[:, 0:15], in_=halo[:, g, 2:17]), r=['halo'], w=['pz'])
            for hf in range(NH):
                ps = nextps()
                proj_fm(wp_t, wp_k, g, hf, ps)
                S.op('act', lambda ps=ps, hf=hf: nc.scalar.copy(out=pz[:, 15 + hf * 512:15 + (hf + 1) * 512], in_=ps['t'][:]),
                     r=[ps['k']], w=['pz'])
            src, sk = pz, 'pz'
            bufs = [(pa, 'pa'), (pb, 'pb')]
            for k_ in range(g + 1):
                sh = 2 ** k_
                lo = 2 ** (k_ + 1) - 1
                dst, dk = bufs[k_ % 2]
                S.op('dve', lambda src=src, dst=dst, lo=lo, sh=sh: nc.vector.tensor_tensor(
                    out=dst[:, lo:L], in0=src[:, lo:L], in1=src[:, lo - sh:L - sh], op=ALU.add), r=[sk], w=[dk])
                src, sk = dst, dk
            S.op('dve', lambda w_=w_: nc.vector.tensor_scalar(out=idv[:], in0=qpos[:], scalar1=1.0, scalar2=float(w_),
                                                             op0=ALU.add, op1=ALU.min), r=['qpos'], w=['idv'])
            S.op('dve', lambda: nc.vector.reciprocal(out=idv[:], in_=idv[:]), r=['idv'], w=['idv'])
            S.op('dve', lambda src=src: nc.vector.tensor_tensor(out=idv[:], in0=src[:, 15:L], in1=idv[:], op=ALU.mult),
                 r=[sk, 'idv'], w=['idv'])
            S.op('dve', lambda: nc.vector.tensor_tensor(out=pl[:], in0=idv[:], in1=pz[:, 15:L], op=ALU.subtract),
                 r=['idv', 'pz'], w=['pl'])
            for hf in range(NH):
                ps = nextps()
                S.op('pe', lambda ps=ps, g=g, hf=hf: nc.tensor.matmul(ps['t'][:], pwb[:, g, :], pl[:, hf * 512:(hf + 1) * 512],
                                                                     start=True, stop=True), r=['pwb', 'pl'], w=[ps['k']])
                S.op('dve', lambda ps=ps, g=g, hf=hf: nc.vector.tensor_scalar(out=br[:, 2, g, hf * 512:(hf + 1) * 512], in0=ps['t'][:],
                                                                             scalar1=psc[:, g:g + 1], scalar2=None, op0=ALU.mult),
                     r=[ps['k'], 'psc'], w=['br2'])
    S.scope_barrier()

    scale = 128.0 ** -0.5
    with ExitStack() as st2:
        KTh = b.sb("KTh", [128, SEQ], BF16, st2)
        Vh = b.sb("Vh", [128, NB, 128], BF16, st2)
        mk = [b.sb(f"mk{i}", [128, 512], F32, st2) for i in range(4)]
        tm = [b.sb(f"tm{i}", [128, 512], F32, st2) for i in range(4)]
        PT = [b.sb(f"PT{i}", [128, 512], BF16, st2) for i in range(4)]
        pSb = [P[0], P[1], P[6], P[7]]
        rr = b.sb("rr", [128, 512], F32, st2)
        it = 0
        for h in range(4):
            for q4 in range(4):
                S.dma('sp', KTh[:, q4 * 2048:(q4 + 1) * 2048], KT_d[h][:, q4 * 2048:(q4 + 1) * 2048], w=['KTh'], append=(q4 > 0))
            for q4 in range(4):
                S.dma('sp', Vh[:, q4 * 16:(q4 + 1) * 16, :], V_d[h][:, q4 * 16:(q4 + 1) * 16, :], w=['Vh'], append=(q4 > 0))
            for hf in range(NH):
                cols = slice(hf * 512, (hf + 1) * 512)
                pv = P[2 + (h * NH + hf) % 2]
                rs = P[4 + (h * NH + hf) % 2]
                for bk in range(NB):
                    i2 = it % 4
                    it += 1
                    pS = pSb[i2]
                    S.op('dve', lambda i2=i2, bk=bk: nc.vector.tensor_scalar(out=mk[i2][:], in0=qpos[:, cols], scalar1=kpos[:, bk:bk + 1],
                                                                            scalar2=NEG, op0=ALU.is_lt, op1=ALU.mult),
                         r=['qpos', 'kpos'], w=[f'mk{i2}'])
                    S.op('pe', lambda pS=pS, bk=bk: nc.tensor.matmul(pS['t'][:], KTh[:, bk * 128:(bk + 1) * 128], qT[:, h, cols],
                                                                    start=True, stop=True), r=['KTh', 'qT'], w=[pS['k']])
                    S.op('dve', lambda pS=pS, i2=i2: nc.vector.scalar_tensor_tensor(out=tm[i2][:], in0=pS['t'][:], scalar=scale,
                                                                                  in1=FqB[:, h, cols], op0=ALU.mult, op1=ALU.add),
                         r=[pS['k'], 'FqB'], w=[f'tm{i2}'])
                    S.op('pool', lambda i2=i2: nc.gpsimd.tensor_tensor(out=tm[i2][:], in0=tm[i2][:], in1=mk[i2][:], op=ALU.add),
                         r=[f'tm{i2}', f'mk{i2}'], w=[f'tm{i2}'])
                    S.op('act', lambda i2=i2, bk=bk: nc.scalar.activation(out=PT[i2][:], in_=tm[i2][:], func=AF.Exp,
                                                                         bias=negFk[:, h, bk:bk + 1]),
                         r=[f'tm{i2}', 'negFk'], w=[f'PT{i2}'])
                    S.op('pe', lambda i2=i2, bk=bk: nc.tensor.matmul(pv['t'][:], Vh[:, bk, :], PT[i2][:], start=(bk == 0), stop=(bk == NB - 1)),
                         r=['Vh', f'PT{i2}'], w=[pv['k']], fin=False)
                    S.op('pe', lambda i2=i2, bk=bk: nc.tensor.matmul(rs['t'][:], c['ones_b'], PT[i2][:], start=(bk == 0), stop=(bk == NB - 1)),
                         r=['cst_bf', f'PT{i2}'], w=[rs['k']], fin=True)
                S.op('dve', lambda: nc.vector.reciprocal(out=rr[:], in_=rs['t'][:]), r=[rs['k']], w=['rr'])
                S.op('dve', lambda: nc.vector.tensor_tensor(out=br[:, 3, h, cols], in0=pv['t'][:], in1=rr[:], op=ALU.mult),
                     r=[pv['k'], 'rr'], w=['br3'])
    for n_ in range(4):
        S.dma('sp', dbg_br[:, n_], br[:, n_], r=[f'br{n_}'])
    stA.close()
    S.scope_barrier()

    with ExitStack() as st3:
        ring = WRing(b, "wg", 2, st3)
        wbt = [b.sb(f"wbt{i}", [128, 4, 512], BF16, st3) for i in range(2)]
        mergedT = b.sb("mergedT", [128, NCH, T], BF16, st3)
        accs = [b.sb(f"macc{i}", [128, 512], F32, st3) for i in range(8)]
        gs = [b.sb(f"gs{i}", [128, 512], F32, st3) for i in range(2)]
        tmpm = [b.sb(f"tmpm{i}", [128, 512], F32, st3) for i in range(2)]
        hres = [b.sb(f"hres{i}", [128, 512], F32, st3) for i in range(2)]
        otl = [b.sb(f"otl{i}", [128, 512], F32, st3) for i in range(2)]
        it = 0
        items = [(dq, n) for dq in range(4) for n in range(4)]
        loaded = {}

        def issue(i):
            dq, n = items[i]
            wt, wk = ring.load(w_in[:, OFF_G + n * D + dq * 512:OFF_G + n * D + (dq + 1) * 512])
            wb_, wbk = wbt[i % 2], f"wbt{i % 2}"
            S.dma('pool', wb_[:], wbr_d[n][:, dq * 512:(dq + 1) * 512].rearrange("(c p) d -> p c d", p=128), w=[wbk])
            loaded[i] = (wt, wk, wb_, wbk)

        issue(0)
        oproj0 = None
        for i, (dq, n) in enumerate(items):
            if i + 1 < len(items):
                issue(i + 1)
            else:
                oproj0 = ring.load(wout_d[:, 0:512])
            wt, wk, wb_, wbk = loaded[i]
            for dtl in range(4):
                for hf in range(NH):
                    cols = slice(hf * 512, (hf + 1) * 512)
                    i2 = it % 2
                    it += 1
                    pg, py = P[i2], P[2 + i2]
                    for ch in range(NCH):
                        S.op('pe', lambda ch=ch: nc.tensor.matmul(pg['t'][:], wt[:, ch, dtl * 128:(dtl + 1) * 128], xnT[:, ch, cols],
                                                                 start=(ch == 0), stop=(ch == NCH - 1)),
                             r=[wk, 'xnT'], w=[pg['k']], fin=(ch == NCH - 1))
                    for mc in range(4):
                        S.op('pe', lambda mc=mc: nc.tensor.matmul(py['t'][:], wb_[:, mc, dtl * 128:(dtl + 1) * 128], br[:, n, mc, cols],
                                                                 start=(mc == 0), stop=(mc == 3)),
                             r=[wbk, f'br{n}'], w=[py['k']], fin=(mc == 3))
                    S.op('act', lambda: nc.scalar.activation(out=gs[i2][:], in_=pg['t'][:], func=AF.Sigmoid),
                         r=[pg['k']], w=[f'gs{i2}'])
                    a_ = accs[dtl * NH + hf]
                    ak = f"macc{dtl * NH + hf}"
                    if n == 0:
                        S.op('dve', lambda: nc.vector.tensor_tensor(out=a_[:], in0=gs[i2][:], in1=py['t'][:], op=ALU.mult),
                             r=[f'gs{i2}', py['k']], w=[ak])
                    else:
                        S.op('dve', lambda: nc.vector.tensor_tensor(out=tmpm[i2][:], in0=gs[i2][:], in1=py['t'][:], op=ALU.mult),
                             r=[f'gs{i2}', py['k']], w=[f'tmpm{i2}'])
                        if n < 3:
                            S.op('dve', lambda: nc.vector.tensor_tensor(out=a_[:], in0=a_[:], in1=tmpm[i2][:], op=ALU.add),
                                 r=[ak, f'tmpm{i2}'], w=[ak])
                        else:
                            S.op('dve', lambda: nc.vector.tensor_tensor(out=mergedT[:, dq * 4 + dtl, cols], in0=a_[:], in1=tmpm[i2][:],
                                                                       op=ALU.add), r=[ak, f'tmpm{i2}'], w=['mergedT'])
        for ch in range(NCH):
            S.dma('sp', dbg_m[:, ch, :], mergedT[:, ch, :], r=['mergedT'])
        it = 0
        nxt = oproj0
        for dq in range(4):
            wt, wk = nxt
            if dq + 1 < 4:
                nxt = ring.load(wout_d[:, (dq + 1) * 512:(dq + 2) * 512])
            for dtl in range(4):
                dt = dq * 4 + dtl
                for hf in range(NH):
                    cols = slice(hf * 512, (hf + 1) * 512)
                    i2 = it % 2
                    it += 1
                    po = P[4 + i2]
                    S.dma('sp', hres[i2][:], hT_d[:, dt, cols], w=[f'hres{i2}'])
                    for ch in range(NCH):
                        S.op('pe', lambda ch=ch: nc.tensor.matmul(po['t'][:], wt[:, ch, dtl * 128:(dtl + 1) * 128], mergedT[:, ch, cols],
                                                                 start=(ch == 0), stop=(ch == NCH - 1)),
                             r=[wk, 'mergedT'], w=[po['k']], fin=(ch == NCH - 1))
                    S.op('dve', lambda: nc.vector.tensor_tensor(out=otl[i2][:], in0=po['t'][:], in1=hres[i2][:], op=ALU.add),
                         r=[po['k'], f'hres{i2}'], w=[f'otl{i2}'])
                    S.dma('sp', hT_o[:, dt, cols], otl[i2][:], r=[f'otl{i2}'])
    S.wait_all('sp')
    b.stack.close()
    return nc


def build_peer(TP=2048):
    b = B()
    nc, S = b.nc, b.S
    TS = 512
    NTT = TS // 128
    hT_d = b.din("hT", [128, NCH, TP])
    g2_d = b.din("g2", [128, NCH])
    wq_d = b.din("wq", [D, D])
    kT_d = b.din("keysT", [128, 16, 128])
    uT_d = b.din("uT", [D, PEER_N])
    v_d = b.din("v", [PEER_N, D])
    cst_d = b.din("cst", [128, 3, 128])
    hT_o = b.dout("hTo", [128, NCH, TP])

    c = emit_consts(b, cst_d)
    P = [{'t': b.ps(f"ps{i}"), 'k': f"ps{i}"} for i in range(8)]
    g2 = b.sb("g2", [128, NCH])
    keysTb = b.sb("keysTb", [128, 16, 128], BF16)
    S.dma('sp', g2[:], g2_d, w=['g2'])
    S.dma('pool', keysTb[:], kT_d, w=['keysTb'])
    acc = b.sb("acc", [128, NCH, TS])
    hnT = b.sb("hnT", [128, NCH, TS], BF16)
    a1 = b.sb("a1", [128, NTT, 8, 128])
    a2n = b.sb("a2n", [128, NTT, 8, 128])
    Dc = b.sb("Dc", [128, NTT, 8, 128], BF16)
    uring = WRing(b, "ur", 2)

    for half in range(TP // TS):
        hc = slice(half * TS, (half + 1) * TS)
        HK = f"_{half}"
        for ch in range(NCH):
            S.dma('sp', acc[:, ch, :], hT_d[:, ch, hc], w=['acc'], append=(ch > 0))
        emit_rmsnorm(b, c, acc, 'acc', g2, 'g2', hnT, 'hnT', P[6:8], TS)
        with ExitStack() as st:
            qTp = b.sb("qTp" + HK, [128, 16, TS], BF16, st)
            s_sb = b.sb("s_sb" + HK, [128, 16, 128], F32, st)
            swork = b.sb("swork" + HK, [128, 128], F32, st)
            tv = b.sb("tv" + HK, [128, 16, 16], F32, st)
            cand = b.sb("cand" + HK, [128, 8, 256], F32, st)
            cwork = b.sb("cwork" + HK, [128, 256], F32, st)
            cv = b.sb("cv" + HK, [128, 8, 16], F32, st)
            cvs = b.sb("cvs" + HK, [128, 8, 16], F32, st)
            sm = b.sb("sm" + HK, [128, 16, 128], F32, st)
            e2 = b.sb("e2" + HK, [128, 8, 128], F32, st)
            Z = b.sb("Z" + HK, [128, 8], F32, st)
            rZ = b.sb("rZ" + HK, [128, 8], F32, st)
            e1 = b.sb("e1" + HK, [128, 8], F32, st)
            k2 = b.sb("k2" + HK, [128, 8], F32, st)
            cthr = b.sb("cthr" + HK, [128, 8], F32, st)
            for wq4 in range(4):
                wt, wk = uring.load(wq_d[:, wq4 * 512:(wq4 + 1) * 512])
                for jl in range(4):
                    hp = wq4 * 4 + jl
                    ps = P[6 + hp % 2]
                    for ch in range(NCH):
                        S.op('pe', lambda ch=ch: nc.tensor.matmul(ps['t'][:], wt[:, ch, jl * 128:(jl + 1) * 128], hnT[:, ch, :],
                                                                 start=(ch == 0), stop=(ch == NCH - 1)),
                             r=[wk, 'hnT'], w=[ps['k']], fin=(ch == NCH - 1))
                    S.op('act', lambda: nc.scalar.copy(out=qTp[:, hp, :], in_=ps['t'][:]), r=[ps['k']], w=['qTp'])
            for tt in range(NTT):
                tc = slice(tt * 128, (tt + 1) * 128)
                for grp in range(4):
                    ps = P[6 + grp % 2]
                    for jl in range(4):
                        hp = grp * 4 + jl
                        S.op('pe', lambda: nc.tensor.matmul(ps['t'][:, jl * 128:(jl + 1) * 128], qTp[:, hp, tc], keysTb[:, hp, :],
                                                           start=True, stop=True), r=['qTp', 'keysTb'], w=[ps['k']], fin=(jl == 3))
                    S.op('act', lambda: nc.scalar.copy(out=s_sb[:, grp * 4:(grp + 1) * 4, :],
                                                       in_=ps['t'][:].rearrange("p (j n) -> p j n", n=128)), r=[ps['k']], w=['s_sb'])
                for hp in range(16):
                    S.op('dve', lambda: nc.vector.max(out=tv[:, hp, 0:8], in_=s_sb[:, hp, :]), r=['s_sb'], w=['tv'])
                    S.op('dve', lambda: nc.vector.match_replace(out=swork[:], in_to_replace=tv[:, hp, 0:8], in_values=s_sb[:, hp, :],
                                                                imm_value=-1e30), r=['s_sb', 'tv'], w=['swork'])
                    S.op('dve', lambda: nc.vector.max(out=tv[:, hp, 8:16], in_=swork[:]), r=['swork'], w=['tv'])
                tv4 = tv[:].rearrange("p (h two) k -> p h two k", two=2)
                S.op('dve', lambda: nc.vector.tensor_tensor(
                    out=cand[:].rearrange("p h (a b) -> p h a b", b=16),
                    in0=tv4[:, :, 0, :].unsqueeze(3).to_broadcast([128, 8, 16, 16]),
                    in1=tv4[:, :, 1, :].unsqueeze(2).to_broadcast([128, 8, 16, 16]), op=ALU.add), r=['tv'], w=['cand'])
                for h in range(8):
                    S.op('dve', lambda: nc.vector.max(out=cv[:, h, 0:8], in_=cand[:, h, :]), r=['cand'], w=['cv'])
                    S.op('dve', lambda: nc.vector.match_replace(out=cwork[:], in_to_replace=cv[:, h, 0:8], in_values=cand[:, h, :],
                                                                imm_value=-1e30), r=['cand', 'cv'], w=['cwork'])
                    S.op('dve', lambda: nc.vector.max(out=cv[:, h, 8:16], in_=cwork[:]), r=['cwork'], w=['cv'])
                S.op('dve', lambda: nc.vector.tensor_tensor(out=cvs[:], in0=cv[:], in1=cv[:, :, 0:1].to_broadcast([128, 8, 16]),
                                                            op=ALU.subtract), r=['cv'], w=['cvs'])
                S.op('act', lambda: nc.scalar.activation(out=cvs[:], in_=cvs[:], func=AF.Exp), r=['cvs'], w=['cvs'])
                S.op('dve', lambda: nc.vector.reduce_sum(out=Z[:], in_=cvs[:], axis=AX.X), r=['cvs'], w=['Z'])
                S.op('dve', lambda: nc.vector.reciprocal(out=rZ[:], in_=Z[:]), r=['Z'], w=['rZ'])
                S.op('dve', lambda: nc.vector.tensor_scalar(out=e1[:], in0=cvs[:, :, 15], scalar1=1.0 - 1e-4, scalar2=None, op0=ALU.mult),
                     r=['cvs'], w=['e1'])
                S.op('dve', lambda: nc.vector.reciprocal(out=k2[:], in_=e1[:]), r=['e1'], w=['k2'])
                S.op('dve', lambda: nc.vector.tensor_tensor(out=cthr[:], in0=e1[:], in1=rZ[:], op=ALU.mult), r=['e1', 'rZ'], w=['cthr'])
                S.op('dve', lambda: nc.vector.tensor_tensor(out=sm[:], in0=s_sb[:], in1=tv[:, :, 0:1].to_broadcast([128, 16, 128]),
                                                            op=ALU.subtract), r=['s_sb', 'tv'], w=['sm'])
                sm4 = sm[:].rearrange("p (h two) n -> p h two n", two=2)
                S.op('act', lambda: nc.scalar.activation(out=a1[:, tt, :, :], in_=sm4[:, :, 0, :], func=AF.Exp), r=['sm'], w=['a1'])
                S.op('act', lambda: nc.scalar.activation(out=e2[:], in_=sm4[:, :, 1, :], func=AF.Exp), r=['sm'], w=['e2'])
                S.op('dve', lambda: nc.vector.tensor_tensor(out=a2n[:, tt, :, :], in0=e2[:], in1=k2[:].unsqueeze(2).to_broadcast([128, 8, 128]),
                                                            op=ALU.mult), r=['e2', 'k2'], w=['a2n'])
                S.op('dve', lambda: nc.vector.tensor_tensor(out=Dc[:, tt, :, :], in0=c['ident_b'].unsqueeze(1).to_broadcast([128, 8, 128]),
                                                            in1=cthr[:].unsqueeze(2).to_broadcast([128, 8, 128]), op=ALU.mult),
                     r=['cst_bf', 'cthr'], w=['Dc'])
        S.scope_barrier()
        with ExitStack() as st:
            vr = [b.sb(f"vr{i}" + HK, [128, 4, D], BF16, st) for i in range(2)]
            GT = [b.sb(f"GT{i}" + HK, [128, TS], BF16, st) for i in range(8)]
            Pp = [b.sb(f"Pp{i}" + HK, [128, 8, 128], F32, st) for i in range(4)]
            Wp = [b.sb(f"Wp{i}" + HK, [128, 8, 128], BF16, st) for i in range(8)]
            ga = [b.sb(f"ga{i}" + HK, [128, TS], F32, st) for i in range(2)]
            NEC = PEER_N // 512
            NJT = NEC * 4

            def load_chunk(ec):
                ut, uk = uring.load(uT_d[:, ec * 512:(ec + 1) * 512])
                vt, vk = vr[ec % 2], f"vr{ec % 2}"
                vsrc = v_d[ec * 512:(ec + 1) * 512, :].rearrange("(j p) d -> p j d", p=128)
                for j in range(4):
                    S.dma('pool', vt[:, j, :], vsrc[:, j, :], w=[vk], append=(j > 0))
                return (ut, uk, vt, vk)

            chunks = {0: load_chunk(0)}

            def emit_A(J):
                ec, j = divmod(J, 4)
                ut, uk, _, _ = chunks[ec]
                pA = P[J % 2]
                for ch in range(NCH):
                    S.op('pe', lambda ch=ch: nc.tensor.matmul(pA['t'][:], ut[:, ch, j * 128:(j + 1) * 128], hnT[:, ch, :],
                                                             start=(ch == 0), stop=(ch == NCH - 1)),
                         r=[uk, 'hnT'], w=[pA['k']], fin=(ch == NCH - 1))

            def emit_PW(J):
                i1 = J
                for tt in range(NTT):
                    pp, pk = Pp[tt], f'Pp{tt}'
                    ws = (J % 2) * 4 + tt
                    wp, wk_ = Wp[ws], f'Wp{ws}'
                    if tt < 2:
                        for h in range(8):
                            S.op('act', lambda h=h: nc.scalar.activation(out=pp[:, h, :], in_=a2n[:, tt, h, :], func=AF.Identity,
                                                                        scale=a1[:, tt, h, i1:i1 + 1]),
                                 r=['a1', 'a2n'], w=[pk], fin=(h == 7))
                    else:
                        S.op('pool', lambda: nc.gpsimd.tensor_tensor(out=pp[:], in0=a2n[:, tt, :, :],
                                                                    in1=a1[:, tt, :, i1:i1 + 1].to_broadcast([128, 8, 128]), op=ALU.mult),
                             r=['a1', 'a2n'], w=[pk])
                    S.op('dve', lambda: nc.vector.scalar_tensor_tensor(out=wp[:], in0=pp[:], scalar=1.0, in1=pp[:],
                                                                      op0=ALU.is_ge, op1=ALU.mult), r=[pk], w=[wk_])

            def emit_T(J):
                ec, j = divmod(J, 4)
                gi = (ec % 2) * 4 + j
                pA, pW = P[J % 2], P[2 + J % 2]
                for tt in range(NTT):
                    ws = (J % 2) * 4 + tt
                    wp, wk_ = Wp[ws], f'Wp{ws}'
                    for h in range(8):
                        S.op('pe', lambda h=h: nc.tensor.matmul(pW['t'][:, tt * 128:(tt + 1) * 128], wp[:, h, :], Dc[:, tt, h, :],
                                                               start=(h == 0), stop=(h == 7)),
                             r=[wk_, 'Dc'], w=[pW['k']], fin=(h == 7))
                S.op('act', lambda: nc.scalar.activation(out=ga[J % 2][:], in_=pA['t'][:], func=AF.Gelu_apprx_tanh),
                     r=[pA['k']], w=[f'ga{J % 2}'])
                S.op('dve', lambda: nc.vector.tensor_tensor(out=GT[gi][:], in0=ga[J % 2][:], in1=pW['t'][:], op=ALU.mult),
                     r=[f'ga{J % 2}', pW['k']], w=[f'GT{gi}'])

            def emit_out(ec):
                _, _, vt, vk = chunks[ec]
                for dt in range(NCH):
                    pO = P[4 + dt % 2]
                    for j in range(4):
                        gi = (ec % 2) * 4 + j
                        S.op('pe', lambda j=j, gi=gi: nc.tensor.matmul(pO['t'][:], vt[:, j, dt * 128:(dt + 1) * 128], GT[gi][:],
                                                                      start=(j == 0), stop=(j == 3)),
                             r=[vk, f'GT{gi}'], w=[pO['k']], fin=(j == 3))
                    S.op('dve', lambda: nc.vector.tensor_tensor(out=acc[:, dt, :], in0=acc[:, dt, :], in1=pO['t'][:], op=ALU.add),
                         r=[pO['k'], 'acc'], w=['acc'])

            for J in range(NJT + 2):
                if J < NJT:
                    ec, j = divmod(J, 4)
                    if j == 2 and ec + 1 < NEC:
                        chunks[ec + 1] = load_chunk(ec + 1)
                    emit_A(J)
                    emit_PW(J)
                if 1 <= J <= NJT:
                    emit_T(J - 1)
                if J >= 5 and (J - 5) % 4 == 0:
                    emit_out((J - 5) // 4)
            for ch in range(NCH):
                S.dma('sp', hT_o[:, ch, hc], acc[:, ch, :], r=['acc'])
        S.scope_barrier()
    S.wait_all('sp')
    b.stack.close()
    return nc


def build_fin():
    b = B()
    nc, S = b.nc, b.S
    hT_d = b.din("hT", [128, NCH, T])
    g_d = b.din("gF", [128, NCH])
    cst_d = b.din("cst", [128, 3, 128])
    o_d = b.dout("oT", [128, NCH, T])
    c = emit_consts(b, cst_d)
    P = [{'t': b.ps(f"ps{i}"), 'k': f"ps{i}"} for i in range(2)]
    g = b.sb("gF", [128, NCH])
    hT = b.sb("hT", [128, NCH, T])
    oT = b.sb("oT", [128, NCH, T])
    S.dma('sp', g[:], g_d, w=['g'])
    for ch in range(NCH):
        S.dma('sp', hT[:, ch, :], hT_d[:, ch, :], w=[f'hT{ch}'])
    S.op('act', lambda: nc.scalar.copy(out=c['eps'][:, 0:1], in_=c['eps'][:, 0:1]),
         r=[f'hT{ch}' for ch in range(NCH)] + ['cst_eps'], w=['hTj'])
    emit_rmsnorm(b, c, hT, 'hTj', g, 'g', oT, 'oT', P, T)
    for ch in range(NCH):
        S.dma('sp', o_d[:, ch, :], oT[:, ch, :], r=['oT'])
    S.wait_all('sp')
    b.stack.close()
    return nc


_CACHE = {}


def _consts():
    cst = np.zeros((128, 3, 128), np.float32)
    cst[:, 0, :] = np.eye(128, dtype=np.float32)
    cst[:, 1, :] = np.triu(np.ones((128, 128), np.float32))
    cst[:, 2, :] = 1.0
    return cst


def to_fm(h):
    return np.ascontiguousarray(h.reshape(h.shape[0], NCH, 128).transpose(2, 1, 0))


def from_fm(hT):
    return np.ascontiguousarray(hT.transpose(2, 1, 0).reshape(hT.shape[2], D))


def run_kv(hT_list, l, inp):
    if 'kv' not in _CACHE:
        _CACHE['kv'] = build_kv()
    nc = _CACHE['kv']
    g1 = np.ascontiguousarray(inp['norm1_g'][l].reshape(NCH, 128).T)
    fb = np.ascontiguousarray(np.broadcast_to(inp['forget_b'][l][None, :], (128, 4))).astype(np.float32)
    W = inp['w_in'][l]
    w_kv = np.ascontiguousarray(np.concatenate(
        [W[:, OFF_D + 512:OFF_D + 1536], W[:, OFF_A + 512:OFF_A + 1536], W[:, OFF_C:OFF_C + 512],
         W[:, OFF_D + 1536:OFF_D + 1540]], axis=1))
    cst = _consts()
    maps = [{"hT": hT_list[c], "g1": g1, "w_kv": w_kv, "fb": fb, "cst": cst} for c in range(NCORES)]
    res = run_bass_kernel_spmd(nc, maps, core_ids=list(range(NCORES)))
    return res.results


def run_mix(hT_list, l, inp, kvres):
    if 'mix' not in _CACHE:
        _CACHE['mix'] = build_mix()
    nc = _CACHE['mix']
    NB = SEQ // 128
    rep = lambda a: np.ascontiguousarray(np.broadcast_to(a, (128,) + a.shape)).astype(np.float32)
    g1 = np.ascontiguousarray(inp['norm1_g'][l].reshape(NCH, 128).T)
    cw = np.ascontiguousarray(inp['conv_w'][l].reshape(3, 4, 128).transpose(2, 1, 0))
    sgg = rep(inp['sgu_norm_g'][l])
    swT = np.ascontiguousarray(inp['sgu_w'][l].transpose(2, 0, 1))
    sgb = rep(inp['sgu_b'][l])
    pw = np.ascontiguousarray(inp['pool_w'][l].transpose(1, 0, 2))
    psc = np.ascontiguousarray(inp['pool_scale'][l].reshape(4, 128).T)
    w_in = np.ascontiguousarray(inp['w_in'][l])
    wbr = np.ascontiguousarray(inp['w_branch'][l])
    wout = np.ascontiguousarray(inp['w_out'][l])
    KT = np.ascontiguousarray(np.concatenate([np.asarray(kvres[c]['kT']) for c in range(NCORES)], axis=2))
    V = np.ascontiguousarray(np.concatenate([np.asarray(kvres[c]['v']) for c in range(NCORES)], axis=2))
    fla = np.ascontiguousarray(np.concatenate([np.asarray(kvres[c]['flog']) for c in range(NCORES)], axis=1))
    kpos = np.ascontiguousarray((np.arange(NB)[None, :] * 128 + np.arange(128)[:, None]).astype(np.float32))
    cst = _consts()
    maps = []
    for c in range(NCORES):
        bef = rep((np.arange(NB) < c * (T // 128)).astype(np.float32))
        qpos = rep((c * T + np.arange(T)).astype(np.float32))
        halo = np.asarray(kvres[c - 1]['halo']) if c > 0 else np.zeros((128, 4, 17), np.float32)
        maps.append({"hT": hT_list[c], "g1": g1, "w_in": w_in, "cw": cw, "sgg": sgg, "swT": swT, "sgb": sgb, "pw": pw,
                     "psc": psc, "wbr": wbr, "wout": wout, "KT": KT, "V": V, "fla": fla,
                     "flo": np.ascontiguousarray(np.asarray(kvres[c]['flog'])), "bef": bef, "halo": np.ascontiguousarray(halo),
                     "qpos": qpos, "kpos": kpos, "cst": cst})
    res = run_bass_kernel_spmd(nc, maps, core_ids=list(range(NCORES)))
    _CACHE["mix_dbg"] = res.results
    return [np.asarray(r["hTo"]) for r in res.results]


def run_peer(hT_list, l, inp, uT=None):
    NCP = 8
    per = NCORES // NCP
    if 'peer' not in _CACHE:
        _CACHE['peer'] = build_peer(T * per)
    nc = _CACHE['peer']
    g2 = np.ascontiguousarray(inp['norm2_g'][l].reshape(NCH, 128).T)
    wq = np.ascontiguousarray(inp['peer_wq'][l])
    keysT = np.ascontiguousarray(inp['peer_keys'][l].reshape(16, 128, 128).transpose(2, 0, 1))
    if uT is None:
        uT = np.ascontiguousarray(inp['peer_u'][l].T)
    v = np.ascontiguousarray(inp['peer_v'][l])
    cst = _consts()
    maps = [{"hT": np.ascontiguousarray(np.concatenate(hT_list[c * per:(c + 1) * per], axis=2)), "g2": g2, "wq": wq,
             "keysT": keysT, "uT": uT, "v": v, "cst": cst} for c in range(NCP)]
    res = run_bass_kernel_spmd(nc, maps, core_ids=list(range(NCP)))
    out = []
    for r in res.results:
        o = np.asarray(r["hTo"])
        for i in range(per):
            out.append(np.ascontiguousarray(o[:, :, i * T:(i + 1) * T]))
    return out


def run_fin(hT_list, inp):
    if 'fin' not in _CACHE:
        _CACHE['fin'] = build_fin()
    nc = _CACHE['fin']
    gF = np.ascontiguousarray(inp['final_g'].reshape(NCH, 128).T)
    cst = _consts()
    maps = [{"hT": hT_list[c], "gF": gF, "cst": cst} for c in range(NCORES)]
    res = run_bass_kernel_spmd(nc, maps, core_ids=list(range(NCORES)))
    return [np.asarray(r["oT"]) for r in res.results]


def kernel(**inputs):
    inp = {k: np.asarray(v) for k, v in inputs.items()}
    x = inp['x'][0]
    hT = [to_fm(x[c * T:(c + 1) * T]) for c in range(NCORES)]
    for l in range(DEPTH):
        kv = run_kv(hT, l, inp)
        hT = run_mix(hT, l, inp, kv)
        hT = run_peer(hT, l, inp)
    oT = run_fin(hT, inp)
    out = np.concatenate([from_fm(a) for a in oT], axis=0)
    return out[None].astype(np.float32)
```

```python
import numpy as np
from contextlib import ExitStack
import concourse.bass as bass
import concourse.mybir as mybir
from concourse.bass_utils import run_bass_kernel_spmd

F32 = mybir.dt.float32
BF16 = mybir.dt.bfloat16
AF = mybir.ActivationFunctionType
ALU = mybir.AluOpType
AX = mybir.AxisListType

NCORES = 8
D = 2048
SEQ = 8192
T = SEQ // NCORES
NCH = D // 128
DEPTH = 4
BW = 512
OFF_A = 0
OFF_B = 1536
OFF_C = 2560
OFF_D = 3072
OFF_G = OFF_D + 1536 + 4
IN_COLS = OFF_G + 4 * D
EPS = 1e-6
NEG = -1.0e5
PEER_N = 16384


class Sched:
    def __init__(self, nc, n_dma_sems=24):
        self.nc = nc
        self.E = {'pe': nc.tensor, 'act': nc.scalar, 'dve': nc.vector,
                  'pool': nc.gpsimd, 'sp': nc.sync}
        self.sem = {e: nc.alloc_semaphore(name=f"sem_{e}") for e in ('pe', 'act', 'dve', 'pool')}
        self.cnt = {e: 0 for e in self.sem}
        self.dsem = [nc.alloc_semaphore(name=f"dsem{i}") for i in range(n_dma_sems)]
        self.dval = [0] * n_dma_sems
        self.dnext = 0
        self.known = {e: {} for e in self.E}
        self.lastw = {}
        self.readers = {}
        self.pend = {e: (set(), set()) for e in self.E}
        self.barrier = []
        self.seen = set()

    def _handle(self, key):
        return self.sem[key[1]] if key[0] == 'c' else self.dsem[key[1]]

    def _wait(self, e, toks):
        best = {}
        for (k, v) in toks:
            if v > best.get(k, 0):
                best[k] = v
        for k, v in best.items():
            if self.known[e].get(k, 0) >= v:
                continue
            if k == ('c', 'pe') and e == 'pe':
                continue
            self.E[e].wait_ge(self._handle(k), v)
            self.known[e][k] = v

    def scope_barrier(self):
        self.barrier = [(('c', k), v) for k, v in self.cnt.items() if v > 0]
        self.barrier += [(('d', i), v) for i, v in enumerate(self.dval) if v > 0]
        self.seen = set()

    def _deps(self, e, r, w):
        toks = []
        for k in w:
            if k not in self.seen:
                self.seen.add(k)
                toks.extend(self.barrier)
        for k in r:
            toks.extend(self.lastw.get(k, ()))
        for k in w:
            toks.extend(self.lastw.get(k, ()))
            for sk, sv in self.readers.get(k, {}).items():
                if sk == ('c', e):
                    continue
                toks.append((sk, sv))
        return toks

    def _register(self, tok, r, w, append=False):
        for k in w:
            if append:
                self.lastw.setdefault(k, []).append(tok)
            else:
                self.lastw[k] = [tok]
            self.readers[k] = {}
        for k in r:
            if k in w:
                continue
            d = self.readers.setdefault(k, {})
            if d.get(tok[0], 0) < tok[1]:
                d[tok[0]] = tok[1]

    def op(self, e, fn, r=(), w=(), fin=True):
        self._wait(e, self._deps(e, r, w))
        ins = fn()
        pr, pw = self.pend[e]
        pr.update(r)
        pw.update(w)
        if fin:
            self.cnt[e] += 1
            ins.then_inc(self.sem[e], 1)
            tok = (('c', e), self.cnt[e])
            self._register(tok, pr, pw)
            self.pend[e] = (set(), set())
        return ins

    def dma(self, q, out, in_, r=(), w=(), append=False):
        toks = self._deps(q, r, w) if not append else []
        i = self.dnext
        self.dnext = (self.dnext + 1) % len(self.dsem)
        if self.dval[i] > 0:
            toks.append((('d', i), self.dval[i]))
        self._wait(q, toks)
        ins = self.E[q].dma_start(out=out, in_=in_)
        self.dval[i] += 16
        ins.then_inc(self.dsem[i], 16)
        tok = (('d', i), self.dval[i])
        self._register(tok, set(r), set(w), append)
        return tok

    def wait_all(self, e):
        toks = [(('c', k), v) for k, v in self.cnt.items() if v > 0]
        toks += [(('d', i), v) for i, v in enumerate(self.dval) if v > 0]
        self._wait(e, toks)


class B:
    def __init__(self):
        self.nc = bass.Bass("TRN2", target_bir_lowering=False)
        self.S = Sched(self.nc)
        self.stack = ExitStack()
        self.uid = 0

    def din(self, name, shape, dt=F32):
        return self.nc.dram_tensor(name, list(shape), dt, kind="ExternalInput").ap()

    def dout(self, name, shape, dt=F32):
        return self.nc.dram_tensor(name, list(shape), dt, kind="ExternalOutput").ap()

    def sb(self, name, shape, dt=F32, stack=None):
        st = stack if stack is not None else self.stack
        return st.enter_context(self.nc.sbuf_tensor("s_" + name, list(shape), dt))

    def ps(self, name, stack=None):
        st = stack if stack is not None else self.stack
        return st.enter_context(self.nc.psum_tensor("p_" + name, [128, 512], F32))


def emit_consts(b, cst_d):
    S = b.S
    c = {}
    c['f32'] = b.sb("cst_f32", [128, 3, 128], F32)
    c['bf'] = b.sb("cst_bf", [128, 3, 128], BF16)
    S.dma('sp', c['f32'][:], cst_d, w=['cst_f32'])
    S.dma('pool', c['bf'][:], cst_d, w=['cst_bf'])
    c['eps'] = b.sb("cst_eps", [128, 1], F32)
    S.op('dve', lambda: b.nc.vector.memset(c['eps'][:], EPS), w=['cst_eps'])
    c['ident_f'] = c['f32'][:, 0, :]
    c['U_f'] = c['f32'][:, 1, :]
    c['ones_f'] = c['f32'][:, 2, :]
    c['ident_b'] = c['bf'][:, 0, :]
    c['U_b'] = c['bf'][:, 1, :]
    c['ones_b'] = c['bf'][:, 2, :]
    return c


def emit_rmsnorm(b, c, hT, hkey, g_sb, gkey, xnT, xkey, ps_list, ntok):
    S, nc = b.S, b.nc
    with ExitStack() as st:
        sq = [b.sb(f"rn_sq{i}_{b.uid}", [128, 512], F32, st) for i in range(2)]
        rstd = b.sb(f"rn_rstd_{b.uid}", [128, 512], F32, st)
        b.uid += 1
        for hf in range(ntok // 512):
            cols = slice(hf * 512, (hf + 1) * 512)
            ps = ps_list[hf % len(ps_list)]
            pk = ps['k']
            for ch in range(NCH):
                s = sq[ch % 2]
                sk = f"rn_sq{ch % 2}"
                S.op('act', lambda s=s, ch=ch: nc.scalar.activation(out=s[:], in_=hT[:, ch, cols], func=AF.Square),
                     r=[hkey], w=[sk])
                S.op('pe', lambda s=s, ch=ch: nc.tensor.matmul(ps['t'][:], c['ones_f'], s[:], start=(ch == 0), stop=(ch == NCH - 1)),
                     r=[sk, 'cst_f32'], w=[pk], fin=True)
            S.op('act', lambda: nc.scalar.activation(out=rstd[:], in_=ps['t'][:], func=AF.Sqrt, scale=1.0 / D, bias=c['eps'][:, 0:1]),
                 r=[pk, 'cst_eps'], w=['rn_rstd'])
            S.op('dve', lambda: nc.vector.reciprocal(out=rstd[:], in_=rstd[:]), r=['rn_rstd'], w=['rn_rstd'])
            for ch in range(NCH):
                S.op('dve', lambda ch=ch: nc.vector.scalar_tensor_tensor(
                    out=xnT[:, ch, cols], in0=hT[:, ch, cols], scalar=g_sb[:, ch:ch + 1], in1=rstd[:],
                    op0=ALU.mult, op1=ALU.mult), r=[hkey, gkey, 'rn_rstd'], w=[xkey])
    S.scope_barrier()


class WRing:
    def __init__(self, b, name, n, stack=None, width=512):
        self.b = b
        self.n = n
        self.name = name
        self.width = width
        self.t = [b.sb(f"{name}{i}", [128, NCH, width], BF16, stack) for i in range(n)]
        self.i = 0

    def load(self, src_rows_cols, ncols=None):
        b = self.b
        i = self.i
        self.i = (self.i + 1) % self.n
        t = self.t[i]
        key = f"{self.name}{i}"
        nco = src_rows_cols.shape[1]
        src = src_rows_cols.rearrange("(c p) n -> p c n", p=128)
        for q in range(4):
            b.S.dma('pool', t[:, 4 * q:4 * q + 4, 0:nco], src[:, 4 * q:4 * q + 4, :], w=[key], append=(q > 0))
        return t, key


def build_kv():
    b = B()
    nc, S = b.nc, b.S
    hT_d = b.din("hT", [128, NCH, T])
    g1_d = b.din("g1", [128, NCH])
    w_in = b.din("w_kv", [D, 2564])
    fb_d = b.din("fb", [128, 4])
    cst_d = b.din("cst", [128, 3, 128])
    kT_o = b.dout("kT", [4, 128, T], BF16)
    v_o = b.dout("v", [4, 128, T // 128, 128], BF16)
    fl_o = b.dout("flog", [128, T // 128, 4])
    halo_o = b.dout("halo", [128, 4, 17])

    c = emit_consts(b, cst_d)
    hT = b.sb("hT", [128, NCH, T])
    xnT = b.sb("xnT", [128, NCH, T], BF16)
    g1 = b.sb("g1", [128, NCH])
    fb = b.sb("fb", [128, 4])
    S.dma('sp', g1[:], g1_d, w=['g1'])
    S.dma('sp', fb[:], fb_d, w=['fb'])
    for ch in range(NCH):
        S.dma('sp', hT[:, ch, :], hT_d[:, ch, :], w=['hT'])
    P = [{'t': b.ps(f"ps{i}"), 'k': f"ps{i}"} for i in range(6)]
    ring = WRing(b, "wr", 3)
    ksb = b.sb("ksb", [128, 4, T], BF16)
    vsb = b.sb("vsb", [128, T // 128, 512], BF16)
    wf = b.sb("wf", [128, NCH, 4], BF16)
    fl = b.sb("fl", [128, T // 128, 4])
    halo = b.sb("halo", [128, 4, 17])
    ctmp = b.sb("ctmp", [128, 4, 16])
    emit_rmsnorm(b, c, hT, 'hT', g1, 'g1', xnT, 'xnT', P[0:2], T)
    wt, wk = ring.load(w_in[:, 0:512])
    pi = 0
    for h in range(4):
        for hf in range(T // 512):
            ps = P[pi % 4]
            pi += 1
            cols = slice(hf * 512, (hf + 1) * 512)
            for ch in range(NCH):
                S.op('pe', lambda ch=ch, ps=ps, cols=cols, h=h: nc.tensor.matmul(
                    ps['t'][:], wt[:, ch, h * 128:(h + 1) * 128], xnT[:, ch, cols], start=(ch == 0), stop=(ch == NCH - 1)),
                    r=[wk, 'xnT'], w=[ps['k']], fin=(ch == NCH - 1))
            S.op('act', lambda ps=ps, cols=cols, h=h: nc.scalar.copy(out=ksb[:, h, cols], in_=ps['t'][:]),
                 r=[ps['k']], w=['ksb'])
    for h in range(4):
        S.dma('sp', kT_o[h], ksb[:, h, :], r=['ksb'])
    wt, wk = ring.load(w_in[:, 512:1024])
    for tt in range(T // 128):
        ps = P[pi % 4]
        pi += 1
        tc = slice(tt * 128, (tt + 1) * 128)
        for ch in range(NCH):
            S.op('pe', lambda ch=ch, ps=ps, tc=tc: nc.tensor.matmul(
                ps['t'][:], xnT[:, ch, tc], wt[:, ch, :], start=(ch == 0), stop=(ch == NCH - 1)),
                r=[wk, 'xnT'], w=[ps['k']], fin=(ch == NCH - 1))
        S.op('act', lambda ps=ps, tt=tt: nc.scalar.copy(out=vsb[:, tt, :], in_=ps['t'][:]), r=[ps['k']], w=['vsb'])
    for h in range(4):
        S.dma('sp', v_o[h], vsb[:, :, h * 128:(h + 1) * 128], r=['vsb'])
    S.dma('pool', wf[:], w_in[:, 2560:2564].rearrange("(c p) n -> p c n", p=128), w=['wf'])
    psf = P[4]
    for tt in range(T // 128):
        tc = slice(tt * 128, (tt + 1) * 128)
        for ch in range(NCH):
            S.op('pe', lambda ch=ch, tc=tc, tt=tt: nc.tensor.matmul(
                psf['t'][:, tt * 4:(tt + 1) * 4], xnT[:, ch, tc], wf[:, ch, :], start=(ch == 0), stop=(ch == NCH - 1)),
                r=['wf', 'xnT'], w=[psf['k']], fin=(ch == NCH - 1 and tt == T // 128 - 1))
    S.op('dve', lambda: nc.vector.tensor_tensor(
        out=fl[:], in0=psf['t'][:, 0:(T // 128) * 4].rearrange("p (j h) -> p j h", h=4),
        in1=fb[:].unsqueeze(1).to_broadcast([128, T // 128, 4]), op=ALU.add), r=[psf['k'], 'fb'], w=['fl'])
    S.op('act', lambda: nc.scalar.activation(out=fl[:], in_=fl[:], func=AF.Exp, scale=-1.0), r=['fl'], w=['fl'])
    S.op('act', lambda: nc.scalar.activation(out=fl[:], in_=fl[:], func=AF.Ln, bias=1.0), r=['fl'], w=['fl'])
    S.op('dve', lambda: nc.vector.tensor_scalar(out=fl[:], in0=fl[:], scalar1=-1.0, scalar2=None, op0=ALU.mult),
         r=['fl'], w=['fl'])
    S.dma('sp', fl_o, fl[:], r=['fl'])
    tl = slice(T - 16, T)
    psh = P[5]
    col_sets = [1024, 1536, 2048]
    for si, c0 in enumerate(col_sets):
        wt, wk = ring.load(w_in[:, c0:c0 + 512])
        for dt in range(4):
            o = (si * 4 + dt) * 16
            for ch in range(NCH):
                S.op('pe', lambda ch=ch, dt=dt, o=o, wt=wt: nc.tensor.matmul(
                    psh['t'][:, o:o + 16], wt[:, ch, dt * 128:(dt + 1) * 128], xnT[:, ch, tl],
                    start=(ch == 0), stop=(ch == NCH - 1)),
                    r=[wk, 'xnT'], w=[psh['k']], fin=(ch == NCH - 1 and dt == 3))
    S.op('act', lambda: nc.scalar.copy(out=ctmp[:], in_=psh['t'][:, 0:64].rearrange("p (d t) -> p d t", t=16)),
         r=[psh['k']], w=['ctmp'])
    S.op('dve', lambda: nc.vector.tensor_tensor(
        out=halo[:, :, 0:2], in0=ctmp[:, :, 14:16],
        in1=psh['t'][:, 64:128].rearrange("p (d t) -> p d t", t=16)[:, :, 14:16], op=ALU.mult),
        r=['ctmp', psh['k']], w=['halo'])
    S.op('dve', lambda: nc.vector.tensor_copy(
        out=halo[:, :, 2:17], in_=psh['t'][:, 128:192].rearrange("p (d t) -> p d t", t=16)[:, :, 1:16]),
        r=[psh['k']], w=['halo'])
    S.dma('sp', halo_o, halo[:], r=['halo'])
    S.wait_all('sp')
    b.stack.close()
    return nc


def build_mix():
    b = B()
    nc, S = b.nc, b.S
    NB = SEQ // 128
    NJ = T // 128
    NH = T // 512
    hT_d = b.din("hT", [128, NCH, T])
    g1_d = b.din("g1", [128, NCH])
    w_in = b.din("w_in", [D, IN_COLS])
    cw_d = b.din("cw", [128, 4, 3])
    sgg_d = b.din("sgg", [128, 512])
    swT_d = b.din("swT", [128, 4, 128])
    sgb_d = b.din("sgb", [128, 4, 128])
    pw_d = b.din("pw", [128, 4, 128])
    psc_d = b.din("psc", [128, 4])
    wbr_d = b.din("wbr", [4, 512, D])
    wout_d = b.din("wout", [D, D])
    KT_d = b.din("KT", [4, 128, SEQ], BF16)
    V_d = b.din("V", [4, 128, NB, 128], BF16)
    fla_d = b.din("fla", [128, NB, 4])
    flo_d = b.din("flo", [128, NJ, 4])
    bef_d = b.din("bef", [128, NB])
    halo_d = b.din("halo", [128, 4, 17])
    qpos_d = b.din("qpos", [128, T])
    kpos_d = b.din("kpos", [128, NB])
    cst_d = b.din("cst", [128, 3, 128])
    hT_o = b.dout("hTo", [128, NCH, T])
    dbg_br = b.dout("dbg_br", [128, 4, 4, T], BF16)
    dbg_m = b.dout("dbg_m", [128, NCH, T], BF16)

    c = emit_consts(b, cst_d)
    P = [{'t': b.ps(f"ps{i}"), 'k': f"ps{i}"} for i in range(8)]
    xnT = b.sb("xnT", [128, NCH, T], BF16)
    br = b.sb("br", [128, 4, 4, T], BF16)
    g1 = b.sb("g1", [128, NCH])
    cw = b.sb("cw", [128, 4, 3])
    sgg = b.sb("sgg", [128, 512])
    swT = b.sb("swT", [128, 4, 128])
    swTm = b.sb("swTm", [128, 4, 128], BF16)
    sgb = b.sb("sgb", [128, 4, 128])
    pwb = b.sb("pwb", [128, 4, 128], BF16)
    psc = b.sb("psc", [128, 4])
    halo = b.sb("halo", [128, 4, 17])
    qpos = b.sb("qpos", [128, T])
    kpos = b.sb("kpos", [128, NB])
    for (t_, d_, k_) in [(g1, g1_d, 'g1'), (cw, cw_d, 'cw'), (sgg, sgg_d, 'sgg'), (swT, swT_d, 'swT'), (sgb, sgb_d, 'sgb'),
                         (psc, psc_d, 'psc'), (halo, halo_d, 'halo'), (qpos, qpos_d, 'qpos'), (kpos, kpos_d, 'kpos')]:
        S.dma('sp', t_[:], d_, w=[k_])
    S.dma('pool', pwb[:], pw_d, w=['pwb'])
    S.op('dve', lambda: nc.vector.tensor_tensor(out=swTm[:], in0=swT[:], in1=c['U_f'].unsqueeze(1).to_broadcast([128, 4, 128]),
                                                op=ALU.mult), r=['swT', 'cst_f32'], w=['swTm'])

    with ExitStack() as st0:
        hT = b.sb("hT", [128, NCH, T], F32, st0)
        for ch in range(NCH):
            S.dma('sp', hT[:, ch, :], hT_d[:, ch, :], w=[f'hT{ch}'])
        S.op('act', lambda: nc.scalar.copy(out=c['eps'][:, 0:1], in_=c['eps'][:, 0:1]),
             r=[f'hT{ch}' for ch in range(NCH)] + ['cst_eps'], w=['hTj'])
        emit_rmsnorm(b, c, hT, 'hTj', g1, 'g1', xnT, 'xnT', P[0:2], T)
    S.scope_barrier()

    stA = ExitStack()
    negFk = b.sb("negFk", [128, 4, NB], F32, stA)
    FqB = b.sb("FqB", [128, 4, T], F32, stA)
    qT = b.sb("qT", [128, 4, T], BF16, stA)
    with ExitStack() as stf:
        FL = b.sb("FL", [128, NB * 4], F32, stf)
        FLo = b.sb("FLo", [128, NJ * 4], F32, stf)
        bef = b.sb("bef", [128, NB], F32, stf)
        TotT = b.sb("TotT", [128, 4, NB], F32, stf)
        FlocT = b.sb("FlocT", [128, 4, NB], F32, stf)
        incl = b.sb("incl", [128, 4, NB], F32, stf)
        ones64 = b.sb("ones64", [128, NB], F32, stf)
        pref = b.sb("pref", [128, 4], F32, stf)
        tmpF = b.sb("tmpF", [128, 4, NB], F32, stf)
        TotoT = b.sb("TotoT", [128, 4, NJ], F32, stf)
        FlocoT = b.sb("FlocoT", [128, 4, NJ], F32, stf)
        inclo = b.sb("inclo", [128, 4, NJ], F32, stf)
        Fown = b.sb("Fown", [128, 4, NJ], F32, stf)
        dg = [b.sb(f"dg{i}", [128, 128], F32, stf) for i in range(2)]
        S.dma('sp', FL[:], fla_d.rearrange("p b h -> p (b h)"), w=['FL'])
        S.dma('sp', FLo[:], flo_d.rearrange("p b h -> p (b h)"), w=['FLo'])
        S.dma('sp', bef[:], bef_d, w=['bef'])
        S.op('dve', lambda: nc.vector.memset(ones64[:], 1.0), w=['ones64'])
        S.op('pe', lambda: nc.tensor.matmul(P[2]['t'][:, 0:NB * 4], c['U_f'], FL[:], start=True, stop=True),
             r=['FL', 'cst_f32'], w=['ps2'])
        S.op('pe', lambda: nc.tensor.matmul(P[3]['t'][:, 0:NB * 4], c['ones_f'], FL[:], start=True, stop=True),
             r=['FL', 'cst_f32'], w=['ps3'])
        S.op('act', lambda: nc.scalar.copy(out=FlocT[:], in_=P[2]['t'][:, 0:NB * 4].rearrange("p (b h) -> p h b", h=4)),
             r=['ps2'], w=['FlocT'])
        S.op('act', lambda: nc.scalar.copy(out=TotT[:], in_=P[3]['t'][:, 0:NB * 4].rearrange("p (b h) -> p h b", h=4)),
             r=['ps3'], w=['TotT'])
        for h in range(4):
            S.op('dve', lambda h=h: nc.vector.tensor_tensor_scan(out=incl[:, h, :], data0=ones64[:], data1=TotT[:, h, :],
                                                                initial=0.0, op0=ALU.mult, op1=ALU.add),
                 r=['ones64', 'TotT'], w=['incl'])
        S.op('dve', lambda: nc.vector.tensor_tensor(out=tmpF[:], in0=incl[:], in1=TotT[:], op=ALU.subtract),
             r=['incl', 'TotT'], w=['tmpF'])
        S.op('dve', lambda: nc.vector.tensor_tensor(out=tmpF[:], in0=tmpF[:], in1=FlocT[:], op=ALU.add),
             r=['tmpF', 'FlocT'], w=['tmpF'])
        S.op('dve', lambda: nc.vector.tensor_scalar(out=negFk[:], in0=tmpF[:], scalar1=-1.0, scalar2=None, op0=ALU.mult),
             r=['tmpF'], w=['negFk'])
        S.op('dve', lambda: nc.vector.tensor_tensor(out=tmpF[:], in0=TotT[:], in1=bef[:].unsqueeze(1).to_broadcast([128, 4, NB]),
                                                    op=ALU.mult), r=['TotT', 'bef', 'negFk'], w=['tmpF'])
        S.op('dve', lambda: nc.vector.reduce_sum(out=pref[:], in_=tmpF[:], axis=AX.X), r=['tmpF'], w=['pref'])
        S.op('pe', lambda: nc.tensor.matmul(P[2]['t'][:, 0:NJ * 4], c['U_f'], FLo[:], start=True, stop=True),
             r=['FLo', 'cst_f32'], w=['ps2'])
        S.op('pe', lambda: nc.tensor.matmul(P[3]['t'][:, 0:NJ * 4], c['ones_f'], FLo[:], start=True, stop=True),
             r=['FLo', 'cst_f32'], w=['ps3'])
        S.op('act', lambda: nc.scalar.copy(out=FlocoT[:], in_=P[2]['t'][:, 0:NJ * 4].rearrange("p (b h) -> p h b", h=4)),
             r=['ps2'], w=['FlocoT'])
        S.op('act', lambda: nc.scalar.copy(out=TotoT[:], in_=P[3]['t'][:, 0:NJ * 4].rearrange("p (b h) -> p h b", h=4)),
             r=['ps3'], w=['TotoT'])
        for h in range(4):
            S.op('dve', lambda h=h: nc.vector.tensor_tensor_scan(out=inclo[:, h, :], data0=ones64[:, 0:NJ], data1=TotoT[:, h, :],
                                                                initial=0.0, op0=ALU.mult, op1=ALU.add),
                 r=['ones64', 'TotoT'], w=['inclo'])
        S.op('dve', lambda: nc.vector.tensor_tensor(out=Fown[:], in0=inclo[:], in1=TotoT[:], op=ALU.subtract),
             r=['inclo', 'TotoT'], w=['Fown'])
        S.op('dve', lambda: nc.vector.tensor_tensor(out=Fown[:], in0=Fown[:], in1=FlocoT[:], op=ALU.add),
             r=['Fown', 'FlocoT'], w=['Fown'])
        S.op('dve', lambda: nc.vector.tensor_tensor(out=Fown[:], in0=Fown[:], in1=pref[:].unsqueeze(2).to_broadcast([128, 4, NJ]),
                                                    op=ALU.add), r=['Fown', 'pref'], w=['Fown'])
        for h in range(4):
            for hf in range(NH):
                ps = P[4 + (h * NH + hf) % 2]
                for jj in range(4):
                    j = hf * 4 + jj
                    d = dg[(j + h) % 2]
                    dk = f"dg{(j + h) % 2}"
                    S.op('dve', lambda d=d, h=h, j=j: nc.vector.tensor_scalar(out=d[:], in0=c['ident_f'], scalar1=Fown[:, h, j:j + 1],
                                                                            scalar2=None, op0=ALU.mult),
                         r=['Fown', 'cst_f32'], w=[dk])
                    S.op('pe', lambda d=d, jj=jj, ps=ps: nc.tensor.matmul(ps['t'][:, jj * 128:(jj + 1) * 128], c['ones_f'], d[:],
                                                                       start=True, stop=True),
                         r=[dk, 'cst_f32'], w=[ps['k']])
                S.op('act', lambda ps=ps, h=h, hf=hf: nc.scalar.copy(out=FqB[:, h, hf * 512:(hf + 1) * 512], in_=ps['t'][:]),
                     r=[ps['k']], w=['FqB'])
    S.scope_barrier()

    with ExitStack() as st1:
        ring = WRing(b, "wr", 3, st1)
        zz = b.sb("zz", [128, T + 2], F32, st1)
        bsb = b.sb("bsb", [128, T], F32, st1)
        acc = b.sb("acc", [128, T], F32, st1)
        ctmp = b.sb("ctmp", [128, 512], F32, st1)
        ugT = b.sb("ugT", [128, 4, T], BF16, st1)
        gv = b.sb("gv", [128, 512], F32, st1)
        junk = b.sb("junk", [128, 512], F32, st1)
        ss = b.sb("ss", [128, 1], F32, st1)
        vn = b.sb("vn", [128, NJ, 512], BF16, st1)
        pz = b.sb("pz", [128, T + 15], F32, st1)
        pa = b.sb("pa", [128, T + 15], F32, st1)
        pb = b.sb("pb", [128, T + 15], F32, st1)
        idv = b.sb("idv", [128, T], F32, st1)
        pl = b.sb("pl", [128, T], BF16, st1)
        pi = [0]

        def nextps():
            p_ = P[pi[0] % 6]
            pi[0] += 1
            return p_

        def proj_fm(wt, wk, dt, hf, ps):
            cols = slice(hf * 512, (hf + 1) * 512)
            for ch in range(NCH):
                S.op('pe', lambda ch=ch: nc.tensor.matmul(ps['t'][:], wt[:, ch, dt * 128:(dt + 1) * 128], xnT[:, ch, cols],
                                                         start=(ch == 0), stop=(ch == NCH - 1)),
                     r=[wk, 'xnT'], w=[ps['k']], fin=(ch == NCH - 1))

        wt, wk = ring.load(w_in[:, OFF_D:OFF_D + 512])
        for h in range(4):
            for hf in range(NH):
                ps = nextps()
                proj_fm(wt, wk, h, hf, ps)
                S.op('act', lambda ps=ps, h=h, hf=hf: nc.scalar.copy(out=qT[:, h, hf * 512:(hf + 1) * 512], in_=ps['t'][:]),
                     r=[ps['k']], w=['qT'])
        wb_t, wb_k = ring.load(w_in[:, OFF_A:OFF_A + 512])
        wc_t, wc_k = ring.load(w_in[:, OFF_A + 512:OFF_A + 1024])
        wh_t, wh_k = ring.load(w_in[:, OFF_A + 1024:OFF_A + 1536])
        for dt in range(4):
            S.op('dve', lambda dt=dt: nc.vector.tensor_copy(out=zz[:, 0:2], in_=halo[:, dt, 0:2]), r=['halo'], w=['zz'])
            for hf in range(NH):
                cols = slice(hf * 512, (hf + 1) * 512)
                pc, ph, pb_ = nextps(), nextps(), nextps()
                proj_fm(wc_t, wc_k, dt, hf, pc)
                proj_fm(wh_t, wh_k, dt, hf, ph)
                proj_fm(wb_t, wb_k, dt, hf, pb_)
                S.op('act', lambda pc=pc: nc.scalar.copy(out=ctmp[:], in_=pc['t'][:]), r=[pc['k']], w=['ctmp'])
                S.op('dve', lambda ph=ph, hf=hf: nc.vector.tensor_tensor(out=zz[:, 2 + hf * 512:2 + (hf + 1) * 512], in0=ctmp[:],
                                                                        in1=ph['t'][:], op=ALU.mult),
                     r=['ctmp', ph['k']], w=['zz'])
                S.op('act', lambda pb_=pb_, cols=cols: nc.scalar.copy(out=bsb[:, cols], in_=pb_['t'][:]), r=[pb_['k']], w=['bsb'])
            S.op('dve', lambda dt=dt: nc.vector.tensor_scalar(out=acc[:], in0=zz[:, 2:T + 2], scalar1=cw[:, dt, 2:3], scalar2=None,
                                                             op0=ALU.mult), r=['zz', 'cw'], w=['acc'])
            S.op('dve', lambda dt=dt: nc.vector.scalar_tensor_tensor(out=acc[:], in0=zz[:, 1:T + 1], scalar=cw[:, dt, 1:2], in1=acc[:],
                                                                    op0=ALU.mult, op1=ALU.add), r=['zz', 'cw', 'acc'], w=['acc'])
            S.op('dve', lambda dt=dt: nc.vector.scalar_tensor_tensor(out=acc[:], in0=zz[:, 0:T], scalar=cw[:, dt, 0:1], in1=acc[:],
                                                                    op0=ALU.mult, op1=ALU.add), r=['zz', 'cw', 'acc'], w=['acc'])
            S.op('dve', lambda dt=dt: nc.vector.tensor_tensor(out=br[:, 0, dt, :], in0=acc[:], in1=bsb[:], op=ALU.mult),
                 r=['acc', 'bsb'], w=['br0'])
        wu_t, wu_k = ring.load(w_in[:, OFF_B:OFF_B + 512])
        wv_t, wv_k = ring.load(w_in[:, OFF_B + 512:OFF_B + 1024])
        for dt in range(4):
            for hf in range(NH):
                ps = nextps()
                proj_fm(wu_t, wu_k, dt, hf, ps)
                S.op('act', lambda ps=ps, dt=dt, hf=hf: nc.scalar.activation(out=ugT[:, dt, hf * 512:(hf + 1) * 512], in_=ps['t'][:],
                                                                            func=AF.Gelu_apprx_tanh), r=[ps['k']], w=['ugT'])
        for tt in range(NJ):
            ps = nextps()
            tc = slice(tt * 128, (tt + 1) * 128)
            for ch in range(NCH):
                S.op('pe', lambda ch=ch, ps=ps, tc=tc: nc.tensor.matmul(ps['t'][:], xnT[:, ch, tc], wv_t[:, ch, :],
                                                                       start=(ch == 0), stop=(ch == NCH - 1)),
                     r=[wv_k, 'xnT'], w=[ps['k']], fin=(ch == NCH - 1))
            S.op('act', lambda ps=ps: nc.scalar.activation(out=gv[:], in_=ps['t'][:], func=AF.Gelu_apprx_tanh), r=[ps['k']], w=['gv'])
            S.op('act', lambda: nc.scalar.activation(out=junk[:], in_=gv[:], func=AF.Square, accum_out=ss[:, 0:1]),
                 r=['gv'], w=['junk', 'ss'])
            S.op('act', lambda: nc.scalar.activation(out=ss[:], in_=ss[:], func=AF.Sqrt, scale=1.0 / 512, bias=c['eps'][:, 0:1]),
                 r=['ss', 'cst_eps'], w=['ss'])
            S.op('dve', lambda: nc.vector.reciprocal(out=ss[:], in_=ss[:]), r=['ss'], w=['ss'])
            S.op('dve', lambda tt=tt: nc.vector.scalar_tensor_tensor(out=vn[:, tt, :], in0=gv[:], scalar=ss[:, 0:1], in1=sgg[:],
                                                                    op0=ALU.mult, op1=ALU.mult), r=['gv', 'ss', 'sgg'], w=['vn'])
        for g in range(4):
            for hf in range(NH):
                ps = nextps()
                for jj in range(4):
                    S.op('pe', lambda jj=jj, ps=ps, g=g, hf=hf: nc.tensor.matmul(
                        ps['t'][:, jj * 128:(jj + 1) * 128], vn[:, hf * 4 + jj, g * 128:(g + 1) * 128], swTm[:, g, :],
                        start=True, stop=True), r=['vn', 'swTm'], w=[ps['k']], fin=(jj == 3))
                S.op('dve', lambda ps=ps, g=g: nc.vector.tensor_tensor(
                    out=ctmp[:].rearrange("p (j t) -> p j t", t=128), in0=ps['t'][:].rearrange("p (j t) -> p j t", t=128),
                    in1=sgb[:, g, :].unsqueeze(1).to_broadcast([128, 4, 128]), op=ALU.add), r=[ps['k'], 'sgb'], w=['ctmp'])
                S.op('dve', lambda g=g, hf=hf: nc.vector.tensor_tensor(out=br[:, 1, g, hf * 512:(hf + 1) * 512], in0=ctmp[:],
                                                                      in1=ugT[:, g, hf * 512:(hf + 1) * 512], op=ALU.mult),
                     r=['ctmp', 'ugT'], w=['br1'])
        wp_t, wp_k = ring.load(w_in[:, OFF_C:OFF_C + 512])
        L = T + 15
        for g in range(4):
            w_ = 2 ** (g + 1)
            S.op('dve', lambda g=g: nc.vector.tensor_copy(out=pz[:, 0:15], in_=halo[:, g, 2:17]), r=['halo'], w=['pz'])
            for hf in range(NH):
                ps = nextps()
                proj_fm(wp_t, wp_k, g, hf, ps)
                S.op('act', lambda ps=ps, hf=hf: nc.scalar.copy(out=pz[:, 15 + hf * 512:15 + (hf + 1) * 512], in_=ps['t'][:]),
                     r=[ps['k']], w=['pz'])
            src, sk = pz, 'pz'
            bufs = [(pa, 'pa'), (pb, 'pb')]
            for k_ in range(g + 1):
                sh = 2 ** k_
                lo = 2 ** (k_ + 1) - 1
                dst, dk = bufs[k_ % 2]
                S.op('dve', lambda src=src, dst=dst, lo=lo, sh=sh: nc.vector.tensor_tensor(
                    out=dst[:, lo:L], in0=src[:, lo:L], in1=src[:, lo - sh:L - sh], op=ALU.add), r=[sk], w=[dk])
                src, sk = dst, dk
            S.op('dve', lambda w_=w_: nc.vector.tensor_scalar(out=idv[:], in0=qpos[:], scalar1=1.0, scalar2=float(w_),
                                                             op0=ALU.add, op1=ALU.min), r=['qpos'], w=['idv'])
            S.op('dve', lambda: nc.vector.reciprocal(out=idv[:], in_=idv[:]), r=['idv'], w=['idv'])
            S.op('dve', lambda src=src: nc.vector.tensor_tensor(out=idv[:], in0=src[:, 15:L], in1=idv[:], op=ALU.mult),
                 r=[sk, 'idv'], w=['idv'])
            S.op('dve', lambda: nc.vector.tensor_tensor(out=pl[:], in0=idv[:], in1=pz[:, 15:L], op=ALU.subtract),
                 r=['idv', 'pz'], w=['pl'])
            for hf in range(NH):
                ps = nextps()
                S.op('pe', lambda ps=ps, g=g, hf=hf: nc.tensor.matmul(ps['t'][:], pwb[:, g, :], pl[:, hf * 512:(hf + 1) * 512],
                                                                     start=True, stop=True), r=['pwb', 'pl'], w=[ps['k']])
                S.op('dve', lambda ps=ps, g=g, hf=hf: nc.vector.tensor_scalar(out=br[:, 2, g, hf * 512:(hf + 1) * 512], in0=ps['t'][:],
                                                                             scalar1=psc[:, g:g + 1], scalar2=None, op0=ALU.mult),
                     r=[ps['k'], 'psc'], w=['br2'])
    S.scope_barrier()

    scale = 128.0 ** -0.5
    with ExitStack() as st2:
        KTh = b.sb("KTh", [128, SEQ], BF16, st2)
        Vh = b.sb("Vh", [128, NB, 128], BF16, st2)
        mk = [b.sb(f"mk{i}", [128, 512], F32, st2) for i in range(4)]
        tm = [b.sb(f"tm{i}", [128, 512], F32, st2) for i in range(4)]
        PT = [b.sb(f"PT{i}", [128, 512], BF16, st2) for i in range(4)]
        pSb = [P[0], P[1], P[6], P[7]]
        rr = b.sb("rr", [128, 512], F32, st2)
        it = 0
        for h in range(4):
            for q4 in range(4):
                S.dma('sp', KTh[:, q4 * 2048:(q4 + 1) * 2048], KT_d[h][:, q4 * 2048:(q4 + 1) * 2048], w=['KTh'], append=(q4 > 0))
            for q4 in range(4):
                S.dma('sp', Vh[:, q4 * 16:(q4 + 1) * 16, :], V_d[h][:, q4 * 16:(q4 + 1) * 16, :], w=['Vh'], append=(q4 > 0))
            for hf in range(NH):
                cols = slice(hf * 512, (hf + 1) * 512)
                pv = P[2 + (h * NH + hf) % 2]
                rs = P[4 + (h * NH + hf) % 2]
                base = it
                it += NB

                def front(bk):
                    i2 = (base + bk) % 4
                    pS = pSb[i2]
                    S.op('dve', lambda: nc.vector.tensor_scalar(out=mk[i2][:], in0=qpos[:, cols], scalar1=kpos[:, bk:bk + 1],
                                                                scalar2=NEG, op0=ALU.is_lt, op1=ALU.mult),
                         r=['qpos', 'kpos'], w=[f'mk{i2}'])
                    S.op('pe', lambda: nc.tensor.matmul(pS['t'][:], KTh[:, bk * 128:(bk + 1) * 128], qT[:, h, cols],
                                                        start=True, stop=True), r=['KTh', 'qT'], w=[pS['k']])
                    S.op('dve', lambda: nc.vector.scalar_tensor_tensor(out=tm[i2][:], in0=pS['t'][:], scalar=scale,
                                                                      in1=FqB[:, h, cols], op0=ALU.mult, op1=ALU.add),
                         r=[pS['k'], 'FqB'], w=[f'tm{i2}'])
                    S.op('pool', lambda: nc.gpsimd.tensor_tensor(out=tm[i2][:], in0=tm[i2][:], in1=mk[i2][:], op=ALU.add),
                         r=[f'tm{i2}', f'mk{i2}'], w=[f'tm{i2}'])
                    S.op('act', lambda: nc.scalar.activation(out=PT[i2][:], in_=tm[i2][:], func=AF.Exp,
                                                             bias=negFk[:, h, bk:bk + 1]),
                         r=[f'tm{i2}', 'negFk'], w=[f'PT{i2}'])

                def back(bk):
                    i2 = (base + bk) % 4
                    S.op('pe', lambda: nc.tensor.matmul(pv['t'][:], Vh[:, bk, :], PT[i2][:], start=(bk == 0), stop=(bk == NB - 1)),
                         r=['Vh', f'PT{i2}'], w=[pv['k']], fin=False)
                    S.op('pe', lambda: nc.tensor.matmul(rs['t'][:], c['ones_b'], PT[i2][:], start=(bk == 0), stop=(bk == NB - 1)),
                         r=['cst_bf', f'PT{i2}'], w=[rs['k']], fin=True)

                LAG = 2
                for step in range(NB + LAG):
                    if step < NB:
                        front(step)
                    if step >= LAG:
                        back(step - LAG)
                S.op('dve', lambda: nc.vector.reciprocal(out=rr[:], in_=rs['t'][:]), r=[rs['k']], w=['rr'])
                S.op('dve', lambda: nc.vector.tensor_tensor(out=br[:, 3, h, cols], in0=pv['t'][:], in1=rr[:], op=ALU.mult),
                     r=[pv['k'], 'rr'], w=['br3'])
    for n_ in range(4):
        S.dma('sp', dbg_br[:, n_], br[:, n_], r=[f'br{n_}'])
    stA.close()
    S.scope_barrier()

    with ExitStack() as st3:
        ring = WRing(b, "wg", 2, st3)
        wbt = [b.sb(f"wbt{i}", [128, 4, 512], BF16, st3) for i in range(2)]
        mergedT = b.sb("mergedT", [128, NCH, T], BF16, st3)
        accs = [b.sb(f"macc{i}", [128, 512], F32, st3) for i in range(8)]
        gs = [b.sb(f"gs{i}", [128, 512], F32, st3) for i in range(2)]
        tmpm = [b.sb(f"tmpm{i}", [128, 512], F32, st3) for i in range(2)]
        hres = [b.sb(f"hres{i}", [128, 512], F32, st3) for i in range(2)]
        otl = [b.sb(f"otl{i}", [128, 512], F32, st3) for i in range(2)]
        it = 0
        items = [(dq, n) for dq in range(4) for n in range(4)]
        loaded = {}

        def issue(i):
            dq, n = items[i]
            wt, wk = ring.load(w_in[:, OFF_G + n * D + dq * 512:OFF_G + n * D + (dq + 1) * 512])
            wb_, wbk = wbt[i % 2], f"wbt{i % 2}"
            S.dma('pool', wb_[:], wbr_d[n][:, dq * 512:(dq + 1) * 512].rearrange("(c p) d -> p c d", p=128), w=[wbk])
            loaded[i] = (wt, wk, wb_, wbk)

        issue(0)
        oproj0 = None
        for i, (dq, n) in enumerate(items):
            if i + 1 < len(items):
                issue(i + 1)
            else:
                oproj0 = ring.load(wout_d[:, 0:512])
            wt, wk, wb_, wbk = loaded[i]
            for dtl in range(4):
                for hf in range(NH):
                    cols = slice(hf * 512, (hf + 1) * 512)
                    i2 = it % 2
                    it += 1
                    pg, py = P[i2], P[2 + i2]
                    for ch in range(NCH):
                        S.op('pe', lambda ch=ch: nc.tensor.matmul(pg['t'][:], wt[:, ch, dtl * 128:(dtl + 1) * 128], xnT[:, ch, cols],
                                                                 start=(ch == 0), stop=(ch == NCH - 1)),
                             r=[wk, 'xnT'], w=[pg['k']], fin=(ch == NCH - 1))
                    for mc in range(4):
                        S.op('pe', lambda mc=mc: nc.tensor.matmul(py['t'][:], wb_[:, mc, dtl * 128:(dtl + 1) * 128], br[:, n, mc, cols],
                                                                 start=(mc == 0), stop=(mc == 3)),
                             r=[wbk, f'br{n}'], w=[py['k']], fin=(mc == 3))
                    S.op('act', lambda: nc.scalar.activation(out=gs[i2][:], in_=pg['t'][:], func=AF.Sigmoid),
                         r=[pg['k']], w=[f'gs{i2}'])
                    a_ = accs[dtl * NH + hf]
                    ak = f"macc{dtl * NH + hf}"
                    if n == 0:
                        S.op('dve', lambda: nc.vector.tensor_tensor(out=a_[:], in0=gs[i2][:], in1=py['t'][:], op=ALU.mult),
                             r=[f'gs{i2}', py['k']], w=[ak])
                    else:
                        S.op('dve', lambda: nc.vector.tensor_tensor(out=tmpm[i2][:], in0=gs[i2][:], in1=py['t'][:], op=ALU.mult),
                             r=[f'gs{i2}', py['k']], w=[f'tmpm{i2}'])
                        if n < 3:
                            S.op('dve', lambda: nc.vector.tensor_tensor(out=a_[:], in0=a_[:], in1=tmpm[i2][:], op=ALU.add),
                                 r=[ak, f'tmpm{i2}'], w=[ak])
                        else:
                            S.op('dve', lambda: nc.vector.tensor_tensor(out=mergedT[:, dq * 4 + dtl, cols], in0=a_[:], in1=tmpm[i2][:],
                                                                       op=ALU.add), r=[ak, f'tmpm{i2}'], w=['mergedT'])
        for ch in range(NCH):
            S.dma('sp', dbg_m[:, ch, :], mergedT[:, ch, :], r=['mergedT'])
        it = 0
        nxt = oproj0
        for dq in range(4):
            wt, wk = nxt
            if dq + 1 < 4:
                nxt = ring.load(wout_d[:, (dq + 1) * 512:(dq + 2) * 512])
            for dtl in range(4):
                dt = dq * 4 + dtl
                for hf in range(NH):
                    cols = slice(hf * 512, (hf + 1) * 512)
                    i2 = it % 2
                    it += 1
                    po = P[4 + i2]
                    S.dma('sp', hres[i2][:], hT_d[:, dt, cols], w=[f'hres{i2}'])
                    for ch in range(NCH):
                        S.op('pe', lambda ch=ch: nc.tensor.matmul(po['t'][:], wt[:, ch, dtl * 128:(dtl + 1) * 128], mergedT[:, ch, cols],
                                                                 start=(ch == 0), stop=(ch == NCH - 1)),
                             r=[wk, 'mergedT'], w=[po['k']], fin=(ch == NCH - 1))
                    S.op('dve', lambda: nc.vector.tensor_tensor(out=otl[i2][:], in0=po['t'][:], in1=hres[i2][:], op=ALU.add),
                         r=[po['k'], f'hres{i2}'], w=[f'otl{i2}'])
                    S.dma('sp', hT_o[:, dt, cols], otl[i2][:], r=[f'otl{i2}'])
    S.wait_all('sp')
    b.stack.close()
    return nc


def build_peer(TP=2048):
    b = B()
    nc, S = b.nc, b.S
    TS = 512
    NTT = TS // 128
    hT_d = b.din("hT", [128, NCH, TP])
    g2_d = b.din("g2", [128, NCH])
    wq_d = b.din("wq", [D, D])
    kT_d = b.din("keysT", [128, 16, 128])
    uT_d = b.din("uT", [D, PEER_N])
    v_d = b.din("v", [PEER_N, D])
    cst_d = b.din("cst", [128, 3, 128])
    hT_o = b.dout("hTo", [128, NCH, TP])

    c = emit_consts(b, cst_d)
    P = [{'t': b.ps(f"ps{i}"), 'k': f"ps{i}"} for i in range(8)]
    g2 = b.sb("g2", [128, NCH])
    keysTb = b.sb("keysTb", [128, 16, 128], BF16)
    S.dma('sp', g2[:], g2_d, w=['g2'])
    S.dma('pool', keysTb[:], kT_d, w=['keysTb'])
    acc = b.sb("acc", [128, NCH, TS])
    hnT = b.sb("hnT", [128, NCH, TS], BF16)
    a1 = b.sb("a1", [128, NTT, 8, 128])
    a2n = b.sb("a2n", [128, NTT, 8, 128])
    Dc = b.sb("Dc", [128, NTT, 8, 128], BF16)
    uring = WRing(b, "ur", 2)

    for half in range(TP // TS):
        hc = slice(half * TS, (half + 1) * TS)
        HK = f"_{half}"
        for ch in range(NCH):
            S.dma('sp', acc[:, ch, :], hT_d[:, ch, hc], w=['acc'], append=(ch > 0))
        emit_rmsnorm(b, c, acc, 'acc', g2, 'g2', hnT, 'hnT', P[6:8], TS)
        with ExitStack() as st:
            qTp = b.sb("qTp" + HK, [128, 16, TS], BF16, st)
            s_sb = b.sb("s_sb" + HK, [128, 16, 128], F32, st)
            swork = b.sb("swork" + HK, [128, 128], F32, st)
            tv = b.sb("tv" + HK, [128, 16, 16], F32, st)
            cand = b.sb("cand" + HK, [128, 8, 256], F32, st)
            cwork = b.sb("cwork" + HK, [128, 256], F32, st)
            cv = b.sb("cv" + HK, [128, 8, 16], F32, st)
            cvs = b.sb("cvs" + HK, [128, 8, 16], F32, st)
            sm = b.sb("sm" + HK, [128, 16, 128], F32, st)
            e2 = b.sb("e2" + HK, [128, 8, 128], F32, st)
            Z = b.sb("Z" + HK, [128, 8], F32, st)
            rZ = b.sb("rZ" + HK, [128, 8], F32, st)
            e1 = b.sb("e1" + HK, [128, 8], F32, st)
            k2 = b.sb("k2" + HK, [128, 8], F32, st)
            cthr = b.sb("cthr" + HK, [128, 8], F32, st)
            for wq4 in range(4):
                wt, wk = uring.load(wq_d[:, wq4 * 512:(wq4 + 1) * 512])
                for jl in range(4):
                    hp = wq4 * 4 + jl
                    ps = P[6 + hp % 2]
                    for ch in range(NCH):
                        S.op('pe', lambda ch=ch: nc.tensor.matmul(ps['t'][:], wt[:, ch, jl * 128:(jl + 1) * 128], hnT[:, ch, :],
                                                                 start=(ch == 0), stop=(ch == NCH - 1)),
                             r=[wk, 'hnT'], w=[ps['k']], fin=(ch == NCH - 1))
                    S.op('act', lambda: nc.scalar.copy(out=qTp[:, hp, :], in_=ps['t'][:]), r=[ps['k']], w=['qTp'])
            for tt in range(NTT):
                tc = slice(tt * 128, (tt + 1) * 128)
                for grp in range(4):
                    ps = P[6 + grp % 2]
                    for jl in range(4):
                        hp = grp * 4 + jl
                        S.op('pe', lambda: nc.tensor.matmul(ps['t'][:, jl * 128:(jl + 1) * 128], qTp[:, hp, tc], keysTb[:, hp, :],
                                                           start=True, stop=True), r=['qTp', 'keysTb'], w=[ps['k']], fin=(jl == 3))
                    S.op('act', lambda: nc.scalar.copy(out=s_sb[:, grp * 4:(grp + 1) * 4, :],
                                                       in_=ps['t'][:].rearrange("p (j n) -> p j n", n=128)), r=[ps['k']], w=['s_sb'])
                for hp in range(16):
                    S.op('dve', lambda: nc.vector.max(out=tv[:, hp, 0:8], in_=s_sb[:, hp, :]), r=['s_sb'], w=['tv'])
                    S.op('dve', lambda: nc.vector.match_replace(out=swork[:], in_to_replace=tv[:, hp, 0:8], in_values=s_sb[:, hp, :],
                                                                imm_value=-1e30), r=['s_sb', 'tv'], w=['swork'])
                    S.op('dve', lambda: nc.vector.max(out=tv[:, hp, 8:16], in_=swork[:]), r=['swork'], w=['tv'])
                tv4 = tv[:].rearrange("p (h two) k -> p h two k", two=2)
                S.op('dve', lambda: nc.vector.tensor_tensor(
                    out=cand[:].rearrange("p h (a b) -> p h a b", b=16),
                    in0=tv4[:, :, 0, :].unsqueeze(3).to_broadcast([128, 8, 16, 16]),
                    in1=tv4[:, :, 1, :].unsqueeze(2).to_broadcast([128, 8, 16, 16]), op=ALU.add), r=['tv'], w=['cand'])
                for h in range(8):
                    S.op('dve', lambda: nc.vector.max(out=cv[:, h, 0:8], in_=cand[:, h, :]), r=['cand'], w=['cv'])
                    S.op('dve', lambda: nc.vector.match_replace(out=cwork[:], in_to_replace=cv[:, h, 0:8], in_values=cand[:, h, :],
                                                                imm_value=-1e30), r=['cand', 'cv'], w=['cwork'])
                    S.op('dve', lambda: nc.vector.max(out=cv[:, h, 8:16], in_=cwork[:]), r=['cwork'], w=['cv'])
                S.op('dve', lambda: nc.vector.tensor_tensor(out=cvs[:], in0=cv[:], in1=cv[:, :, 0:1].to_broadcast([128, 8, 16]),
                                                            op=ALU.subtract), r=['cv'], w=['cvs'])
                S.op('act', lambda: nc.scalar.activation(out=cvs[:], in_=cvs[:], func=AF.Exp), r=['cvs'], w=['cvs'])
                S.op('dve', lambda: nc.vector.reduce_sum(out=Z[:], in_=cvs[:], axis=AX.X), r=['cvs'], w=['Z'])
                S.op('dve', lambda: nc.vector.reciprocal(out=rZ[:], in_=Z[:]), r=['Z'], w=['rZ'])
                S.op('dve', lambda: nc.vector.tensor_scalar(out=e1[:], in0=cvs[:, :, 15], scalar1=1.0 - 1e-4, scalar2=None, op0=ALU.mult),
                     r=['cvs'], w=['e1'])
                S.op('dve', lambda: nc.vector.reciprocal(out=k2[:], in_=e1[:]), r=['e1'], w=['k2'])
                S.op('dve', lambda: nc.vector.tensor_tensor(out=cthr[:], in0=e1[:], in1=rZ[:], op=ALU.mult), r=['e1', 'rZ'], w=['cthr'])
                S.op('dve', lambda: nc.vector.tensor_tensor(out=sm[:], in0=s_sb[:], in1=tv[:, :, 0:1].to_broadcast([128, 16, 128]),
                                                            op=ALU.subtract), r=['s_sb', 'tv'], w=['sm'])
                sm4 = sm[:].rearrange("p (h two) n -> p h two n", two=2)
                S.op('act', lambda: nc.scalar.activation(out=a1[:, tt, :, :], in_=sm4[:, :, 0, :], func=AF.Exp), r=['sm'], w=['a1'])
                S.op('act', lambda: nc.scalar.activation(out=e2[:], in_=sm4[:, :, 1, :], func=AF.Exp), r=['sm'], w=['e2'])
                S.op('dve', lambda: nc.vector.tensor_tensor(out=a2n[:, tt, :, :], in0=e2[:], in1=k2[:].unsqueeze(2).to_broadcast([128, 8, 128]),
                                                            op=ALU.mult), r=['e2', 'k2'], w=['a2n'])
                S.op('dve', lambda: nc.vector.tensor_tensor(out=Dc[:, tt, :, :], in0=c['ident_b'].unsqueeze(1).to_broadcast([128, 8, 128]),
                                                            in1=cthr[:].unsqueeze(2).to_broadcast([128, 8, 128]), op=ALU.mult),
                     r=['cst_bf', 'cthr'], w=['Dc'])
        S.scope_barrier()
        with ExitStack() as st:
            vr = [b.sb(f"vr{i}" + HK, [128, 4, D], BF16, st) for i in range(2)]
            GT = [b.sb(f"GT{i}" + HK, [128, TS], BF16, st) for i in range(8)]
            Pp = [b.sb(f"Pp{i}" + HK, [128, 8, 128], F32, st) for i in range(4)]
            Wp = [b.sb(f"Wp{i}" + HK, [128, 8, 128], BF16, st) for i in range(8)]
            ga = [b.sb(f"ga{i}" + HK, [128, TS], F32, st) for i in range(2)]
            NEC = PEER_N // 512
            NJT = NEC * 4

            def load_chunk(ec):
                ut, uk = uring.load(uT_d[:, ec * 512:(ec + 1) * 512])
                vt, vk = vr[ec % 2], f"vr{ec % 2}"
                vsrc = v_d[ec * 512:(ec + 1) * 512, :].rearrange("(j p) d -> p j d", p=128)
                for j in range(4):
                    S.dma('pool', vt[:, j, :], vsrc[:, j, :], w=[vk], append=(j > 0))
                return (ut, uk, vt, vk)

            chunks = {0: load_chunk(0)}

            def emit_A(J):
                ec, j = divmod(J, 4)
                ut, uk, _, _ = chunks[ec]
                pA = P[J % 2]
                for ch in range(NCH):
                    S.op('pe', lambda ch=ch: nc.tensor.matmul(pA['t'][:], ut[:, ch, j * 128:(j + 1) * 128], hnT[:, ch, :],
                                                             start=(ch == 0), stop=(ch == NCH - 1)),
                         r=[uk, 'hnT'], w=[pA['k']], fin=(ch == NCH - 1))

            def emit_PW(J):
                i1 = J
                for tt in range(NTT):
                    pp, pk = Pp[tt], f'Pp{tt}'
                    ws = (J % 2) * 4 + tt
                    wp, wk_ = Wp[ws], f'Wp{ws}'
                    if tt < 2:
                        for h in range(8):
                            S.op('act', lambda h=h: nc.scalar.activation(out=pp[:, h, :], in_=a2n[:, tt, h, :], func=AF.Identity,
                                                                        scale=a1[:, tt, h, i1:i1 + 1]),
                                 r=['a1', 'a2n'], w=[pk], fin=(h == 7))
                    else:
                        S.op('pool', lambda: nc.gpsimd.tensor_tensor(out=pp[:], in0=a2n[:, tt, :, :],
                                                                    in1=a1[:, tt, :, i1:i1 + 1].to_broadcast([128, 8, 128]), op=ALU.mult),
                             r=['a1', 'a2n'], w=[pk])
                    S.op('dve', lambda: nc.vector.scalar_tensor_tensor(out=wp[:], in0=pp[:], scalar=1.0, in1=pp[:],
                                                                      op0=ALU.is_ge, op1=ALU.mult), r=[pk], w=[wk_])

            def emit_T(J):
                ec, j = divmod(J, 4)
                gi = (ec % 2) * 4 + j
                pA, pW = P[J % 2], P[2 + J % 2]
                for tt in range(NTT):
                    ws = (J % 2) * 4 + tt
                    wp, wk_ = Wp[ws], f'Wp{ws}'
                    for h in range(8):
                        S.op('pe', lambda h=h: nc.tensor.matmul(pW['t'][:, tt * 128:(tt + 1) * 128], wp[:, h, :], Dc[:, tt, h, :],
                                                               start=(h == 0), stop=(h == 7)),
                             r=[wk_, 'Dc'], w=[pW['k']], fin=(h == 7))
                S.op('act', lambda: nc.scalar.activation(out=ga[J % 2][:], in_=pA['t'][:], func=AF.Gelu_apprx_tanh),
                     r=[pA['k']], w=[f'ga{J % 2}'])
                S.op('dve', lambda: nc.vector.tensor_tensor(out=GT[gi][:], in0=ga[J % 2][:], in1=pW['t'][:], op=ALU.mult),
                     r=[f'ga{J % 2}', pW['k']], w=[f'GT{gi}'])

            def emit_out(ec):
                _, _, vt, vk = chunks[ec]
                for dt in range(NCH):
                    pO = P[4 + dt % 2]
                    for j in range(4):
                        gi = (ec % 2) * 4 + j
                        S.op('pe', lambda j=j, gi=gi: nc.tensor.matmul(pO['t'][:], vt[:, j, dt * 128:(dt + 1) * 128], GT[gi][:],
                                                                      start=(j == 0), stop=(j == 3)),
                             r=[vk, f'GT{gi}'], w=[pO['k']], fin=(j == 3))
                    S.op('dve', lambda: nc.vector.tensor_tensor(out=acc[:, dt, :], in0=acc[:, dt, :], in1=pO['t'][:], op=ALU.add),
                         r=[pO['k'], 'acc'], w=['acc'])

            for J in range(NJT + 2):
                if J < NJT:
                    ec, j = divmod(J, 4)
                    if j == 2 and ec + 1 < NEC:
                        chunks[ec + 1] = load_chunk(ec + 1)
                    emit_A(J)
                    emit_PW(J)
                if 1 <= J <= NJT:
                    emit_T(J - 1)
                if J >= 5 and (J - 5) % 4 == 0:
                    emit_out((J - 5) // 4)
            for ch in range(NCH):
                S.dma('sp', hT_o[:, ch, hc], acc[:, ch, :], r=['acc'])
        S.scope_barrier()
    S.wait_all('sp')
    b.stack.close()
    return nc


def build_fin():
    b = B()
    nc, S = b.nc, b.S
    hT_d = b.din("hT", [128, NCH, T])
    g_d = b.din("gF", [128, NCH])
    cst_d = b.din("cst", [128, 3, 128])
    o_d = b.dout("oT", [128, NCH, T])
    c = emit_consts(b, cst_d)
    P = [{'t': b.ps(f"ps{i}"), 'k': f"ps{i}"} for i in range(2)]
    g = b.sb("gF", [128, NCH])
    hT = b.sb("hT", [128, NCH, T])
    oT = b.sb("oT", [128, NCH, T])
    S.dma('sp', g[:], g_d, w=['g'])
    for ch in range(NCH):
        S.dma('sp', hT[:, ch, :], hT_d[:, ch, :], w=[f'hT{ch}'])
    S.op('act', lambda: nc.scalar.copy(out=c['eps'][:, 0:1], in_=c['eps'][:, 0:1]),
         r=[f'hT{ch}' for ch in range(NCH)] + ['cst_eps'], w=['hTj'])
    emit_rmsnorm(b, c, hT, 'hTj', g, 'g', oT, 'oT', P, T)
    for ch in range(NCH):
        S.dma('sp', o_d[:, ch, :], oT[:, ch, :], r=['oT'])
    S.wait_all('sp')
    b.stack.close()
    return nc


_CACHE = {}


def _consts():
    cst = np.zeros((128, 3, 128), np.float32)
    cst[:, 0, :] = np.eye(128, dtype=np.float32)
    cst[:, 1, :] = np.triu(np.ones((128, 128), np.float32))
    cst[:, 2, :] = 1.0
    return cst


def to_fm(h):
    return np.ascontiguousarray(h.reshape(h.shape[0], NCH, 128).transpose(2, 1, 0))


def from_fm(hT):
    return np.ascontiguousarray(hT.transpose(2, 1, 0).reshape(hT.shape[2], D))


def run_kv(hT_list, l, inp):
    if 'kv' not in _CACHE:
        _CACHE['kv'] = build_kv()
    nc = _CACHE['kv']
    g1 = np.ascontiguousarray(inp['norm1_g'][l].reshape(NCH, 128).T)
    fb = np.ascontiguousarray(np.broadcast_to(inp['forget_b'][l][None, :], (128, 4))).astype(np.float32)
    W = inp['w_in'][l]
    w_kv = np.ascontiguousarray(np.concatenate(
        [W[:, OFF_D + 512:OFF_D + 1536], W[:, OFF_A + 512:OFF_A + 1536], W[:, OFF_C:OFF_C + 512],
         W[:, OFF_D + 1536:OFF_D + 1540]], axis=1))
    cst = _consts()
    maps = [{"hT": hT_list[c], "g1": g1, "w_kv": w_kv, "fb": fb, "cst": cst} for c in range(NCORES)]
    res = run_bass_kernel_spmd(nc, maps, core_ids=list(range(NCORES)))
    return res.results


def run_mix(hT_list, l, inp, kvres):
    if 'mix' not in _CACHE:
        _CACHE['mix'] = build_mix()
    nc = _CACHE['mix']
    NB = SEQ // 128
    rep = lambda a: np.ascontiguousarray(np.broadcast_to(a, (128,) + a.shape)).astype(np.float32)
    g1 = np.ascontiguousarray(inp['norm1_g'][l].reshape(NCH, 128).T)
    cw = np.ascontiguousarray(inp['conv_w'][l].reshape(3, 4, 128).transpose(2, 1, 0))
    sgg = rep(inp['sgu_norm_g'][l])
    swT = np.ascontiguousarray(inp['sgu_w'][l].transpose(2, 0, 1))
    sgb = rep(inp['sgu_b'][l])
    pw = np.ascontiguousarray(inp['pool_w'][l].transpose(1, 0, 2))
    psc = np.ascontiguousarray(inp['pool_scale'][l].reshape(4, 128).T)
    w_in = np.ascontiguousarray(inp['w_in'][l])
    wbr = np.ascontiguousarray(inp['w_branch'][l])
    wout = np.ascontiguousarray(inp['w_out'][l])
    KT = np.ascontiguousarray(np.concatenate([np.asarray(kvres[c]['kT']) for c in range(NCORES)], axis=2))
    V = np.ascontiguousarray(np.concatenate([np.asarray(kvres[c]['v']) for c in range(NCORES)], axis=2))
    fla = np.ascontiguousarray(np.concatenate([np.asarray(kvres[c]['flog']) for c in range(NCORES)], axis=1))
    kpos = np.ascontiguousarray((np.arange(NB)[None, :] * 128 + np.arange(128)[:, None]).astype(np.float32))
    cst = _consts()
    maps = []
    for c in range(NCORES):
        bef = rep((np.arange(NB) < c * (T // 128)).astype(np.float32))
        qpos = rep((c * T + np.arange(T)).astype(np.float32))
        halo = np.asarray(kvres[c - 1]['halo']) if c > 0 else np.zeros((128, 4, 17), np.float32)
        maps.append({"hT": hT_list[c], "g1": g1, "w_in": w_in, "cw": cw, "sgg": sgg, "swT": swT, "sgb": sgb, "pw": pw,
                     "psc": psc, "wbr": wbr, "wout": wout, "KT": KT, "V": V, "fla": fla,
                     "flo": np.ascontiguousarray(np.asarray(kvres[c]['flog'])), "bef": bef, "halo": np.ascontiguousarray(halo),
                     "qpos": qpos, "kpos": kpos, "cst": cst})
    res = run_bass_kernel_spmd(nc, maps, core_ids=list(range(NCORES)))
    _CACHE["mix_dbg"] = res.results
    return [np.asarray(r["hTo"]) for r in res.results]


def run_peer(hT_list, l, inp, uT=None):
    NCP = 8
    per = NCORES // NCP
    if 'peer' not in _CACHE:
        _CACHE['peer'] = build_peer(T * per)
    nc = _CACHE['peer']
    g2 = np.ascontiguousarray(inp['norm2_g'][l].reshape(NCH, 128).T)
    wq = np.ascontiguousarray(inp['peer_wq'][l])
    keysT = np.ascontiguousarray(inp['peer_keys'][l].reshape(16, 128, 128).transpose(2, 0, 1))
    if uT is None:
        uT = np.ascontiguousarray(inp['peer_u'][l].T)
    v = np.ascontiguousarray(inp['peer_v'][l])
    cst = _consts()
    maps = [{"hT": np.ascontiguousarray(np.concatenate(hT_list[c * per:(c + 1) * per], axis=2)), "g2": g2, "wq": wq,
             "keysT": keysT, "uT": uT, "v": v, "cst": cst} for c in range(NCP)]
    res = run_bass_kernel_spmd(nc, maps, core_ids=list(range(NCP)))
    out = []
    for r in res.results:
        o = np.asarray(r["hTo"])
        for i in range(per):
            out.append(np.ascontiguousarray(o[:, :, i * T:(i + 1) * T]))
    return out


def run_fin(hT_list, inp):
    if 'fin' not in _CACHE:
        _CACHE['fin'] = build_fin()
    nc = _CACHE['fin']
    gF = np.ascontiguousarray(inp['final_g'].reshape(NCH, 128).T)
    cst = _consts()
    maps = [{"hT": hT_list[c], "gF": gF, "cst": cst} for c in range(NCORES)]
    res = run_bass_kernel_spmd(nc, maps, core_ids=list(range(NCORES)))
    return [np.asarray(r["oT"]) for r in res.results]


def kernel(**inputs):
    inp = {k: np.asarray(v) for k, v in inputs.items()}
    x = inp['x'][0]
    hT = [to_fm(x[c * T:(c + 1) * T]) for c in range(NCORES)]
    for l in range(DEPTH):
        kv = run_kv(hT, l, inp)
        hT = run_mix(hT, l, inp, kv)
        hT = run_peer(hT, l, inp)
    oT = run_fin(hT, inp)
    out = np.concatenate([from_fm(a) for a in oT], axis=0)
    return out[None].astype(np.float32)
```

```python
import numpy as np
from contextlib import ExitStack
import concourse.bass as bass
import concourse.mybir as mybir
from concourse.bass_utils import run_bass_kernel_spmd

F32 = mybir.dt.float32
BF16 = mybir.dt.bfloat16
AF = mybir.ActivationFunctionType
ALU = mybir.AluOpType
AX = mybir.AxisListType

NCORES = 8
D = 2048
SEQ = 8192
T = SEQ // NCORES
NCH = D // 128
DEPTH = 4
BW = 512
OFF_A = 0
OFF_B = 1536
OFF_C = 2560
OFF_D = 3072
OFF_G = OFF_D + 1536 + 4
IN_COLS = OFF_G + 4 * D
EPS = 1e-6
NEG = -1.0e5
PEER_N = 16384


class Sched:
    def __init__(self, nc, n_dma_sems=24):
        self.nc = nc
        self.E = {'pe': nc.tensor, 'act': nc.scalar, 'dve': nc.vector,
                  'pool': nc.gpsimd, 'sp': nc.sync}
        self.sem = {e: nc.alloc_semaphore(name=f"sem_{e}") for e in ('pe', 'act', 'dve', 'pool')}
        self.cnt = {e: 0 for e in self.sem}
        self.dsem = [nc.alloc_semaphore(name=f"dsem{i}") for i in range(n_dma_sems)]
        self.dval = [0] * n_dma_sems
        self.dnext = 0
        self.known = {e: {} for e in self.E}
        self.lastw = {}
        self.readers = {}
        self.pend = {e: (set(), set()) for e in self.E}
        self.barrier = []
        self.seen = set()

    def _handle(self, key):
        return self.sem[key[1]] if key[0] == 'c' else self.dsem[key[1]]

    def _wait(self, e, toks):
        best = {}
        for (k, v) in toks:
            if v > best.get(k, 0):
                best[k] = v
        for k, v in best.items():
            if self.known[e].get(k, 0) >= v:
                continue
            if k == ('c', 'pe') and e == 'pe':
                continue
            self.E[e].wait_ge(self._handle(k), v)
            self.known[e][k] = v

    def scope_barrier(self):
        self.barrier = [(('c', k), v) for k, v in self.cnt.items() if v > 0]
        self.barrier += [(('d', i), v) for i, v in enumerate(self.dval) if v > 0]
        self.seen = set()

    def _deps(self, e, r, w):
        toks = []
        for k in w:
            if k not in self.seen:
                self.seen.add(k)
                toks.extend(self.barrier)
        for k in r:
            toks.extend(self.lastw.get(k, ()))
        for k in w:
            toks.extend(self.lastw.get(k, ()))
            for sk, sv in self.readers.get(k, {}).items():
                if sk == ('c', e):
                    continue
                toks.append((sk, sv))
        return toks

    def _register(self, tok, r, w, append=False):
        for k in w:
            if append:
                self.lastw.setdefault(k, []).append(tok)
            else:
                self.lastw[k] = [tok]
            self.readers[k] = {}
        for k in r:
            if k in w:
                continue
            d = self.readers.setdefault(k, {})
            if d.get(tok[0], 0) < tok[1]:
                d[tok[0]] = tok[1]

    def op(self, e, fn, r=(), w=(), fin=True):
        self._wait(e, self._deps(e, r, w))
        ins = fn()
        pr, pw = self.pend[e]
        pr.update(r)
        pw.update(w)
        if fin:
            self.cnt[e] += 1
            ins.then_inc(self.sem[e], 1)
            tok = (('c', e), self.cnt[e])
            self._register(tok, pr, pw)
            self.pend[e] = (set(), set())
        return ins

    def dma(self, q, out, in_, r=(), w=(), append=False):
        toks = self._deps(q, r, w) if not append else []
        i = self.dnext
        self.dnext = (self.dnext + 1) % len(self.dsem)
        if self.dval[i] > 0:
            toks.append((('d', i), self.dval[i]))
        self._wait(q, toks)
        ins = self.E[q].dma_start(out=out, in_=in_)
        self.dval[i] += 16
        ins.then_inc(self.dsem[i], 16)
        tok = (('d', i), self.dval[i])
        self._register(tok, set(r), set(w), append)
        return tok

    def wait_all(self, e):
        toks = [(('c', k), v) for k, v in self.cnt.items() if v > 0]
        toks += [(('d', i), v) for i, v in enumerate(self.dval) if v > 0]
        self._wait(e, toks)


class B:
    def __init__(self):
        self.nc = bass.Bass("TRN2", target_bir_lowering=False)
        self.S = Sched(self.nc)
        self.stack = ExitStack()
        self.uid = 0

    def din(self, name, shape, dt=F32):
        return self.nc.dram_tensor(name, list(shape), dt, kind="ExternalInput").ap()

    def dout(self, name, shape, dt=F32):
        return self.nc.dram_tensor(name, list(shape), dt, kind="ExternalOutput").ap()

    def sb(self, name, shape, dt=F32, stack=None):
        st = stack if stack is not None else self.stack
        return st.enter_context(self.nc.sbuf_tensor("s_" + name, list(shape), dt))

    def ps(self, name, stack=None):
        st = stack if stack is not None else self.stack
        return st.enter_context(self.nc.psum_tensor("p_" + name, [128, 512], F32))


def emit_consts(b, cst_d):
    S = b.S
    c = {}
    c['f32'] = b.sb("cst_f32", [128, 3, 128], F32)
    c['bf'] = b.sb("cst_bf", [128, 3, 128], BF16)
    S.dma('sp', c['f32'][:], cst_d, w=['cst_f32'])
    S.dma('pool', c['bf'][:], cst_d, w=['cst_bf'])
    c['eps'] = b.sb("cst_eps", [128, 1], F32)
    S.op('dve', lambda: b.nc.vector.memset(c['eps'][:], EPS), w=['cst_eps'])
    c['ident_f'] = c['f32'][:, 0, :]
    c['U_f'] = c['f32'][:, 1, :]
    c['ones_f'] = c['f32'][:, 2, :]
    c['ident_b'] = c['bf'][:, 0, :]
    c['U_b'] = c['bf'][:, 1, :]
    c['ones_b'] = c['bf'][:, 2, :]
    return c


def emit_rmsnorm(b, c, hT, hkey, g_sb, gkey, xnT, xkey, ps_list, ntok):
    S, nc = b.S, b.nc
    with ExitStack() as st:
        sq = [b.sb(f"rn_sq{i}_{b.uid}", [128, 512], F32, st) for i in range(2)]
        rstd = b.sb(f"rn_rstd_{b.uid}", [128, 512], F32, st)
        b.uid += 1
        for hf in range(ntok // 512):
            cols = slice(hf * 512, (hf + 1) * 512)
            ps = ps_list[hf % len(ps_list)]
            pk = ps['k']
            for ch in range(NCH):
                s = sq[ch % 2]
                sk = f"rn_sq{ch % 2}"
                S.op('act', lambda s=s, ch=ch: nc.scalar.activation(out=s[:], in_=hT[:, ch, cols], func=AF.Square),
                     r=[hkey], w=[sk])
                S.op('pe', lambda s=s, ch=ch: nc.tensor.matmul(ps['t'][:], c['ones_f'], s[:], start=(ch == 0), stop=(ch == NCH - 1)),
                     r=[sk, 'cst_f32'], w=[pk], fin=True)
            S.op('act', lambda: nc.scalar.activation(out=rstd[:], in_=ps['t'][:], func=AF.Sqrt, scale=1.0 / D, bias=c['eps'][:, 0:1]),
                 r=[pk, 'cst_eps'], w=['rn_rstd'])
            S.op('dve', lambda: nc.vector.reciprocal(out=rstd[:], in_=rstd[:]), r=['rn_rstd'], w=['rn_rstd'])
            for ch in range(NCH):
                S.op('dve', lambda ch=ch: nc.vector.scalar_tensor_tensor(
                    out=xnT[:, ch, cols], in0=hT[:, ch, cols], scalar=g_sb[:, ch:ch + 1], in1=rstd[:],
                    op0=ALU.mult, op1=ALU.mult), r=[hkey, gkey, 'rn_rstd'], w=[xkey])
    S.scope_barrier()


class WRing:
    def __init__(self, b, name, n, stack=None, width=512):
        self.b = b
        self.n = n
        self.name = name
        self.width = width
        self.t = [b.sb(f"{name}{i}", [128, NCH, width], BF16, stack) for i in range(n)]
        self.i = 0

    def load(self, src_rows_cols, ncols=None):
        b = self.b
        i = self.i
        self.i = (self.i + 1) % self.n
        t = self.t[i]
        key = f"{self.name}{i}"
        nco = src_rows_cols.shape[1]
        src = src_rows_cols.rearrange("(c p) n -> p c n", p=128)
        for q in range(4):
            b.S.dma('pool', t[:, 4 * q:4 * q + 4, 0:nco], src[:, 4 * q:4 * q + 4, :], w=[key], append=(q > 0))
        return t, key


def build_kv():
    b = B()
    nc, S = b.nc, b.S
    hT_d = b.din("hT", [128, NCH, T])
    g1_d = b.din("g1", [128, NCH])
    w_in = b.din("w_kv", [D, 2564])
    fb_d = b.din("fb", [128, 4])
    cst_d = b.din("cst", [128, 3, 128])
    kT_o = b.dout("kT", [4, 128, T], BF16)
    v_o = b.dout("v", [4, 128, T // 128, 128], BF16)
    fl_o = b.dout("flog", [128, T // 128, 4])
    halo_o = b.dout("halo", [128, 4, 17])

    c = emit_consts(b, cst_d)
    hT = b.sb("hT", [128, NCH, T])
    xnT = b.sb("xnT", [128, NCH, T], BF16)
    g1 = b.sb("g1", [128, NCH])
    fb = b.sb("fb", [128, 4])
    S.dma('sp', g1[:], g1_d, w=['g1'])
    S.dma('sp', fb[:], fb_d, w=['fb'])
    for ch in range(NCH):
        S.dma('sp', hT[:, ch, :], hT_d[:, ch, :], w=['hT'])
    P = [{'t': b.ps(f"ps{i}"), 'k': f"ps{i}"} for i in range(6)]
    ring = WRing(b, "wr", 3)
    ksb = b.sb("ksb", [128, 4, T], BF16)
    vsb = b.sb("vsb", [128, T // 128, 512], BF16)
    wf = b.sb("wf", [128, NCH, 4], BF16)
    fl = b.sb("fl", [128, T // 128, 4])
    halo = b.sb("halo", [128, 4, 17])
    ctmp = b.sb("ctmp", [128, 4, 16])
    emit_rmsnorm(b, c, hT, 'hT', g1, 'g1', xnT, 'xnT', P[0:2], T)
    wt, wk = ring.load(w_in[:, 0:512])
    pi = 0
    for h in range(4):
        for hf in range(T // 512):
            ps = P[pi % 4]
            pi += 1
            cols = slice(hf * 512, (hf + 1) * 512)
            for ch in range(NCH):
                S.op('pe', lambda ch=ch, ps=ps, cols=cols, h=h: nc.tensor.matmul(
                    ps['t'][:], wt[:, ch, h * 128:(h + 1) * 128], xnT[:, ch, cols], start=(ch == 0), stop=(ch == NCH - 1)),
                    r=[wk, 'xnT'], w=[ps['k']], fin=(ch == NCH - 1))
            S.op('act', lambda ps=ps, cols=cols, h=h: nc.scalar.copy(out=ksb[:, h, cols], in_=ps['t'][:]),
                 r=[ps['k']], w=['ksb'])
    for h in range(4):
        S.dma('sp', kT_o[h], ksb[:, h, :], r=['ksb'])
    wt, wk = ring.load(w_in[:, 512:1024])
    for tt in range(T // 128):
        ps = P[pi % 4]
        pi += 1
        tc = slice(tt * 128, (tt + 1) * 128)
        for ch in range(NCH):
            S.op('pe', lambda ch=ch, ps=ps, tc=tc: nc.tensor.matmul(
                ps['t'][:], xnT[:, ch, tc], wt[:, ch, :], start=(ch == 0), stop=(ch == NCH - 1)),
                r=[wk, 'xnT'], w=[ps['k']], fin=(ch == NCH - 1))
        S.op('act', lambda ps=ps, tt=tt: nc.scalar.copy(out=vsb[:, tt, :], in_=ps['t'][:]), r=[ps['k']], w=['vsb'])
    for h in range(4):
        S.dma('sp', v_o[h], vsb[:, :, h * 128:(h + 1) * 128], r=['vsb'])
    S.dma('pool', wf[:], w_in[:, 2560:2564].rearrange("(c p) n -> p c n", p=128), w=['wf'])
    psf = P[4]
    for tt in range(T // 128):
        tc = slice(tt * 128, (tt + 1) * 128)
        for ch in range(NCH):
            S.op('pe', lambda ch=ch, tc=tc, tt=tt: nc.tensor.matmul(
                psf['t'][:, tt * 4:(tt + 1) * 4], xnT[:, ch, tc], wf[:, ch, :], start=(ch == 0), stop=(ch == NCH - 1)),
                r=['wf', 'xnT'], w=[psf['k']], fin=(ch == NCH - 1 and tt == T // 128 - 1))
    S.op('dve', lambda: nc.vector.tensor_tensor(
        out=fl[:], in0=psf['t'][:, 0:(T // 128) * 4].rearrange("p (j h) -> p j h", h=4),
        in1=fb[:].unsqueeze(1).to_broadcast([128, T // 128, 4]), op=ALU.add), r=[psf['k'], 'fb'], w=['fl'])
    S.op('act', lambda: nc.scalar.activation(out=fl[:], in_=fl[:], func=AF.Exp, scale=-1.0), r=['fl'], w=['fl'])
    S.op('act', lambda: nc.scalar.activation(out=fl[:], in_=fl[:], func=AF.Ln, bias=1.0), r=['fl'], w=['fl'])
    S.op('dve', lambda: nc.vector.tensor_scalar(out=fl[:], in0=fl[:], scalar1=-1.0, scalar2=None, op0=ALU.mult),
         r=['fl'], w=['fl'])
    S.dma('sp', fl_o, fl[:], r=['fl'])
    tl = slice(T - 16, T)
    psh = P[5]
    col_sets = [1024, 1536, 2048]
    for si, c0 in enumerate(col_sets):
        wt, wk = ring.load(w_in[:, c0:c0 + 512])
        for dt in range(4):
            o = (si * 4 + dt) * 16
            for ch in range(NCH):
                S.op('pe', lambda ch=ch, dt=dt, o=o, wt=wt: nc.tensor.matmul(
                    psh['t'][:, o:o + 16], wt[:, ch, dt * 128:(dt + 1) * 128], xnT[:, ch, tl],
                    start=(ch == 0), stop=(ch == NCH - 1)),
                    r=[wk, 'xnT'], w=[psh['k']], fin=(ch == NCH - 1 and dt == 3))
    S.op('act', lambda: nc.scalar.copy(out=ctmp[:], in_=psh['t'][:, 0:64].rearrange("p (d t) -> p d t", t=16)),
         r=[psh['k']], w=['ctmp'])
    S.op('dve', lambda: nc.vector.tensor_tensor(
        out=halo[:, :, 0:2], in0=ctmp[:, :, 14:16],
        in1=psh['t'][:, 64:128].rearrange("p (d t) -> p d t", t=16)[:, :, 14:16], op=ALU.mult),
        r=['ctmp', psh['k']], w=['halo'])
    S.op('dve', lambda: nc.vector.tensor_copy(
        out=halo[:, :, 2:17], in_=psh['t'][:, 128:192].rearrange("p (d t) -> p d t", t=16)[:, :, 1:16]),
        r=[psh['k']], w=['halo'])
    S.dma('sp', halo_o, halo[:], r=['halo'])
    S.wait_all('sp')
    b.stack.close()
    return nc


def build_mix():
    b = B()
    nc, S = b.nc, b.S
    NB = SEQ // 128
    NJ = T // 128
    NH = T // 512
    hT_d = b.din("hT", [128, NCH, T])
    g1_d = b.din("g1", [128, NCH])
    w_in = b.din("w_in", [D, IN_COLS])
    cw_d = b.din("cw", [128, 4, 3])
    sgg_d = b.din("sgg", [128, 512])
    swT_d = b.din("swT", [128, 4, 128])
    sgb_d = b.din("sgb", [128, 4, 128])
    pw_d = b.din("pw", [128, 4, 128])
    psc_d = b.din("psc", [128, 4])
    wbr_d = b.din("wbr", [4, 512, D])
    wout_d = b.din("wout", [D, D])
    KT_d = b.din("KT", [4, 128, SEQ], BF16)
    V_d = b.din("V", [4, 128, NB, 128], BF16)
    fla_d = b.din("fla", [128, NB, 4])
    flo_d = b.din("flo", [128, NJ, 4])
    bef_d = b.din("bef", [128, NB])
    halo_d = b.din("halo", [128, 4, 17])
    qpos_d = b.din("qpos", [128, T])
    kpos_d = b.din("kpos", [128, NB])
    cst_d = b.din("cst", [128, 3, 128])
    hT_o = b.dout("hTo", [128, NCH, T])

    c = emit_consts(b, cst_d)
    P = [{'t': b.ps(f"ps{i}"), 'k': f"ps{i}"} for i in range(8)]
    xnT = b.sb("xnT", [128, NCH, T], BF16)
    br = b.sb("br", [128, 4, 4, T], BF16)
    g1 = b.sb("g1", [128, NCH])
    cw = b.sb("cw", [128, 4, 3])
    sgg = b.sb("sgg", [128, 512])
    swT = b.sb("swT", [128, 4, 128])
    swTm = b.sb("swTm", [128, 4, 128], BF16)
    sgb = b.sb("sgb", [128, 4, 128])
    pwb = b.sb("pwb", [128, 4, 128], BF16)
    psc = b.sb("psc", [128, 4])
    halo = b.sb("halo", [128, 4, 17])
    qpos = b.sb("qpos", [128, T])
    kpos = b.sb("kpos", [128, NB])
    for (t_, d_, k_) in [(g1, g1_d, 'g1'), (cw, cw_d, 'cw'), (sgg, sgg_d, 'sgg'), (swT, swT_d, 'swT'), (sgb, sgb_d, 'sgb'),
                         (psc, psc_d, 'psc'), (halo, halo_d, 'halo'), (qpos, qpos_d, 'qpos'), (kpos, kpos_d, 'kpos')]:
        S.dma('sp', t_[:], d_, w=[k_])
    S.dma('pool', pwb[:], pw_d, w=['pwb'])
    S.op('dve', lambda: nc.vector.tensor_tensor(out=swTm[:], in0=swT[:], in1=c['U_f'].unsqueeze(1).to_broadcast([128, 4, 128]),
                                                op=ALU.mult), r=['swT', 'cst_f32'], w=['swTm'])

    with ExitStack() as st0:
        hT = b.sb("hT", [128, NCH, T], F32, st0)
        for ch in range(NCH):
            S.dma('sp', hT[:, ch, :], hT_d[:, ch, :], w=[f'hT{ch}'])
        S.op('act', lambda: nc.scalar.copy(out=c['eps'][:, 0:1], in_=c['eps'][:, 0:1]),
             r=[f'hT{ch}' for ch in range(NCH)] + ['cst_eps'], w=['hTj'])
        emit_rmsnorm(b, c, hT, 'hTj', g1, 'g1', xnT, 'xnT', P[0:2], T)
    S.scope_barrier()

    stA = ExitStack()
    negFk = b.sb("negFk", [128, 4, NB], F32, stA)
    FqB = b.sb("FqB", [128, 4, T], F32, stA)
    qT = b.sb("qT", [128, 4, T], BF16, stA)
    with ExitStack() as stf:
        FL = b.sb("FL", [128, NB * 4], F32, stf)
        FLo = b.sb("FLo", [128, NJ * 4], F32, stf)
        bef = b.sb("bef", [128, NB], F32, stf)
        TotT = b.sb("TotT", [128, 4, NB], F32, stf)
        FlocT = b.sb("FlocT", [128, 4, NB], F32, stf)
        incl = b.sb("incl", [128, 4, NB], F32, stf)
        ones64 = b.sb("ones64", [128, NB], F32, stf)
        pref = b.sb("pref", [128, 4], F32, stf)
        tmpF = b.sb("tmpF", [128, 4, NB], F32, stf)
        TotoT = b.sb("TotoT", [128, 4, NJ], F32, stf)
        FlocoT = b.sb("FlocoT", [128, 4, NJ], F32, stf)
        inclo = b.sb("inclo", [128, 4, NJ], F32, stf)
        Fown = b.sb("Fown", [128, 4, NJ], F32, stf)
        dg = [b.sb(f"dg{i}", [128, 128], F32, stf) for i in range(2)]
        S.dma('sp', FL[:], fla_d.rearrange("p b h -> p (b h)"), w=['FL'])
        S.dma('sp', FLo[:], flo_d.rearrange("p b h -> p (b h)"), w=['FLo'])
        S.dma('sp', bef[:], bef_d, w=['bef'])
        S.op('dve', lambda: nc.vector.memset(ones64[:], 1.0), w=['ones64'])
        S.op('pe', lambda: nc.tensor.matmul(P[2]['t'][:, 0:NB * 4], c['U_f'], FL[:], start=True, stop=True),
             r=['FL', 'cst_f32'], w=['ps2'])
        S.op('pe', lambda: nc.tensor.matmul(P[3]['t'][:, 0:NB * 4], c['ones_f'], FL[:], start=True, stop=True),
             r=['FL', 'cst_f32'], w=['ps3'])
        S.op('act', lambda: nc.scalar.copy(out=FlocT[:], in_=P[2]['t'][:, 0:NB * 4].rearrange("p (b h) -> p h b", h=4)),
             r=['ps2'], w=['FlocT'])
        S.op('act', lambda: nc.scalar.copy(out=TotT[:], in_=P[3]['t'][:, 0:NB * 4].rearrange("p (b h) -> p h b", h=4)),
             r=['ps3'], w=['TotT'])
        for h in range(4):
            S.op('dve', lambda h=h: nc.vector.tensor_tensor_scan(out=incl[:, h, :], data0=ones64[:], data1=TotT[:, h, :],
                                                                initial=0.0, op0=ALU.mult, op1=ALU.add),
                 r=['ones64', 'TotT'], w=['incl'])
        S.op('dve', lambda: nc.vector.tensor_tensor(out=tmpF[:], in0=incl[:], in1=TotT[:], op=ALU.subtract),
             r=['incl', 'TotT'], w=['tmpF'])
        S.op('dve', lambda: nc.vector.tensor_tensor(out=tmpF[:], in0=tmpF[:], in1=FlocT[:], op=ALU.add),
             r=['tmpF', 'FlocT'], w=['tmpF'])
        S.op('dve', lambda: nc.vector.tensor_scalar(out=negFk[:], in0=tmpF[:], scalar1=-1.0, scalar2=None, op0=ALU.mult),
             r=['tmpF'], w=['negFk'])
        S.op('dve', lambda: nc.vector.tensor_tensor(out=tmpF[:], in0=TotT[:], in1=bef[:].unsqueeze(1).to_broadcast([128, 4, NB]),
                                                    op=ALU.mult), r=['TotT', 'bef', 'negFk'], w=['tmpF'])
        S.op('dve', lambda: nc.vector.reduce_sum(out=pref[:], in_=tmpF[:], axis=AX.X), r=['tmpF'], w=['pref'])
        S.op('pe', lambda: nc.tensor.matmul(P[2]['t'][:, 0:NJ * 4], c['U_f'], FLo[:], start=True, stop=True),
             r=['FLo', 'cst_f32'], w=['ps2'])
        S.op('pe', lambda: nc.tensor.matmul(P[3]['t'][:, 0:NJ * 4], c['ones_f'], FLo[:], start=True, stop=True),
             r=['FLo', 'cst_f32'], w=['ps3'])
        S.op('act', lambda: nc.scalar.copy(out=FlocoT[:], in_=P[2]['t'][:, 0:NJ * 4].rearrange("p (b h) -> p h b", h=4)),
             r=['ps2'], w=['FlocoT'])
        S.op('act', lambda: nc.scalar.copy(out=TotoT[:], in_=P[3]['t'][:, 0:NJ * 4].rearrange("p (b h) -> p h b", h=4)),
             r=['ps3'], w=['TotoT'])
        for h in range(4):
            S.op('dve', lambda h=h: nc.vector.tensor_tensor_scan(out=inclo[:, h, :], data0=ones64[:, 0:NJ], data1=TotoT[:, h, :],
                                                                initial=0.0, op0=ALU.mult, op1=ALU.add),
                 r=['ones64', 'TotoT'], w=['inclo'])
        S.op('dve', lambda: nc.vector.tensor_tensor(out=Fown[:], in0=inclo[:], in1=TotoT[:], op=ALU.subtract),
             r=['inclo', 'TotoT'], w=['Fown'])
        S.op('dve', lambda: nc.vector.tensor_tensor(out=Fown[:], in0=Fown[:], in1=FlocoT[:], op=ALU.add),
             r=['Fown', 'FlocoT'], w=['Fown'])
        S.op('dve', lambda: nc.vector.tensor_tensor(out=Fown[:], in0=Fown[:], in1=pref[:].unsqueeze(2).to_broadcast([128, 4, NJ]),
                                                    op=ALU.add), r=['Fown', 'pref'], w=['Fown'])
        for h in range(4):
            for hf in range(NH):
                ps = P[4 + (h * NH + hf) % 2]
                for jj in range(4):
                    j = hf * 4 + jj
                    d = dg[(j + h) % 2]
                    dk = f"dg{(j + h) % 2}"
                    S.op('dve', lambda d=d, h=h, j=j: nc.vector.tensor_scalar(out=d[:], in0=c['ident_f'], scalar1=Fown[:, h, j:j + 1],
                                                                            scalar2=None, op0=ALU.mult),
                         r=['Fown', 'cst_f32'], w=[dk])
                    S.op('pe', lambda d=d, jj=jj, ps=ps: nc.tensor.matmul(ps['t'][:, jj * 128:(jj + 1) * 128], c['ones_f'], d[:],
                                                                       start=True, stop=True),
                         r=[dk, 'cst_f32'], w=[ps['k']])
                S.op('act', lambda ps=ps, h=h, hf=hf: nc.scalar.copy(out=FqB[:, h, hf * 512:(hf + 1) * 512], in_=ps['t'][:]),
                     r=[ps['k']], w=['FqB'])
    S.scope_barrier()

    with ExitStack() as st1:
        ring = WRing(b, "wr", 3, st1)
        zz = b.sb("zz", [128, T + 2], F32, st1)
        bsb = b.sb("bsb", [128, T], F32, st1)
        acc = b.sb("acc", [128, T], F32, st1)
        ctmp = b.sb("ctmp", [128, 512], F32, st1)
        ugT = b.sb("ugT", [128, 4, T], BF16, st1)
        gv = b.sb("gv", [128, 512], F32, st1)
        junk = b.sb("junk", [128, 512], F32, st1)
        ss = b.sb("ss", [128, 1], F32, st1)
        vn = b.sb("vn", [128, NJ, 512], BF16, st1)
        pz = b.sb("pz", [128, T + 15], F32, st1)
        pa = b.sb("pa", [128, T + 15], F32, st1)
        pb = b.sb("pb", [128, T + 15], F32, st1)
        idv = b.sb("idv", [128, T], F32, st1)
        pl = b.sb("pl", [128, T], BF16, st1)
        pi = [0]

        def nextps():
            p_ = P[pi[0] % 6]
            pi[0] += 1
            return p_

        def proj_fm(wt, wk, dt, hf, ps):
            cols = slice(hf * 512, (hf + 1) * 512)
            for ch in range(NCH):
                S.op('pe', lambda ch=ch: nc.tensor.matmul(ps['t'][:], wt[:, ch, dt * 128:(dt + 1) * 128], xnT[:, ch, cols],
                                                         start=(ch == 0), stop=(ch == NCH - 1)),
                     r=[wk, 'xnT'], w=[ps['k']], fin=(ch == NCH - 1))

        wt, wk = ring.load(w_in[:, OFF_D:OFF_D + 512])
        for h in range(4):
            for hf in range(NH):
                ps = nextps()
                proj_fm(wt, wk, h, hf, ps)
                S.op('act', lambda ps=ps, h=h, hf=hf: nc.scalar.copy(out=qT[:, h, hf * 512:(hf + 1) * 512], in_=ps['t'][:]),
                     r=[ps['k']], w=['qT'])
        wb_t, wb_k = ring.load(w_in[:, OFF_A:OFF_A + 512])
        wc_t, wc_k = ring.load(w_in[:, OFF_A + 512:OFF_A + 1024])
        wh_t, wh_k = ring.load(w_in[:, OFF_A + 1024:OFF_A + 1536])
        for dt in range(4):
            S.op('dve', lambda dt=dt: nc.vector.tensor_copy(out=zz[:, 0:2], in_=halo[:, dt, 0:2]), r=['halo'], w=['zz'])
            for hf in range(NH):
                cols = slice(hf * 512, (hf + 1) * 512)
                pc, ph, pb_ = nextps(), nextps(), nextps()
                proj_fm(wc_t, wc_k, dt, hf, pc)
                proj_fm(wh_t, wh_k, dt, hf, ph)
                proj_fm(wb_t, wb_k, dt, hf, pb_)
                S.op('act', lambda pc=pc: nc.scalar.copy(out=ctmp[:], in_=pc['t'][:]), r=[pc['k']], w=['ctmp'])
                S.op('dve', lambda ph=ph, hf=hf: nc.vector.tensor_tensor(out=zz[:, 2 + hf * 512:2 + (hf + 1) * 512], in0=ctmp[:],
                                                                        in1=ph['t'][:], op=ALU.mult),
                     r=['ctmp', ph['k']], w=['zz'])
                S.op('act', lambda pb_=pb_, cols=cols: nc.scalar.copy(out=bsb[:, cols], in_=pb_['t'][:]), r=[pb_['k']], w=['bsb'])
            S.op('dve', lambda dt=dt: nc.vector.tensor_scalar(out=acc[:], in0=zz[:, 2:T + 2], scalar1=cw[:, dt, 2:3], scalar2=None,
                                                             op0=ALU.mult), r=['zz', 'cw'], w=['acc'])
            S.op('dve', lambda dt=dt: nc.vector.scalar_tensor_tensor(out=acc[:], in0=zz[:, 1:T + 1], scalar=cw[:, dt, 1:2], in1=acc[:],
                                                                    op0=ALU.mult, op1=ALU.add), r=['zz', 'cw', 'acc'], w=['acc'])
            S.op('dve', lambda dt=dt: nc.vector.scalar_tensor_tensor(out=acc[:], in0=zz[:, 0:T], scalar=cw[:, dt, 0:1], in1=acc[:],
                                                                    op0=ALU.mult, op1=ALU.add), r=['zz', 'cw', 'acc'], w=['acc'])
            S.op('dve', lambda dt=dt: nc.vector.tensor_tensor(out=br[:, 0, dt, :], in0=acc[:], in1=bsb[:], op=ALU.mult),
                 r=['acc', 'bsb'], w=['br0'])
        wu_t, wu_k = ring.load(w_in[:, OFF_B:OFF_B + 512])
        wv_t, wv_k = ring.load(w_in[:, OFF_B + 512:OFF_B + 1024])
        for dt in range(4):
            for hf in range(NH):
                ps = nextps()
                proj_fm(wu_t, wu_k, dt, hf, ps)
                S.op('act', lambda ps=ps, dt=dt, hf=hf: nc.scalar.activation(out=ugT[:, dt, hf * 512:(hf + 1) * 512], in_=ps['t'][:],
                                                                            func=AF.Gelu_apprx_tanh), r=[ps['k']], w=['ugT'])
        for tt in range(NJ):
            ps = nextps()
            tc = slice(tt * 128, (tt + 1) * 128)
            for ch in range(NCH):
                S.op('pe', lambda ch=ch, ps=ps, tc=tc: nc.tensor.matmul(ps['t'][:], xnT[:, ch, tc], wv_t[:, ch, :],
                                                                       start=(ch == 0), stop=(ch == NCH - 1)),
                     r=[wv_k, 'xnT'], w=[ps['k']], fin=(ch == NCH - 1))
            S.op('act', lambda ps=ps: nc.scalar.activation(out=gv[:], in_=ps['t'][:], func=AF.Gelu_apprx_tanh), r=[ps['k']], w=['gv'])
            S.op('act', lambda: nc.scalar.activation(out=junk[:], in_=gv[:], func=AF.Square, accum_out=ss[:, 0:1]),
                 r=['gv'], w=['junk', 'ss'])
            S.op('act', lambda: nc.scalar.activation(out=ss[:], in_=ss[:], func=AF.Sqrt, scale=1.0 / 512, bias=c['eps'][:, 0:1]),
                 r=['ss', 'cst_eps'], w=['ss'])
            S.op('dve', lambda: nc.vector.reciprocal(out=ss[:], in_=ss[:]), r=['ss'], w=['ss'])
            S.op('dve', lambda tt=tt: nc.vector.scalar_tensor_tensor(out=vn[:, tt, :], in0=gv[:], scalar=ss[:, 0:1], in1=sgg[:],
                                                                    op0=ALU.mult, op1=ALU.mult), r=['gv', 'ss', 'sgg'], w=['vn'])
        for g in range(4):
            for hf in range(NH):
                ps = nextps()
                for jj in range(4):
                    S.op('pe', lambda jj=jj, ps=ps, g=g, hf=hf: nc.tensor.matmul(
                        ps['t'][:, jj * 128:(jj + 1) * 128], vn[:, hf * 4 + jj, g * 128:(g + 1) * 128], swTm[:, g, :],
                        start=True, stop=True), r=['vn', 'swTm'], w=[ps['k']], fin=(jj == 3))
                S.op('dve', lambda ps=ps, g=g: nc.vector.tensor_tensor(
                    out=ctmp[:].rearrange("p (j t) -> p j t", t=128), in0=ps['t'][:].rearrange("p (j t) -> p j t", t=128),
                    in1=sgb[:, g, :].unsqueeze(1).to_broadcast([128, 4, 128]), op=ALU.add), r=[ps['k'], 'sgb'], w=['ctmp'])
                S.op('dve', lambda g=g, hf=hf: nc.vector.tensor_tensor(out=br[:, 1, g, hf * 512:(hf + 1) * 512], in0=ctmp[:],
                                                                      in1=ugT[:, g, hf * 512:(hf + 1) * 512], op=ALU.mult),
                     r=['ctmp', 'ugT'], w=['br1'])
        wp_t, wp_k = ring.load(w_in[:, OFF_C:OFF_C + 512])
        L = T + 15
        for g in range(4):
            w_ = 2 ** (g + 1)
            S.op('dve', lambda g=g: nc.vector.tensor_copy(out=pz[:, 0:15], in_=halo[:, g, 2:17]), r=['halo'], w=['pz'])
            for hf in range(NH):
                ps = nextps()
                proj_fm(wp_t, wp_k, g, hf, ps)
                S.op('act', lambda ps=ps, hf=hf: nc.scalar.copy(out=pz[:, 15 + hf * 512:15 + (hf + 1) * 512], in_=ps['t'][:]),
                     r=[ps['k']], w=['pz'])
            src, sk = pz, 'pz'
            bufs = [(pa, 'pa'), (pb, 'pb')]
            for k_ in range(g + 1):
                sh = 2 ** k_
                lo = 2 ** (k_ + 1) - 1
                dst, dk = bufs[k_ % 2]
                S.op('dve', lambda src=src, dst=dst, lo=lo, sh=sh: nc.vector.tensor_tensor(
                    out=dst[:, lo:L], in0=src[:, lo:L], in1=src[:, lo - sh:L - sh], op=ALU.add), r=[sk], w=[dk])
                src, sk = dst, dk
            S.op('dve', lambda w_=w_: nc.vector.tensor_scalar(out=idv[:], in0=qpos[:], scalar1=1.0, scalar2=float(w_),
                                                             op0=ALU.add, op1=ALU.min), r=['qpos'], w=['idv'])
            S.op('dve', lambda: nc.vector.reciprocal(out=idv[:], in_=idv[:]), r=['idv'], w=['idv'])
            S.op('dve', lambda src=src: nc.vector.tensor_tensor(out=idv[:], in0=src[:, 15:L], in1=idv[:], op=ALU.mult),
                 r=[sk, 'idv'], w=['idv'])
            S.op('dve', lambda: nc.vector.tensor_tensor(out=pl[:], in0=idv[:], in1=pz[:, 15:L], op=ALU.subtract),
                 r=['idv', 'pz'], w=['pl'])
            for hf in range(NH):
                ps = nextps()
                S.op('pe', lambda ps=ps, g=g, hf=hf: nc.tensor.matmul(ps['t'][:], pwb[:, g, :], pl[:, hf * 512:(hf + 1) * 512],
                                                                     start=True, stop=True), r=['pwb', 'pl'], w=[ps['k']])
                S.op('dve', lambda ps=ps, g=g, hf=hf: nc.vector.tensor_scalar(out=br[:, 2, g, hf * 512:(hf + 1) * 512], in0=ps['t'][:],
                                                                             scalar1=psc[:, g:g + 1], scalar2=None, op0=ALU.mult),
                     r=[ps['k'], 'psc'], w=['br2'])
    S.scope_barrier()

    scale = 128.0 ** -0.5
    with ExitStack() as st2:
        KTh = b.sb("KTh", [128, SEQ], BF16, st2)
        Vh = b.sb("Vh", [128, NB, 128], BF16, st2)
        mk = [b.sb(f"mk{i}", [128, 512], F32, st2) for i in range(4)]
        tm = [b.sb(f"tm{i}", [128, 512], F32, st2) for i in range(4)]
        PT = [b.sb(f"PT{i}", [128, 512], BF16, st2) for i in range(4)]
        pSb = [P[0], P[1], P[6], P[7]]
        rr = b.sb("rr", [128, 512], F32, st2)
        it = 0
        for h in range(4):
            for q4 in range(4):
                S.dma('sp', KTh[:, q4 * 2048:(q4 + 1) * 2048], KT_d[h][:, q4 * 2048:(q4 + 1) * 2048], w=['KTh'], append=(q4 > 0))
            for q4 in range(4):
                S.dma('sp', Vh[:, q4 * 16:(q4 + 1) * 16, :], V_d[h][:, q4 * 16:(q4 + 1) * 16, :], w=['Vh'], append=(q4 > 0))
            for hf in range(NH):
                cols = slice(hf * 512, (hf + 1) * 512)
                pv = P[2 + (h * NH + hf) % 2]
                rs = P[4 + (h * NH + hf) % 2]
                base = it
                it += NB

                def front(bk):
                    i2 = (base + bk) % 4
                    pS = pSb[i2]
                    S.op('dve', lambda: nc.vector.tensor_scalar(out=mk[i2][:], in0=qpos[:, cols], scalar1=kpos[:, bk:bk + 1],
                                                                scalar2=NEG, op0=ALU.is_lt, op1=ALU.mult),
                         r=['qpos', 'kpos'], w=[f'mk{i2}'])
                    S.op('pe', lambda: nc.tensor.matmul(pS['t'][:], KTh[:, bk * 128:(bk + 1) * 128], qT[:, h, cols],
                                                        start=True, stop=True), r=['KTh', 'qT'], w=[pS['k']])
                    S.op('dve', lambda: nc.vector.scalar_tensor_tensor(out=tm[i2][:], in0=pS['t'][:], scalar=scale,
                                                                      in1=FqB[:, h, cols], op0=ALU.mult, op1=ALU.add),
                         r=[pS['k'], 'FqB'], w=[f'tm{i2}'])
                    S.op('pool', lambda: nc.gpsimd.tensor_tensor(out=tm[i2][:], in0=tm[i2][:], in1=mk[i2][:], op=ALU.add),
                         r=[f'tm{i2}', f'mk{i2}'], w=[f'tm{i2}'])
                    S.op('act', lambda: nc.scalar.activation(out=PT[i2][:], in_=tm[i2][:], func=AF.Exp,
                                                             bias=negFk[:, h, bk:bk + 1]),
                         r=[f'tm{i2}', 'negFk'], w=[f'PT{i2}'])

                def back(bk):
                    i2 = (base + bk) % 4
                    S.op('pe', lambda: nc.tensor.matmul(pv['t'][:], Vh[:, bk, :], PT[i2][:], start=(bk == 0), stop=(bk == NB - 1)),
                         r=['Vh', f'PT{i2}'], w=[pv['k']], fin=False)
                    S.op('pe', lambda: nc.tensor.matmul(rs['t'][:], c['ones_b'], PT[i2][:], start=(bk == 0), stop=(bk == NB - 1)),
                         r=['cst_bf', f'PT{i2}'], w=[rs['k']], fin=True)

                LAG = 3
                for step in range(NB + LAG):
                    if step < NB:
                        front(step)
                    if step >= LAG:
                        back(step - LAG)
                S.op('dve', lambda: nc.vector.reciprocal(out=rr[:], in_=rs['t'][:]), r=[rs['k']], w=['rr'])
                S.op('dve', lambda: nc.vector.tensor_tensor(out=br[:, 3, h, cols], in0=pv['t'][:], in1=rr[:], op=ALU.mult),
                     r=[pv['k'], 'rr'], w=['br3'])
    stA.close()
    S.scope_barrier()

    with ExitStack() as st3:
        ring = WRing(b, "wg", 2, st3)
        wbt = [b.sb(f"wbt{i}", [128, 4, 512], BF16, st3) for i in range(2)]
        mergedT = b.sb("mergedT", [128, NCH, T], BF16, st3)
        accs = [b.sb(f"macc{i}", [128, 512], F32, st3) for i in range(8)]
        gs = [b.sb(f"gs{i}", [128, 512], F32, st3) for i in range(2)]
        tmpm = [b.sb(f"tmpm{i}", [128, 512], F32, st3) for i in range(2)]
        hres = [b.sb(f"hres{i}", [128, 512], F32, st3) for i in range(2)]
        otl = [b.sb(f"otl{i}", [128, 512], F32, st3) for i in range(2)]
        it = 0
        items = [(dq, n) for dq in range(4) for n in range(4)]
        loaded = {}

        def issue(i):
            dq, n = items[i]
            wt, wk = ring.load(w_in[:, OFF_G + n * D + dq * 512:OFF_G + n * D + (dq + 1) * 512])
            wb_, wbk = wbt[i % 2], f"wbt{i % 2}"
            S.dma('pool', wb_[:], wbr_d[n][:, dq * 512:(dq + 1) * 512].rearrange("(c p) d -> p c d", p=128), w=[wbk])
            loaded[i] = (wt, wk, wb_, wbk)

        issue(0)
        oproj0 = None
        for i, (dq, n) in enumerate(items):
            if i + 1 < len(items):
                issue(i + 1)
            else:
                oproj0 = ring.load(wout_d[:, 0:512])
            wt, wk, wb_, wbk = loaded[i]
            for dtl in range(4):
                for hf in range(NH):
                    cols = slice(hf * 512, (hf + 1) * 512)
                    i2 = it % 2
                    it += 1
                    pg, py = P[i2], P[2 + i2]
                    for ch in range(NCH):
                        S.op('pe', lambda ch=ch: nc.tensor.matmul(pg['t'][:], wt[:, ch, dtl * 128:(dtl + 1) * 128], xnT[:, ch, cols],
                                                                 start=(ch == 0), stop=(ch == NCH - 1)),
                             r=[wk, 'xnT'], w=[pg['k']], fin=(ch == NCH - 1))
                    for mc in range(4):
                        S.op('pe', lambda mc=mc: nc.tensor.matmul(py['t'][:], wb_[:, mc, dtl * 128:(dtl + 1) * 128], br[:, n, mc, cols],
                                                                 start=(mc == 0), stop=(mc == 3)),
                             r=[wbk, f'br{n}'], w=[py['k']], fin=(mc == 3))
                    S.op('act', lambda: nc.scalar.activation(out=gs[i2][:], in_=pg['t'][:], func=AF.Sigmoid),
                         r=[pg['k']], w=[f'gs{i2}'])
                    a_ = accs[dtl * NH + hf]
                    ak = f"macc{dtl * NH + hf}"
                    if n == 0:
                        S.op('dve', lambda: nc.vector.tensor_tensor(out=a_[:], in0=gs[i2][:], in1=py['t'][:], op=ALU.mult),
                             r=[f'gs{i2}', py['k']], w=[ak])
                    else:
                        S.op('dve', lambda: nc.vector.tensor_tensor(out=tmpm[i2][:], in0=gs[i2][:], in1=py['t'][:], op=ALU.mult),
                             r=[f'gs{i2}', py['k']], w=[f'tmpm{i2}'])
                        if n < 3:
                            S.op('dve', lambda: nc.vector.tensor_tensor(out=a_[:], in0=a_[:], in1=tmpm[i2][:], op=ALU.add),
                                 r=[ak, f'tmpm{i2}'], w=[ak])
                        else:
                            S.op('dve', lambda: nc.vector.tensor_tensor(out=mergedT[:, dq * 4 + dtl, cols], in0=a_[:], in1=tmpm[i2][:],
                                                                       op=ALU.add), r=[ak, f'tmpm{i2}'], w=['mergedT'])
        it = 0
        nxt = oproj0
        for dq in range(4):
            wt, wk = nxt
            if dq + 1 < 4:
                nxt = ring.load(wout_d[:, (dq + 1) * 512:(dq + 2) * 512])
            for dtl in range(4):
                dt = dq * 4 + dtl
                for hf in range(NH):
                    cols = slice(hf * 512, (hf + 1) * 512)
                    i2 = it % 2
                    it += 1
                    po = P[4 + i2]
                    S.dma('sp', hres[i2][:], hT_d[:, dt, cols], w=[f'hres{i2}'])
                    for ch in range(NCH):
                        S.op('pe', lambda ch=ch: nc.tensor.matmul(po['t'][:], wt[:, ch, dtl * 128:(dtl + 1) * 128], mergedT[:, ch, cols],
                                                                 start=(ch == 0), stop=(ch == NCH - 1)),
                             r=[wk, 'mergedT'], w=[po['k']], fin=(ch == NCH - 1))
                    S.op('dve', lambda: nc.vector.tensor_tensor(out=otl[i2][:], in0=po['t'][:], in1=hres[i2][:], op=ALU.add),
                         r=[po['k'], f'hres{i2}'], w=[f'otl{i2}'])
                    S.dma('sp', hT_o[:, dt, cols], otl[i2][:], r=[f'otl{i2}'])
    S.wait_all('sp')
    b.stack.close()
    return nc


def build_peer(TP=2048):
    b = B()
    nc, S = b.nc, b.S
    TS = 512
    NTT = TS // 128
    hT_d = b.din("hT", [128, NCH, TP])
    g2_d = b.din("g2", [128, NCH])
    wq_d = b.din("wq", [D, D])
    kT_d = b.din("keysT", [128, 16, 128])
    uT_d = b.din("uT", [D, PEER_N])
    v_d = b.din("v", [PEER_N, D])
    cst_d = b.din("cst", [128, 3, 128])
    hT_o = b.dout("hTo", [128, NCH, TP])

    c = emit_consts(b, cst_d)
    P = [{'t': b.ps(f"ps{i}"), 'k': f"ps{i}"} for i in range(8)]
    g2 = b.sb("g2", [128, NCH])
    keysTb = b.sb("keysTb", [128, 16, 128], BF16)
    S.dma('sp', g2[:], g2_d, w=['g2'])
    S.dma('pool', keysTb[:], kT_d, w=['keysTb'])
    acc = b.sb("acc", [128, NCH, TS])
    hnT = b.sb("hnT", [128, NCH, TS], BF16)
    a1 = b.sb("a1", [128, NTT, 8, 128])
    a2n = b.sb("a2n", [128, NTT, 8, 128])
    Dc = b.sb("Dc", [128, NTT, 8, 128], BF16)
    uring = WRing(b, "ur", 2)

    for half in range(TP // TS):
        hc = slice(half * TS, (half + 1) * TS)
        HK = f"_{half}"
        for ch in range(NCH):
            S.dma('sp', acc[:, ch, :], hT_d[:, ch, hc], w=['acc'], append=(ch > 0))
        emit_rmsnorm(b, c, acc, 'acc', g2, 'g2', hnT, 'hnT', P[6:8], TS)
        with ExitStack() as st:
            qTp = b.sb("qTp" + HK, [128, 16, TS], BF16, st)
            s_sb = b.sb("s_sb" + HK, [128, 16, 128], F32, st)
            swork = b.sb("swork" + HK, [128, 128], F32, st)
            tv = b.sb("tv" + HK, [128, 16, 16], F32, st)
            cand = b.sb("cand" + HK, [128, 8, 256], F32, st)
            cwork = b.sb("cwork" + HK, [128, 256], F32, st)
            cv = b.sb("cv" + HK, [128, 8, 16], F32, st)
            cvs = b.sb("cvs" + HK, [128, 8, 16], F32, st)
            sm = b.sb("sm" + HK, [128, 16, 128], F32, st)
            e2 = b.sb("e2" + HK, [128, 8, 128], F32, st)
            Z = b.sb("Z" + HK, [128, 8], F32, st)
            rZ = b.sb("rZ" + HK, [128, 8], F32, st)
            e1 = b.sb("e1" + HK, [128, 8], F32, st)
            k2 = b.sb("k2" + HK, [128, 8], F32, st)
            cthr = b.sb("cthr" + HK, [128, 8], F32, st)
            for wq4 in range(4):
                wt, wk = uring.load(wq_d[:, wq4 * 512:(wq4 + 1) * 512])
                for jl in range(4):
                    hp = wq4 * 4 + jl
                    ps = P[6 + hp % 2]
                    for ch in range(NCH):
                        S.op('pe', lambda ch=ch: nc.tensor.matmul(ps['t'][:], wt[:, ch, jl * 128:(jl + 1) * 128], hnT[:, ch, :],
                                                                 start=(ch == 0), stop=(ch == NCH - 1)),
                             r=[wk, 'hnT'], w=[ps['k']], fin=(ch == NCH - 1))
                    S.op('act', lambda: nc.scalar.copy(out=qTp[:, hp, :], in_=ps['t'][:]), r=[ps['k']], w=['qTp'])
            for tt in range(NTT):
                tc = slice(tt * 128, (tt + 1) * 128)
                for grp in range(4):
                    ps = P[6 + grp % 2]
                    for jl in range(4):
                        hp = grp * 4 + jl
                        S.op('pe', lambda: nc.tensor.matmul(ps['t'][:, jl * 128:(jl + 1) * 128], qTp[:, hp, tc], keysTb[:, hp, :],
                                                           start=True, stop=True), r=['qTp', 'keysTb'], w=[ps['k']], fin=(jl == 3))
                    S.op('act', lambda: nc.scalar.copy(out=s_sb[:, grp * 4:(grp + 1) * 4, :],
                                                       in_=ps['t'][:].rearrange("p (j n) -> p j n", n=128)), r=[ps['k']], w=['s_sb'])
                for hp in range(16):
                    S.op('dve', lambda: nc.vector.max(out=tv[:, hp, 0:8], in_=s_sb[:, hp, :]), r=['s_sb'], w=['tv'])
                    S.op('dve', lambda: nc.vector.match_replace(out=swork[:], in_to_replace=tv[:, hp, 0:8], in_values=s_sb[:, hp, :],
                                                                imm_value=-1e30), r=['s_sb', 'tv'], w=['swork'])
                    S.op('dve', lambda: nc.vector.max(out=tv[:, hp, 8:16], in_=swork[:]), r=['swork'], w=['tv'])
                tv4 = tv[:].rearrange("p (h two) k -> p h two k", two=2)
                S.op('dve', lambda: nc.vector.tensor_tensor(
                    out=cand[:].rearrange("p h (a b) -> p h a b", b=16),
                    in0=tv4[:, :, 0, :].unsqueeze(3).to_broadcast([128, 8, 16, 16]),
                    in1=tv4[:, :, 1, :].unsqueeze(2).to_broadcast([128, 8, 16, 16]), op=ALU.add), r=['tv'], w=['cand'])
                for h in range(8):
                    S.op('dve', lambda: nc.vector.max(out=cv[:, h, 0:8], in_=cand[:, h, :]), r=['cand'], w=['cv'])
                    S.op('dve', lambda: nc.vector.match_replace(out=cwork[:], in_to_replace=cv[:, h, 0:8], in_values=cand[:, h, :],
                                                                imm_value=-1e30), r=['cand', 'cv'], w=['cwork'])
                    S.op('dve', lambda: nc.vector.max(out=cv[:, h, 8:16], in_=cwork[:]), r=['cwork'], w=['cv'])
                S.op('dve', lambda: nc.vector.tensor_tensor(out=cvs[:], in0=cv[:], in1=cv[:, :, 0:1].to_broadcast([128, 8, 16]),
                                                            op=ALU.subtract), r=['cv'], w=['cvs'])
                S.op('act', lambda: nc.scalar.activation(out=cvs[:], in_=cvs[:], func=AF.Exp), r=['cvs'], w=['cvs'])
                S.op('dve', lambda: nc.vector.reduce_sum(out=Z[:], in_=cvs[:], axis=AX.X), r=['cvs'], w=['Z'])
                S.op('dve', lambda: nc.vector.reciprocal(out=rZ[:], in_=Z[:]), r=['Z'], w=['rZ'])
                S.op('dve', lambda: nc.vector.tensor_scalar(out=e1[:], in0=cvs[:, :, 15], scalar1=1.0 - 1e-4, scalar2=None, op0=ALU.mult),
                     r=['cvs'], w=['e1'])
                S.op('dve', lambda: nc.vector.reciprocal(out=k2[:], in_=e1[:]), r=['e1'], w=['k2'])
                S.op('dve', lambda: nc.vector.tensor_tensor(out=cthr[:], in0=e1[:], in1=rZ[:], op=ALU.mult), r=['e1', 'rZ'], w=['cthr'])
                S.op('dve', lambda: nc.vector.tensor_tensor(out=sm[:], in0=s_sb[:], in1=tv[:, :, 0:1].to_broadcast([128, 16, 128]),
                                                            op=ALU.subtract), r=['s_sb', 'tv'], w=['sm'])
                sm4 = sm[:].rearrange("p (h two) n -> p h two n", two=2)
                S.op('act', lambda: nc.scalar.activation(out=a1[:, tt, :, :], in_=sm4[:, :, 0, :], func=AF.Exp), r=['sm'], w=['a1'])
                S.op('act', lambda: nc.scalar.activation(out=e2[:], in_=sm4[:, :, 1, :], func=AF.Exp), r=['sm'], w=['e2'])
                S.op('dve', lambda: nc.vector.tensor_tensor(out=a2n[:, tt, :, :], in0=e2[:], in1=k2[:].unsqueeze(2).to_broadcast([128, 8, 128]),
                                                            op=ALU.mult), r=['e2', 'k2'], w=['a2n'])
                S.op('dve', lambda: nc.vector.tensor_tensor(out=Dc[:, tt, :, :], in0=c['ident_b'].unsqueeze(1).to_broadcast([128, 8, 128]),
                                                            in1=cthr[:].unsqueeze(2).to_broadcast([128, 8, 128]), op=ALU.mult),
                     r=['cst_bf', 'cthr'], w=['Dc'])
        S.scope_barrier()
        with ExitStack() as st:
            vr = [b.sb(f"vr{i}" + HK, [128, 4, D], BF16, st) for i in range(2)]
            GT = [b.sb(f"GT{i}" + HK, [128, TS], BF16, st) for i in range(8)]
            Pp = [b.sb(f"Pp{i}" + HK, [128, 8, 128], F32, st) for i in range(4)]
            Wp = [b.sb(f"Wp{i}" + HK, [128, 8, 128], BF16, st) for i in range(8)]
            ga = [b.sb(f"ga{i}" + HK, [128, TS], F32, st) for i in range(2)]
            NEC = PEER_N // 512
            NJT = NEC * 4

            def load_chunk(ec):
                ut, uk = uring.load(uT_d[:, ec * 512:(ec + 1) * 512])
                vt, vk = vr[ec % 2], f"vr{ec % 2}"
                vsrc = v_d[ec * 512:(ec + 1) * 512, :].rearrange("(j p) d -> p j d", p=128)
                for j in range(4):
                    S.dma('pool', vt[:, j, :], vsrc[:, j, :], w=[vk], append=(j > 0))
                return (ut, uk, vt, vk)

            chunks = {0: load_chunk(0)}

            def emit_A(J):
                ec, j = divmod(J, 4)
                ut, uk, _, _ = chunks[ec]
                pA = P[J % 2]
                for ch in range(NCH):
                    S.op('pe', lambda ch=ch: nc.tensor.matmul(pA['t'][:], ut[:, ch, j * 128:(j + 1) * 128], hnT[:, ch, :],
                                                             start=(ch == 0), stop=(ch == NCH - 1)),
                         r=[uk, 'hnT'], w=[pA['k']], fin=(ch == NCH - 1))

            def emit_PW(J):
                i1 = J
                for tt in range(NTT):
                    pp, pk = Pp[tt], f'Pp{tt}'
                    ws = (J % 2) * 4 + tt
                    wp, wk_ = Wp[ws], f'Wp{ws}'
                    if tt < 2:
                        for h in range(8):
                            S.op('act', lambda h=h: nc.scalar.activation(out=pp[:, h, :], in_=a2n[:, tt, h, :], func=AF.Identity,
                                                                        scale=a1[:, tt, h, i1:i1 + 1]),
                                 r=['a1', 'a2n'], w=[pk], fin=(h == 7))
                    else:
                        S.op('pool', lambda: nc.gpsimd.tensor_tensor(out=pp[:], in0=a2n[:, tt, :, :],
                                                                    in1=a1[:, tt, :, i1:i1 + 1].to_broadcast([128, 8, 128]), op=ALU.mult),
                             r=['a1', 'a2n'], w=[pk])
                    S.op('dve', lambda: nc.vector.scalar_tensor_tensor(out=wp[:], in0=pp[:], scalar=1.0, in1=pp[:],
                                                                      op0=ALU.is_ge, op1=ALU.mult), r=[pk], w=[wk_])

            def emit_T(J):
                ec, j = divmod(J, 4)
                gi = (ec % 2) * 4 + j
                pA, pW = P[J % 2], P[2 + J % 2]
                for tt in range(NTT):
                    ws = (J % 2) * 4 + tt
                    wp, wk_ = Wp[ws], f'Wp{ws}'
                    for h in range(8):
                        S.op('pe', lambda h=h: nc.tensor.matmul(pW['t'][:, tt * 128:(tt + 1) * 128], wp[:, h, :], Dc[:, tt, h, :],
                                                               start=(h == 0), stop=(h == 7)),
                             r=[wk_, 'Dc'], w=[pW['k']], fin=(h == 7))
                S.op('act', lambda: nc.scalar.activation(out=ga[J % 2][:], in_=pA['t'][:], func=AF.Gelu_apprx_tanh),
                     r=[pA['k']], w=[f'ga{J % 2}'])
                S.op('dve', lambda: nc.vector.tensor_tensor(out=GT[gi][:], in0=ga[J % 2][:], in1=pW['t'][:], op=ALU.mult),
                     r=[f'ga{J % 2}', pW['k']], w=[f'GT{gi}'])

            def emit_out(ec):
                _, _, vt, vk = chunks[ec]
                for dt in range(NCH):
                    pO = P[4 + dt % 2]
                    for j in range(4):
                        gi = (ec % 2) * 4 + j
                        S.op('pe', lambda j=j, gi=gi: nc.tensor.matmul(pO['t'][:], vt[:, j, dt * 128:(dt + 1) * 128], GT[gi][:],
                                                                      start=(j == 0), stop=(j == 3)),
                             r=[vk, f'GT{gi}'], w=[pO['k']], fin=(j == 3))
                    S.op('dve', lambda: nc.vector.tensor_tensor(out=acc[:, dt, :], in0=acc[:, dt, :], in1=pO['t'][:], op=ALU.add),
                         r=[pO['k'], 'acc'], w=['acc'])

            for J in range(NJT + 2):
                if J < NJT:
                    ec, j = divmod(J, 4)
                    if j == 2 and ec + 1 < NEC:
                        chunks[ec + 1] = load_chunk(ec + 1)
                    emit_A(J)
                    emit_PW(J)
                if 1 <= J <= NJT:
                    emit_T(J - 1)
                if J >= 5 and (J - 5) % 4 == 0:
                    emit_out((J - 5) // 4)
            for ch in range(NCH):
                S.dma('sp', hT_o[:, ch, hc], acc[:, ch, :], r=['acc'])
        S.scope_barrier()
    S.wait_all('sp')
    b.stack.close()
    return nc


def build_fin():
    b = B()
    nc, S = b.nc, b.S
    hT_d = b.din("hT", [128, NCH, T])
    g_d = b.din("gF", [128, NCH])
    cst_d = b.din("cst", [128, 3, 128])
    o_d = b.dout("oT", [128, NCH, T])
    c = emit_consts(b, cst_d)
    P = [{'t': b.ps(f"ps{i}"), 'k': f"ps{i}"} for i in range(2)]
    g = b.sb("gF", [128, NCH])
    hT = b.sb("hT", [128, NCH, T])
    oT = b.sb("oT", [128, NCH, T])
    S.dma('sp', g[:], g_d, w=['g'])
    for ch in range(NCH):
        S.dma('sp', hT[:, ch, :], hT_d[:, ch, :], w=[f'hT{ch}'])
    S.op('act', lambda: nc.scalar.copy(out=c['eps'][:, 0:1], in_=c['eps'][:, 0:1]),
         r=[f'hT{ch}' for ch in range(NCH)] + ['cst_eps'], w=['hTj'])
    emit_rmsnorm(b, c, hT, 'hTj', g, 'g', oT, 'oT', P, T)
    for ch in range(NCH):
        S.dma('sp', o_d[:, ch, :], oT[:, ch, :], r=['oT'])
    S.wait_all('sp')
    b.stack.close()
    return nc


_CACHE = {}


def _consts():
    cst = np.zeros((128, 3, 128), np.float32)
    cst[:, 0, :] = np.eye(128, dtype=np.float32)
    cst[:, 1, :] = np.triu(np.ones((128, 128), np.float32))
    cst[:, 2, :] = 1.0
    return cst


def to_fm(h):
    return np.ascontiguousarray(h.reshape(h.shape[0], NCH, 128).transpose(2, 1, 0))


def from_fm(hT):
    return np.ascontiguousarray(hT.transpose(2, 1, 0).reshape(hT.shape[2], D))


def run_kv(hT_list, l, inp):
    if 'kv' not in _CACHE:
        _CACHE['kv'] = build_kv()
    nc = _CACHE['kv']
    g1 = np.ascontiguousarray(inp['norm1_g'][l].reshape(NCH, 128).T)
    fb = np.ascontiguousarray(np.broadcast_to(inp['forget_b'][l][None, :], (128, 4))).astype(np.float32)
    W = inp['w_in'][l]
    w_kv = np.ascontiguousarray(np.concatenate(
        [W[:, OFF_D + 512:OFF_D + 1536], W[:, OFF_A + 512:OFF_A + 1536], W[:, OFF_C:OFF_C + 512],
         W[:, OFF_D + 1536:OFF_D + 1540]], axis=1))
    cst = _consts()
    maps = [{"hT": hT_list[c], "g1": g1, "w_kv": w_kv, "fb": fb, "cst": cst} for c in range(NCORES)]
    res = run_bass_kernel_spmd(nc, maps, core_ids=list(range(NCORES)))
    return res.results


def run_mix(hT_list, l, inp, kvres):
    if 'mix' not in _CACHE:
        _CACHE['mix'] = build_mix()
    nc = _CACHE['mix']
    NB = SEQ // 128
    rep = lambda a: np.ascontiguousarray(np.broadcast_to(a, (128,) + a.shape)).astype(np.float32)
    g1 = np.ascontiguousarray(inp['norm1_g'][l].reshape(NCH, 128).T)
    cw = np.ascontiguousarray(inp['conv_w'][l].reshape(3, 4, 128).transpose(2, 1, 0))
    sgg = rep(inp['sgu_norm_g'][l])
    swT = np.ascontiguousarray(inp['sgu_w'][l].transpose(2, 0, 1))
    sgb = rep(inp['sgu_b'][l])
    pw = np.ascontiguousarray(inp['pool_w'][l].transpose(1, 0, 2))
    psc = np.ascontiguousarray(inp['pool_scale'][l].reshape(4, 128).T)
    w_in = np.ascontiguousarray(inp['w_in'][l])
    wbr = np.ascontiguousarray(inp['w_branch'][l])
    wout = np.ascontiguousarray(inp['w_out'][l])
    KT = np.ascontiguousarray(np.concatenate([np.asarray(kvres[c]['kT']) for c in range(NCORES)], axis=2))
    V = np.ascontiguousarray(np.concatenate([np.asarray(kvres[c]['v']) for c in range(NCORES)], axis=2))
    fla = np.ascontiguousarray(np.concatenate([np.asarray(kvres[c]['flog']) for c in range(NCORES)], axis=1))
    kpos = np.ascontiguousarray((np.arange(NB)[None, :] * 128 + np.arange(128)[:, None]).astype(np.float32))
    cst = _consts()
    maps = []
    for c in range(NCORES):
        bef = rep((np.arange(NB) < c * (T // 128)).astype(np.float32))
        qpos = rep((c * T + np.arange(T)).astype(np.float32))
        halo = np.asarray(kvres[c - 1]['halo']) if c > 0 else np.zeros((128, 4, 17), np.float32)
        maps.append({"hT": hT_list[c], "g1": g1, "w_in": w_in, "cw": cw, "sgg": sgg, "swT": swT, "sgb": sgb, "pw": pw,
                     "psc": psc, "wbr": wbr, "wout": wout, "KT": KT, "V": V, "fla": fla,
                     "flo": np.ascontiguousarray(np.asarray(kvres[c]['flog'])), "bef": bef, "halo": np.ascontiguousarray(halo),
                     "qpos": qpos, "kpos": kpos, "cst": cst})
    res = run_bass_kernel_spmd(nc, maps, core_ids=list(range(NCORES)))
    _CACHE["mix_dbg"] = res.results
    return [np.asarray(r["hTo"]) for r in res.results]


def run_peer(hT_list, l, inp, uT=None):
    NCP = 8
    per = NCORES // NCP
    if 'peer' not in _CACHE:
        _CACHE['peer'] = build_peer(T * per)
    nc = _CACHE['peer']
    g2 = np.ascontiguousarray(inp['norm2_g'][l].reshape(NCH, 128).T)
    wq = np.ascontiguousarray(inp['peer_wq'][l])
    keysT = np.ascontiguousarray(inp['peer_keys'][l].reshape(16, 128, 128).transpose(2, 0, 1))
    if uT is None:
        uT = np.ascontiguousarray(inp['peer_u'][l].T)
    v = np.ascontiguousarray(inp['peer_v'][l])
    cst = _consts()
    maps = [{"hT": np.ascontiguousarray(np.concatenate(hT_list[c * per:(c + 1) * per], axis=2)), "g2": g2, "wq": wq,
             "keysT": keysT, "uT": uT, "v": v, "cst": cst} for c in range(NCP)]
    res = run_bass_kernel_spmd(nc, maps, core_ids=list(range(NCP)))
    out = []
    for r in res.results:
        o = np.asarray(r["hTo"])
        for i in range(per):
            out.append(np.ascontiguousarray(o[:, :, i * T:(i + 1) * T]))
    return out


def run_fin(hT_list, inp):
    if 'fin' not in _CACHE:
        _CACHE['fin'] = build_fin()
    nc = _CACHE['fin']
    gF = np.ascontiguousarray(inp['final_g'].reshape(NCH, 128).T)
    cst = _consts()
    maps = [{"hT": hT_list[c], "gF": gF, "cst": cst} for c in range(NCORES)]
    res = run_bass_kernel_spmd(nc, maps, core_ids=list(range(NCORES)))
    return [np.asarray(r["oT"]) for r in res.results]


def kernel(**inputs):
    inp = {k: np.asarray(v) for k, v in inputs.items()}
    x = inp['x'][0]
    hT = [to_fm(x[c * T:(c + 1) * T]) for c in range(NCORES)]
    for l in range(DEPTH):
        kv = run_kv(hT, l, inp)
        hT = run_mix(hT, l, inp, kv)
        hT = run_peer(hT, l, inp)
    oT = run_fin(hT, inp)
    out = np.concatenate([from_fm(a) for a in oT], axis=0)
    return out[None].astype(np.float32)
```
